# Optimizing a Trainium2 kernel written in Bass

```python
import math
import jax, jax.numpy as jnp
from jax import lax
import numpy as np

D_MODEL = 1024
BATCH = 4
SEQ = 4096
DEPTH = 1

CHUNK = 64
EPS = 1e-6
HG_HEADS = 8
HG_DK = 128
HG_DV = D_MODEL // HG_HEADS
HG_WIDTH_K = HG_HEADS * HG_DK
HG_WIDTH_V = HG_HEADS * HG_DV
GDN_QK_HEADS = 8
GDN_V_HEADS = 16
GDN_DK = 128
GDN_DV = 128
GDN_WIDTH_K = GDN_QK_HEADS * GDN_DK
GDN_WIDTH_V = GDN_V_HEADS * GDN_DV
CONV_K = 4
D_FF = 2816
IN_SIZES = (HG_WIDTH_K, HG_WIDTH_K, HG_WIDTH_V, HG_WIDTH_V,
            GDN_WIDTH_K, GDN_WIDTH_K, GDN_WIDTH_V, GDN_V_HEADS, GDN_V_HEADS, GDN_WIDTH_V,
            D_MODEL, D_MODEL)
IN_WIDTH = sum(IN_SIZES)

kernel_name = "hgrn2_gdn_gated_macaron_block"


def rmsnorm(x, g):
    xf = x.astype(jnp.float32)
    y = xf * lax.rsqrt(jnp.mean(xf * xf, axis=-1, keepdims=True) + EPS)
    return (y * g).astype(x.dtype)


def l2norm(x):
    return x * lax.rsqrt(jnp.sum(x * x, axis=-1, keepdims=True) + EPS)


def swiglu(x, w_in, w_out):
    a, b = jnp.split(x @ w_in, 2, axis=-1)
    return (jax.nn.silu(a) * b) @ w_out


def to_chunks(t, n_heads):
    B, S = t.shape[:2]
    t = t.reshape(B, S // CHUNK, CHUNK, n_heads, -1)
    return jnp.transpose(t, (0, 3, 1, 2, 4))


def from_chunks(t):
    B, H, NC, C, d = t.shape
    return jnp.transpose(t, (0, 2, 3, 1, 4)).reshape(B, NC * C, H, d)


def causal_short_conv(x, w):
    K = w.shape[0]
    S = x.shape[1]
    xp = jnp.pad(x, ((0, 0), (K - 1, 0), (0, 0)))
    return sum(xp[:, j:j + S] * w[j] for j in range(K))


def hgrn2_chunked(q, k, v, log_f):
    B, H, NC, C, DK = q.shape
    DV = v.shape[-1]
    b_cum = jnp.cumsum(log_f, axis=3)
    causal = jnp.tril(jnp.ones((C, C), dtype=bool))[:, :, None]

    def step(S, inp):
        q_c, k_c, v_c, b_c = inp
        inter = jnp.einsum('bhtk,bhkv->bhtv', q_c * jnp.exp(b_c), S)
        diff = b_c[:, :, :, None, :] - b_c[:, :, None, :, :]
        decay = jnp.exp(jnp.where(causal, diff, -jnp.inf))
        scores = jnp.einsum('bhtk,bhsk,bhtsk->bhts', q_c, k_c, decay)
        intra = jnp.einsum('bhts,bhsv->bhtv', scores, v_c)
        b_end = b_c[:, :, -1, :]
        k_to_end = k_c * jnp.exp(b_end[:, :, None, :] - b_c)
        S = jnp.exp(b_end)[..., None] * S + jnp.einsum('bhsk,bhsv->bhkv', k_to_end, v_c)
        return S, inter + intra

    S0 = jnp.zeros((B, H, DK, DV), jnp.float32)
    xs = (jnp.moveaxis(q, 2, 0), jnp.moveaxis(k, 2, 0), jnp.moveaxis(v, 2, 0), jnp.moveaxis(b_cum, 2, 0))
    _, o = lax.scan(step, S0, xs)
    return jnp.moveaxis(o, 0, 2)


def gated_delta_chunked(q, k, v, beta, g):
    B, H, NC, C, DK = q.shape
    DV = v.shape[-1]
    gam = jnp.cumsum(g, axis=-1)
    incl = jnp.tril(jnp.ones((C, C), dtype=bool))
    strict = jnp.tril(jnp.ones((C, C), dtype=bool), -1)
    diff = gam[..., :, None] - gam[..., None, :]
    Lmat = jnp.exp(jnp.where(incl, diff, -jnp.inf))
    kb = k * beta[..., None]
    A = jnp.where(strict, jnp.einsum('bhntk,bhnsk->bhnts', kb, k) * Lmat, 0.0)
    eye = jnp.eye(C, dtype=A.dtype)
    T = lax.linalg.triangular_solve(eye + A, jnp.broadcast_to(eye, A.shape),
                                    left_side=True, lower=True, unit_diagonal=True)
    u = jnp.matmul(T, v * beta[..., None])
    w = jnp.matmul(T, kb * jnp.exp(gam)[..., None])
    qk = jnp.einsum('bhntk,bhnsk->bhnts', q, k) * Lmat

    def step(S, inp):
        q_c, k_c, u_c, w_c, qk_c, gam_c = inp
        v_new = u_c - jnp.einsum('bhtk,bhkv->bhtv', w_c, S)
        o = (jnp.einsum('bhtk,bhkv->bhtv', q_c * jnp.exp(gam_c)[..., None], S)
             + jnp.einsum('bhts,bhsv->bhtv', qk_c, v_new))
        g_end = gam_c[..., -1]
        k_to_end = k_c * jnp.exp(g_end[..., None] - gam_c)[..., None]
        S = S * jnp.exp(g_end)[..., None, None] + jnp.einsum('bhsk,bhsv->bhkv', k_to_end, v_new)
        return S, o

    S0 = jnp.zeros((B, H, DK, DV), jnp.float32)
    xs = tuple(jnp.moveaxis(t, 2, 0) for t in (q, k, u, w, qk, gam))
    _, o = lax.scan(step, S0, xs)
    return jnp.moveaxis(o, 0, 2)


def hybrid_mixer(u, w_in, lb, hgrn_out_norm, conv_w, a_log, dt_bias, gdn_out_norm,
                 w_branch_hgrn, w_branch_gdn, w_out):
    B, S, _ = u.shape
    f32 = jnp.float32
    proj = (u @ w_in).astype(f32)
    offsets = np.cumsum(IN_SIZES)[:-1].tolist()
    (hq, hf, hi, hg, gq, gk, gv, ga, gb, gz, gate_h, gate_g) = jnp.split(proj, offsets, axis=-1)

    lb = lb.astype(f32)
    log_f = jnp.logaddexp(jnp.log(lb), jnp.log1p(-lb) + jax.nn.log_sigmoid(hf))
    k_h = -jnp.expm1(log_f)
    q_h = jax.nn.silu(hq) * HG_DK ** -0.5
    o_h = hgrn2_chunked(to_chunks(q_h, HG_HEADS), to_chunks(k_h, HG_HEADS),
                        to_chunks(hi, HG_HEADS), to_chunks(log_f, HG_HEADS))
    o_h = rmsnorm(from_chunks(o_h), hgrn_out_norm) * jax.nn.silu(hg).reshape(B, S, HG_HEADS, HG_DV)
    y_h = o_h.reshape(B, S, HG_WIDTH_V) @ w_branch_hgrn

    qkv = jax.nn.silu(causal_short_conv(jnp.concatenate([gq, gk, gv], axis=-1), conv_w))
    cq, ck, cv = jnp.split(qkv, [GDN_WIDTH_K, 2 * GDN_WIDTH_K], axis=-1)
    rep = GDN_V_HEADS // GDN_QK_HEADS
    q_g = l2norm(cq.reshape(B, S, GDN_QK_HEADS, GDN_DK)) * GDN_DK ** -0.5
    k_g = l2norm(ck.reshape(B, S, GDN_QK_HEADS, GDN_DK))
    q_g = jnp.repeat(q_g, rep, axis=2).reshape(B, S, GDN_V_HEADS * GDN_DK)
    k_g = jnp.repeat(k_g, rep, axis=2).reshape(B, S, GDN_V_HEADS * GDN_DK)
    beta = jax.nn.sigmoid(gb)
    g = -jnp.exp(a_log.astype(f32)) * jax.nn.softplus(ga + dt_bias)
    o_g = gated_delta_chunked(to_chunks(q_g, GDN_V_HEADS), to_chunks(k_g, GDN_V_HEADS),
                              to_chunks(cv, GDN_V_HEADS),
                              to_chunks(beta[..., None], GDN_V_HEADS)[..., 0],
                              to_chunks(g[..., None], GDN_V_HEADS)[..., 0])
    o_g = rmsnorm(from_chunks(o_g), gdn_out_norm) * jax.nn.silu(gz).reshape(B, S, GDN_V_HEADS, GDN_DV)
    y_g = o_g.reshape(B, S, GDN_WIDTH_V) @ w_branch_gdn

    y = jax.nn.sigmoid(gate_h) * y_h + jax.nn.sigmoid(gate_g) * y_g
    return (y @ w_out).astype(u.dtype)


def setup_inputs(seed: int = 0) -> dict:
    key = jax.random.key(seed)
    ks = jax.random.split(key, 20)
    f32 = jnp.float32
    L = DEPTH

    def dense(k, shape):
        return jax.random.normal(k, shape, f32) * shape[-2] ** -0.5

    def gain(k, shape):
        return 1.0 + 0.05 * jax.random.normal(k, shape, f32)

    A = jax.random.uniform(ks[8], (L, GDN_V_HEADS), f32, 1.0, 16.0)
    dt = jnp.exp(jax.random.uniform(ks[9], (L, GDN_V_HEADS), f32, math.log(1e-3), math.log(1e-1)))
    dt_bias = dt + jnp.log(-jnp.expm1(-dt))
    return {
        "x": jax.random.normal(ks[0], (BATCH, SEQ, D_MODEL), f32),
        "ffn1_norm": gain(ks[1], (L, D_MODEL)),
        "ffn1_w_in": dense(ks[2], (L, D_MODEL, 2 * D_FF)),
        "ffn1_w_out": dense(ks[3], (L, D_FF, D_MODEL)),
        "mix_norm": gain(ks[4], (L, D_MODEL)),
        "w_in": dense(ks[5], (L, D_MODEL, IN_WIDTH)),
        "hgrn_lb_logits": 0.5 * jax.random.normal(ks[6], (L + 1, HG_WIDTH_K), f32),
        "hgrn_out_norm": gain(ks[7], (L, HG_DV)),
        "gdn_conv_w": 0.5 * jax.random.normal(ks[10], (L, CONV_K, 2 * GDN_WIDTH_K + GDN_WIDTH_V), f32),
        "gdn_a_log": jnp.log(A),
        "gdn_dt_bias": dt_bias,
        "gdn_out_norm": gain(ks[11], (L, GDN_DV)),
        "w_branch_hgrn": dense(ks[12], (L, HG_WIDTH_V, D_MODEL)),
        "w_branch_gdn": dense(ks[13], (L, GDN_WIDTH_V, D_MODEL)),
        "w_out": dense(ks[14], (L, D_MODEL, D_MODEL)),
        "ffn2_norm": gain(ks[15], (L, D_MODEL)),
        "ffn2_w_in": dense(ks[16], (L, D_MODEL, 2 * D_FF)),
        "ffn2_w_out": dense(ks[17], (L, D_FF, D_MODEL)),
        "final_norm": gain(ks[18], (D_MODEL,)),
    }


def reference(x, ffn1_norm, ffn1_w_in, ffn1_w_out, mix_norm, w_in, hgrn_lb_logits,
              hgrn_out_norm, gdn_conv_w, gdn_a_log, gdn_dt_bias, gdn_out_norm,
              w_branch_hgrn, w_branch_gdn, w_out, ffn2_norm, ffn2_w_in, ffn2_w_out,
              final_norm):
    lb_all = jnp.cumsum(jax.nn.softmax(hgrn_lb_logits.astype(jnp.float32), axis=0), axis=0)
    h = x
    for l in range(DEPTH):
        h = h + 0.5 * swiglu(rmsnorm(h, ffn1_norm[l]), ffn1_w_in[l], ffn1_w_out[l])
        h = h + hybrid_mixer(rmsnorm(h, mix_norm[l]), w_in[l], lb_all[l], hgrn_out_norm[l],
                             gdn_conv_w[l], gdn_a_log[l], gdn_dt_bias[l], gdn_out_norm[l],
                             w_branch_hgrn[l], w_branch_gdn[l], w_out[l])
        h = h + 0.5 * swiglu(rmsnorm(h, ffn2_norm[l]), ffn2_w_in[l], ffn2_w_out[l])
    return rmsnorm(h, final_norm)
```

```python
import os
import numpy as np
import concourse.bass as bass
import concourse.mybir as mybir
from concourse.bass_utils import run_bass_kernel_spmd

F32 = mybir.dt.float32
BF16 = mybir.dt.bfloat16
AF = mybir.ActivationFunctionType
ALU = mybir.AluOpType


STRICT_SAME_ENGINE = True


class V:
    __slots__ = ("ap", "keys")

    def __init__(self, ap, keys):
        self.ap = ap
        self.keys = keys

    def __getitem__(self, idx):
        return V(self.ap[idx], self.keys)

    def bc(self, shape):
        return V(self.ap.to_broadcast(list(shape)), self.keys)

    def unsq(self, axis):
        return V(self.ap.unsqueeze(axis), self.keys)

    def bitcast(self, dt):
        return V(self.ap.bitcast(dt), self.keys)

    def rr(self, s, **kw):
        return V(self.ap.rearrange(s, **kw), self.keys)


class Buf:
    def __init__(self, t, name, nsub=1):
        self.t = t
        self.name = name

    def __getitem__(self, idx):
        return V(self.t[idx], ((self.name, 0),))

    def sub(self, k, idx):
        return V(self.t[idx], ((self.name, k),))


class Op:
    __slots__ = ("eng", "fn", "deps", "needs_inc", "sem", "val", "is_dma", "idx")

    def __init__(self, eng, fn, is_dma):
        self.eng = eng
        self.fn = fn
        self.deps = []
        self.needs_inc = False
        self.sem = None
        self.val = None
        self.is_dma = is_dma


class Prog:
    ENGS = ("sync", "scalar", "vector", "gpsimd", "tensor")

    def __init__(self, nc):
        self.nc = nc
        self.ops = {e: [] for e in self.ENGS}
        self.state = {}
        self.stack = []
        self.dma_sems = {}
        self.dma_counts = {}
        self.eng_sems = {}
        self.all_dma_ops = []

    def sbuf(self, name, shape, dt):
        g = self.nc.sbuf_tensor(name, list(shape), dt)
        t = g.__enter__()
        self.stack.append(g)
        return Buf(t, name)

    def psum(self, name, shape, dt):
        g = self.nc.psum_tensor(name, list(shape), dt)
        t = g.__enter__()
        self.stack.append(g)
        return Buf(t, name)

    def sem(self, name):
        g = self.nc.semaphore(name)
        s = g.__enter__()
        self.stack.append(g)
        return s

    def op(self, eng, fn, reads=(), writes=(), dma_key=None):
        o = Op(eng, fn, dma_key is not None)
        rkey = ("dma", dma_key) if dma_key is not None else eng
        deps = {}
        for v in reads:
            for k in v.keys:
                st = self.state.get(k)
                if st and st[0] is not None:
                    deps[id(st[0])] = (st[0], "raw")
        for v in writes:
            for k in v.keys:
                st = self.state.get(k)
                if st:
                    if st[0] is not None and id(st[0]) not in deps:
                        deps[id(st[0])] = (st[0], "waw")
                    for r in st[1].values():
                        if id(r) not in deps:
                            deps[id(r)] = (r, "war")
        for d, kind in deps.values():
            if d is o:
                continue
            if not d.is_dma and not o.is_dma and d.eng == eng:
                if eng == "tensor":
                    continue
                if kind == "waw" and not STRICT_SAME_ENGINE:
                    continue
                if kind == "war" and not STRICT_SAME_ENGINE:
                    continue
            o.deps.append(d)
            d.needs_inc = True
        for v in reads:
            for k in v.keys:
                st = self.state.setdefault(k, [None, {}])
                st[1][rkey] = o
        for v in writes:
            for k in v.keys:
                self.state[k] = [o, {}]
        if dma_key is not None:
            if dma_key not in self.dma_sems:
                self.dma_sems[dma_key] = self.sem("d_" + str(dma_key))
                self.dma_counts[dma_key] = 0
            self.dma_counts[dma_key] += 16
            o.sem = self.dma_sems[dma_key]
            o.val = self.dma_counts[dma_key]
            o.needs_inc = True
            self.all_dma_ops.append(o)
        self.ops[eng].append(o)
        return o

    def dma(self, eng, out, in_, key):
        return self.op(eng, lambda e: e.dma_start(out=out.ap, in_=in_.ap),
                       reads=[in_], writes=[out], dma_key=key)

    def mm(self, out, lhsT, rhs, start=True, stop=True, extra_reads=()):
        return self.op("tensor", lambda e: e.matmul(out.ap, lhsT.ap, rhs.ap, start=start, stop=stop),
                       reads=[lhsT, rhs] + list(extra_reads), writes=[out])

    def transpose(self, out, in_, ident):
        return self.op("tensor", lambda e: e.transpose(out.ap, in_.ap, ident.ap),
                       reads=[in_, ident], writes=[out])

    def act(self, out, in_, func, bias=None, scale=None, accum_out=None, eng="scalar"):
        reads = [in_]
        kw = {}
        if bias is not None:
            if isinstance(bias, V):
                reads.append(bias)
                kw["bias"] = bias.ap
            else:
                kw["bias"] = bias
        if scale is not None:
            if isinstance(scale, V):
                reads.append(scale)
                kw["scale"] = scale.ap
            else:
                kw["scale"] = scale
        writes = [out]
        if accum_out is not None:
            writes.append(accum_out)
            kw["accum_out"] = accum_out.ap
        return self.op("scalar", lambda e: e.activation(out.ap, in_.ap, func, **kw),
                       reads=reads, writes=writes)

    def tt(self, eng, out, in0, in1, op):
        return self.op(eng, lambda e: e.tensor_tensor(out.ap, in0.ap, in1.ap, op),
                       reads=[in0, in1], writes=[out])

    def ts(self, eng, out, in0, s1, s2, op0, op1=None, accum_out=None):
        reads = [in0]
        a1 = s1
        a2 = s2
        if isinstance(s1, V):
            reads.append(s1)
            a1 = s1.ap
        if isinstance(s2, V):
            reads.append(s2)
            a2 = s2.ap
        writes = [out]
        kw = {}
        if accum_out is not None:
            writes.append(accum_out)
            kw["accum_out"] = accum_out.ap
        if op1 is None:
            return self.op(eng, lambda e: e.tensor_scalar(out.ap, in0.ap, a1, a2, op0, **kw),
                           reads=reads, writes=writes)
        return self.op(eng, lambda e: e.tensor_scalar(out.ap, in0.ap, a1, a2, op0, op1, **kw),
                       reads=reads, writes=writes)

    def stt(self, out, in0, scalar, in1, op0, op1, eng="vector"):
        reads = [in0, in1]
        a = scalar
        if isinstance(scalar, V):
            reads.append(scalar)
            a = scalar.ap
        return self.op(eng, lambda e: e.scalar_tensor_tensor(out.ap, in0.ap, a, in1.ap, op0, op1),
                       reads=reads, writes=[out])

    def copy(self, eng, out, in_):
        if eng == "scalar":
            return self.op(eng, lambda e: e.copy(out.ap, in_.ap), reads=[in_], writes=[out])
        return self.op(eng, lambda e: e.tensor_copy(out.ap, in_.ap), reads=[in_], writes=[out])

    def memset(self, eng, out, val):
        return self.op(eng, lambda e: e.memset(out.ap, val), reads=[], writes=[out])

    def emit(self, final_wait_ops=()):
        nc = self.nc
        for eng in self.ENGS:
            cnt = 0
            for o in self.ops[eng]:
                if o.is_dma:
                    continue
                if o.needs_inc:
                    if eng not in self.eng_sems:
                        self.eng_sems[eng] = self.sem("e_" + eng)
                    cnt += 1
                    o.sem = self.eng_sems[eng]
                    o.val = cnt
        final_dmas = list(self.all_dma_ops)
        with nc.Block() as block:
            def run(eng_name):
                def body(e):
                    waited = {}
                    for o in self.ops[eng_name]:
                        need = {}
                        for d in o.deps:
                            k = id(d.sem)
                            if k not in need or need[k][1] < d.val:
                                need[k] = (d.sem, d.val)
                        for k, (s, v) in need.items():
                            if waited.get(k, 0) >= v:
                                continue
                            e.wait_ge(s, v)
                            waited[k] = v
                        ins = o.fn(e)
                        if o.needs_inc:
                            ins.then_inc(o.sem, 16 if o.is_dma else 1)
                    if eng_name == "sync":
                        last = {}
                        for o in final_dmas:
                            last[id(o.sem)] = (o.sem, max(o.val, last.get(id(o.sem), (None, 0))[1]))
                        for k, (s, v) in last.items():
                            if waited.get(k, 0) < v:
                                e.wait_ge(s, v)
                return body
            block.sync(run("sync"))
            block.scalar(run("scalar"))
            block.vector(run("vector"))
            block.gpsimd(run("gpsimd"))
            block.tensor(run("tensor"))

    def close(self):
        while self.stack:
            g = self.stack.pop()
            g.__exit__(None, None, None)


import os

D = 1024
DFF = 2816
NFF = DFF // 128
TT = 512
NB = TT // 128
EPS = 1e-6
PW = 4096
NSLOT = 4
TDT = F32
POOL_CONV = bool(int(os.environ.get('POOL_CONV', '0')))
CHAIN_F32R = bool(int(os.environ.get('CHAIN_F32R', '0')))

O_HQ, O_HF, O_HI, O_HG = 0, 1024, 2048, 3072
O_GQ, O_GK, O_GV, O_GA, O_GB, O_GZ, O_GH, O_GG = 4096, 5120, 6144, 8192, 8208, 8224, 10272, 11296

C_ID, C_ONE, C_TL, C_SL, C_TU, C_RM = 0, 128, 256, 384, 512, 640
C_G1, C_GM, C_G2, C_GF = 1152, 1160, 1168, 1176
C_HN, C_GN = 1184, 1312
C_L0, C_L1 = 1440, 1448
C_CW = 1456
C_AL, C_DT = 1584, 1600
C_MK = 1616
NCST = 1616 + 7 * 256


def piece_names():
    names = []
    for k in (1, 2):
        if k == 2:
            names += ["hg_%d" % j for j in range(8)] + ["ab"]
            for j in range(8):
                names += ["gd_%d" % j, "gz_%d" % j]
            for i in range(2):
                names += ["gh_%d" % i, "bh_%d" % i, "gg_%d" % i, "bg_%d" % (2 * i), "bg_%d" % (2 * i + 1)]
            names += ["wo_0", "wo_1"]
        names += ["f%d_in_%d" % (k, i) for i in range(11)]
        names += ["f%d_out_%d" % (k, m) for m in range(8)]
    return names


def pack_weights(inp):
    names = piece_names()
    ws = np.zeros((len(names), 128, PW), np.float32)

    def put(name, arr):
        a = np.ascontiguousarray(arr).reshape(128, -1)
        ws[names.index(name), :, :a.shape[1]] = a

    for k, (wi, wo) in enumerate(((inp["ffn1_w_in"], inp["ffn1_w_out"]), (inp["ffn2_w_in"], inp["ffn2_w_out"])), 1):
        wr = np.asarray(wi)[0].reshape(8, 128, 2 * DFF)
        for i in range(11):
            cols = []
            for j in (2 * i, 2 * i + 1):
                cols.append(wr[:, :, j * 128:(j + 1) * 128])
                cols.append(wr[:, :, DFF + j * 128:DFF + (j + 1) * 128])
            put("f%d_in_%d" % (k, i), np.stack(cols, axis=2).transpose(1, 0, 2, 3))
        w2 = np.asarray(wo)[0].reshape(NFF, 128, D)
        for m in range(8):
            put("f%d_out_%d" % (k, m), w2[:, :, m * 128:(m + 1) * 128].transpose(1, 0, 2))
    wr = np.asarray(inp["w_in"])[0].reshape(8, 128, -1)

    def cols(off, j, n=128):
        return wr[:, :, off + j * 128: off + j * 128 + n]

    for j in range(8):
        put("hg_%d" % j, np.stack([cols(O_HQ, j), cols(O_HF, j), cols(O_HG, j), cols(O_HI, j)], axis=2).transpose(1, 0, 2, 3))
        put("gd_%d" % j, np.stack([cols(O_GQ, j), cols(O_GK, j), cols(O_GV, 2 * j), cols(O_GV, 2 * j + 1)], axis=2).transpose(1, 0, 2, 3))
        put("gz_%d" % j, np.stack([cols(O_GZ, 2 * j), cols(O_GZ, 2 * j + 1)], axis=2).transpose(1, 0, 2, 3))
    put("ab", wr[:, :, O_GA:O_GA + 32].transpose(1, 0, 2))
    for i in range(2):
        put("gh_%d" % i, np.stack([cols(O_GH, 4 * i + m) for m in range(4)], axis=2).transpose(1, 0, 2, 3))
        put("gg_%d" % i, np.stack([cols(O_GG, 4 * i + m) for m in range(4)], axis=2).transpose(1, 0, 2, 3))
    bh = np.asarray(inp["w_branch_hgrn"])[0].reshape(8, 128, D)
    for i in range(2):
        put("bh_%d" % i, np.stack([bh[:, :, (4 * i + m) * 128:(4 * i + m + 1) * 128] for m in range(4)], axis=0).transpose(2, 0, 1, 3))
    bg = np.asarray(inp["w_branch_gdn"])[0].reshape(16, 128, D)
    for i in range(4):
        put("bg_%d" % i, np.stack([bg[:, :, (2 * i + m) * 128:(2 * i + m + 1) * 128] for m in range(2)], axis=0).transpose(2, 0, 1, 3))
    wo = np.asarray(inp["w_out"])[0].reshape(8, 128, D)
    for i in range(2):
        put("wo_%d" % i, np.stack([wo[:, :, (4 * i + m) * 128:(4 * i + m + 1) * 128] for m in range(4)], axis=0).transpose(2, 0, 1, 3))
    return ws


def pack_consts(inp):
    c = np.zeros((128, NCST), np.float32)
    r = np.arange(128)
    c[:, C_ID:C_ID + 128] = np.eye(128)
    c[:, C_ONE:C_ONE + 128] = 1.0
    c[:, C_TL:C_TL + 128] = (r[None, :] <= r[:, None])
    c[:, C_SL:C_SL + 128] = (r[None, :] < r[:, None])
    c[:, C_TU:C_TU + 128] = (r[:, None] <= r[None, :])
    rm = np.ones(512, np.float32)
    rm[::64] = 0.0
    c[:, C_RM:C_RM + 512] = rm[None, :]
    for col, key in ((C_G1, "ffn1_norm"), (C_GM, "mix_norm"), (C_G2, "ffn2_norm")):
        c[:, col:col + 8] = np.asarray(inp[key])[0].reshape(8, 128).T
    c[:, C_GF:C_GF + 8] = np.asarray(inp["final_norm"]).reshape(8, 128).T
    c[:, C_HN:C_HN + 128] = np.asarray(inp["hgrn_out_norm"])[0][None, :]
    c[:, C_GN:C_GN + 128] = np.asarray(inp["gdn_out_norm"])[0][None, :]
    lbl = np.asarray(inp["hgrn_lb_logits"])
    c[:, C_L0:C_L0 + 8] = lbl[0].reshape(8, 128).T
    c[:, C_L1:C_L1 + 8] = lbl[1].reshape(8, 128).T
    cw = np.asarray(inp["gdn_conv_w"])[0]
    c[:, C_CW:C_CW + 128] = cw.T.reshape(32, 128, 4).transpose(1, 0, 2).reshape(128, 128)
    for k in range(7):
        n = 1 << k
        tt_, ss_ = r[:, None], r[None, :]
        mk = ((tt_ // (2 * n) == ss_ // (2 * n)) & (tt_ % (2 * n) >= n) & (ss_ % (2 * n) < n)).astype(np.float32)
        c[:, C_MK + 256 * k:C_MK + 256 * k + 128] = mk.T
        c[:, C_MK + 256 * k + 128:C_MK + 256 * (k + 1)] = mk
    c[:, C_AL:C_AL + 16] = np.asarray(inp["gdn_a_log"])[0][None, :]
    c[:, C_DT:C_DT + 16] = np.asarray(inp["gdn_dt_bias"])[0][None, :]
    return c


import os
GDN_LEVEL = int(os.environ.get('GDN_LEVEL', '9'))
GDN_SUB = int(os.environ.get('GDN_SUB', '9'))


def build_program(n_tok, taps=None, stages=("ffn1", "hgrn", "gdn", "merge", "ffn2")):
    n_tiles = n_tok // TT
    names = piece_names()
    NP = len(names)
    nc = bass.Bass("TRN2", target_bir_lowering=False)
    P = Prog(nc)
    x_d = Buf(nc.dram_tensor("xT", [D, n_tok], F32, kind="ExternalInput").ap(), "xT")
    WSB = bool(int(os.environ.get("WS_BF16", "0")))
    ws_d = Buf(nc.dram_tensor("ws", [NP, 128, PW], BF16 if WSB else F32, kind="ExternalInput").ap(), "ws")
    cst_d = Buf(nc.dram_tensor("cst", [128, NCST], F32, kind="ExternalInput").ap(), "cstd")
    out_d = Buf(nc.dram_tensor("outT", [D, n_tok], F32, kind="ExternalOutput").ap(), "outT")
    tap_d = {}
    if taps:
        for nm, shp in taps.items():
            tdt = BF16 if nm in ("oh", "og") else F32
            tap_d[nm] = Buf(nc.dram_tensor("tap_" + nm, list(shp), tdt, kind="ExternalOutput").ap(), "tap_" + nm)

    tapped = set()

    def tap(nm, v):
        if taps and nm in tap_d and nm not in tapped:
            tapped.add(nm)
            P.dma("sync", V(tap_d[nm].t, ((tap_d[nm].name, 0),)), v, "tap_" + nm)

    cst = P.sbuf("cst_sb", [128, NCST], F32)
    ident_b = P.sbuf("ident_b", [128, 128], BF16)
    ones_b = P.sbuf("ones_b", [128, 128], BF16)
    tu_b = P.sbuf("tu_b", [128, 128], BF16)
    lbt = P.sbuf("lbt", [128, 16], F32)
    nea = P.sbuf("nea", [128, 16], F32)
    slots = [P.sbuf("slot%d" % i, [128, PW], BF16) for i in range(NSLOT)]
    h = P.sbuf("h", [128, 8, TT], F32)
    xn = P.sbuf("xn", [128, 8, TT], BF16)
    hid = P.sbuf("hid", [128, NFF, TT], BF16)
    fs = [P.sbuf("fs%d" % i, [128, TT + 4], F32) for i in range(10)]
    oh_fm = P.sbuf("oh_fm", [128, 8, TT], BF16)
    og_fm = hid[:, 0:16, :]
    ymf = hid[:, 16:20, :]
    ym = P.sbuf("ym", [128, 8, TT], BF16)
    S_h = P.sbuf("S_h", [128, 8, 128], F32)
    S_g = P.sbuf("S_g", [128, 16, 128], F32)
    Sb_g = P.sbuf("Sb_g", [128, 16, 128], BF16)
    halo = P.sbuf("halo", [128, 32, 4], F32)
    sm = [P.sbuf("sm%d" % i, [128, 64], F32) for i in range(8)]
    sc = P.sbuf("sc", [128, 64], F32)
    hq_e = [P.sbuf("hq_e%d" % i, [128, TT], BF16) for i in range(2)]
    bs = hq_e
    hk_e = [P.sbuf("hk_e%d" % i, [128, TT], BF16) for i in range(2)]
    hv_t = [P.sbuf("hv_t%d" % i, [128, TT], BF16) for i in range(2)]
    hk_z = [[P.sbuf("hk_z%d_%d" % (par, i), [128, TT], BF16) for i in range(2)] for par in range(2)]
    hq_z = [[P.sbuf("hq_z%d_%d" % (par, i), [128, TT], BF16) for i in range(2)] for par in range(2)]
    hg_t = [P.sbuf("hg_t%d" % i, [128, TT], BF16) for i in range(2)]
    hsc = [P.sbuf("hsc%d" % i, [128, 32], F32) for i in range(2)]
    gq_f = P.sbuf("gq_f", [128, TT], BF16)
    gk_f = P.sbuf("gk_f", [128, TT], BF16)
    gv_f = [P.sbuf("gv_f%d" % i, [128, TT], BF16) for i in range(2)]
    gz_t = P.sbuf("gz_t", [128, NB, 2, 128], BF16)
    kbg_t = [P.sbuf("kbg_t%d" % i, [128, TT], BF16) for i in range(2)]
    kte_t = [P.sbuf("kte_t%d" % i, [128, TT], BF16) for i in range(2)]
    bv_t = [P.sbuf("bv_t%d" % i, [128, TT], BF16) for i in range(2)]
    Gm = P.sbuf("Gm", [128, 8, 128], F32)
    Ex = P.sbuf("Ex", [128, 8, 128], F32)
    Ls = P.sbuf("Ls", [128, 8, 128], BF16)
    Lm = P.sbuf("Lm", [128, 8, 128], BF16)
    NCH = 2
    qkl = [P.sbuf("qkl%d" % c, [128, 128], BF16) for c in range(2)]
    qklT_all = P.sbuf("qklT_all", [128, 2, NB, 128], BF16)
    TT2 = [P.sbuf("TT2_%d" % i, [128, 2 * NB, 128], F32) for i in range(2)]
    X2b = P.sbuf("X2b", [128, 2 * NB, 128], F32)
    Ttb_g = P.sbuf("Ttb_g", [128, 2, NB, 128], BF16)
    u_sb = [P.sbuf("u_sb%d" % c, [128, 128], F32) for c in range(NCH)]
    wT_sb = [P.sbuf("wT_sb%d" % c, [128, 128], BF16) for c in range(NCH)]
    vnew = [P.sbuf("vnew%d" % c, [128, 128], BF16) for c in range(NCH)]
    o_sb = [P.sbuf("o_sb%d" % c, [128, 128], F32) for c in range(NCH)]
    o_bf = vnew
    Smid = wT_sb
    scT = [P.sbuf("scT%d" % c, [128, 128], BF16) for c in range(2)]

    NBK = 6
    banks = [P.psum("bank%d" % i, [128, 512], F32) for i in range(NBK)]
    tbanks = [P.psum("tbank%d" % i, [128, 1024], BF16) for i in range(2)]
    st = {"ps": 0, "tp": 0, "piece": 0, "ch": 0}

    def ps():
        b = banks[st["ps"] % NBK]
        st["ps"] += 1
        return b

    def tps():
        k = st["tp"] % 2
        st["tp"] += 1
        return tbanks[k][:, 0:512]

    USE_WSB = bool(int(os.environ.get("USE_WSB", "1"))) and n_tiles > 1 and not WSB
    if USE_WSB:
        wsb_t = nc.dram_tensor("wsb", [NP, 128, PW], BF16, kind="Internal").ap()

    def piece(name, shape):
        i = st["piece"]
        assert names[i % NP] == name, (names[i % NP], name)
        st["piece"] += 1
        k = i % NSLOT
        sl = slots[k]
        n = 1
        for s_ in shape:
            n *= s_
        pi = i % NP
        if USE_WSB and i >= NP:
            P.dma("sync", sl[:, 0:n], V(wsb_t[pi, :, 0:n], (("wsb", pi),)), "slot%d" % k)
        else:
            P.dma("sync" if WSB else "gpsimd", sl[:, 0:n], ws_d[pi, :, 0:n], "slot%d" % k)
            if USE_WSB:
                P.dma("sync", V(wsb_t[pi, :, 0:n], (("wsb", pi),)), sl[:, 0:n], "wb%d" % k)
        v = sl[:, 0:n]
        if len(shape) == 2:
            return v.rr("p (a b) -> p a b", a=shape[0])
        if len(shape) == 3:
            return v.rr("p (a b c) -> p a b c", a=shape[0], b=shape[1])
        return v

    cv = lambda c0, n=1: cst[:, c0:c0 + n]

    P.dma("sync", cst[:, :], cst_d[:, :], "cst")
    P.copy("vector", ident_b[:, :], cv(C_ID, 128))
    P.copy("vector", ones_b[:, :], cv(C_ONE, 128))
    P.copy("vector", tu_b[:, :], cv(C_TU, 128))
    P.tt("vector", sc[:, 0:8], cv(C_L0, 8), cv(C_L1, 8), ALU.subtract)
    P.act(lbt[:, 0:8], sc[:, 0:8], AF.Sigmoid)
    P.act(lbt[:, 8:16], sc[:, 0:8], AF.Sigmoid, scale=-1.0)
    P.act(sc[:, 16:32], cv(C_AL, 16), AF.Exp)
    P.ts("vector", nea[:, :], sc[:, 16:32], -1.0, None, ALU.mult)
    P.memset("vector", S_h[:, :, :], 0.0)
    P.memset("vector", S_g[:, :, :], 0.0)
    P.memset("vector", Sb_g[:, :, :], 0.0)
    P.memset("vector", halo[:, :, :], 0.0)
    for par in range(2):
        for i in range(2):
            P.memset("vector", hk_z[par][i][:, :], 0.0)
            P.memset("vector", hq_z[par][i][:, :], 0.0)
    for i in range(2):
        P.memset("vector", scT[i][:, :], 0.0)

    def rmsnorm(gcol):
        sq = hid[:, 0:8, :]
        P.act(sq, h[:, :, :], AF.Square)
        pb = ps()
        for kc in range(8):
            P.mm(pb[:, :], ones_b[:, :], hid[:, kc, :], start=(kc == 0), stop=(kc == 7))
        P.act(fs[0][:, 0:TT], pb[:, :], AF.Ln, scale=1.0 / D, bias=EPS)
        P.act(fs[1][:, 0:TT], fs[0][:, 0:TT], AF.Exp, scale=-0.5)
        return fs[1][:, 0:TT]

    def apply_norm(rstd, gcol, dst_fn):
        for kc in range(8):
            P.stt(dst_fn(kc), h[:, kc, :], cv(gcol + kc), rstd, ALU.mult, ALU.mult)

    def ffn(k, gcol):
        rstd = rmsnorm(gcol)
        apply_norm(rstd, gcol, lambda kc: xn[:, kc, :])
        for i in range(11):
            W = piece("f%d_in_%d" % (k, i), (8, 4, 128))
            for jj in range(2):
                j = 2 * i + jj
                pa, pb = ps(), ps()
                for kc in range(8):
                    P.mm(pa[:, :], W[:, kc, 2 * jj, :], xn[:, kc, :], start=(kc == 0), stop=(kc == 7))
                for kc in range(8):
                    P.mm(pb[:, :], W[:, kc, 2 * jj + 1, :], xn[:, kc, :], start=(kc == 0), stop=(kc == 7))
                sa = fs[2 + (j % 2)][:, 0:TT]
                P.act(sa, pa[:, :], AF.Silu)
                P.tt("vector", hid[:, j, :], sa, pb[:, :], ALU.mult)
        for m in range(8):
            W2 = piece("f%d_out_%d" % (k, m), (NFF, 128))
            pb = ps()
            for kc in range(NFF):
                P.mm(pb[:, :], W2[:, kc, :], hid[:, kc, :], start=(kc == 0), stop=(kc == NFF - 1))
            P.stt(h[:, m, :], pb[:, :], 0.5, h[:, m, :], ALU.mult, ALU.add)

    def rsum(out_col, in_):
        P.op("vector", lambda e: e.reduce_sum(out_col.ap, in_.ap, mybir.AxisListType.X), reads=[in_], writes=[out_col])

    def small_rstd(ss_col, out_col, n):
        P.act(out_col, ss_col, AF.Ln, scale=1.0 / n, bias=EPS)
        P.act(out_col, out_col, AF.Exp, scale=-0.5)

    def hgrn_head(j):
        pp = j % 2
        NCK = TT // 64
        W = piece("hg_%d" % j, (8, 4, 128))
        pq, pf = ps(), ps()
        for kc in range(8):
            P.mm(pq[:, :], W[:, kc, 0, :], xn[:, kc, :], start=(kc == 0), stop=(kc == 7))
        for kc in range(8):
            P.mm(pf[:, :], W[:, kc, 1, :], xn[:, kc, :], start=(kc == 0), stop=(kc == 7))
        pg, pi = ps(), ps()
        for b in range(NB):
            for kc in range(8):
                P.mm(pg[:, b * 128:(b + 1) * 128], xn[:, kc, b * 128:(b + 1) * 128], W[:, kc, 2, :], start=(kc == 0), stop=(kc == 7))
        for b in range(NB):
            for kc in range(8):
                P.mm(pi[:, b * 128:(b + 1) * 128], xn[:, kc, b * 128:(b + 1) * 128], W[:, kc, 3, :], start=(kc == 0), stop=(kc == 7))
        q, sg, f, lf, bb, bm, eq, ek = [fs[i][:, 0:TT] for i in range(2, 10)]
        P.act(q, pq[:, :], AF.Silu)
        P.act(sg, pf[:, :], AF.Sigmoid)
        P.act(hg_t[pp][:, :], pg[:, :], AF.Silu)
        P.copy("scalar", hv_t[pp][:, :], pi[:, :])
        P.ts("vector", f, sg, lbt[:, 8 + j:9 + j], lbt[:, j:j + 1], ALU.mult, ALU.add)
        P.act(lf, f, AF.Ln)
        P.op("vector", lambda e: e.tensor_tensor_scan(bb.ap, cst.t[:, C_RM:C_RM + TT], lf.ap, 0.0, ALU.mult, ALU.add),
             reads=[cv(C_RM, TT), lf], writes=[bb])
        b3 = bb.rr("p (c t) -> p c t", c=NCK)
        bm3 = bm.rr("p (c t) -> p c t", c=NCK)
        P.tt("vector", bm3, b3, b3[:, :, 31:32].bc([128, NCK, 64]), ALU.subtract)
        P.act(eq, bm, AF.Exp)
        P.act(ek, bm, AF.Exp, scale=-1.0)
        P.stt(hq_e[pp][:, :], q, 128 ** -0.5, eq, ALU.mult, ALU.mult)
        he4 = hq_e[pp][:, :].rr("p (b t) -> p b t", b=NB)
        for par in range(2):
            P.copy("vector", hq_z[par][pp][:, :].rr("p (b t) -> p b t", b=NB)[:, :, par * 64:(par + 1) * 64], he4[:, :, par * 64:(par + 1) * 64])
        P.ts("vector", f, f, -1.0, 1.0, ALU.mult, ALU.add)
        P.tt("vector", hk_e[pp][:, :], f, ek, ALU.mult)
        hs = hsc[pp]
        P.act(hs[:, 0:8], b3[:, :, 63], AF.Exp)
        P.act(hs[:, 8:16], b3[:, :, 31], AF.Exp)
        P.act(hs[:, 16:24], bm3[:, :, 63], AF.Exp)
        tp = tps()
        for b in range(NB):
            P.transpose(tp[:, b * 128:(b + 1) * 128], hk_e[pp][:, b * 128:(b + 1) * 128], ident_b[:, :])
        P.copy("scalar", hk_z[0][pp][0:64, :], tp[0:64, :])
        P.copy("scalar", hk_z[1][pp][64:128, :], tp[64:128, :])

    def hgrn_rec(j0):
        HS = (0, 1)
        js = [j0, j0 + 1]
        Sjs = [V(S_h.t[:, j, :], (("S_h", j),)) for j in js]
        for b in range(NB):
            bsl = slice(b * 128, (b + 1) * 128)
            pAs, pBs = [ps(), ps()], [ps(), ps()]
            for h in HS:
                P.mm(pAs[h][:, 0:128], hk_e[h][:, bsl], hq_e[h][:, bsl])
            for h in HS:
                P.tt("vector", scT[h][0:64, 0:64], pAs[h][0:64, 0:64], cst[0:64, C_TU:C_TU + 64], ALU.mult)
                P.tt("vector", scT[h][64:128, 64:128], pAs[h][64:128, 64:128], cst[64:128, C_TU + 64:C_TU + 128], ALU.mult)
            for h in HS:
                P.mm(pBs[h][:, 0:128], scT[h][:, :], hv_t[h][:, bsl], start=True, stop=False)
            for par in range(2):
                c = 2 * b + par
                for h in HS:
                    P.act(Smid[h][:, :], Sjs[h], AF.Copy, scale=hsc[h][:, 8 + c:9 + c])
                for h in HS:
                    P.mm(pBs[h][:, 0:128], hq_z[par][h][:, bsl], Smid[h][:, :], start=False, stop=(par == 1))
                    P.mm(pAs[h][:, 128 * (par + 1):128 * (par + 2)], hk_z[par][h][:, bsl], hv_t[h][:, bsl])
                for h in HS:
                    P.ts("vector", Sjs[h], Sjs[h], hsc[h][:, c:c + 1], None, ALU.mult)
                    P.stt(Sjs[h], pAs[h][:, 128 * (par + 1):128 * (par + 2)], hsc[h][:, 16 + c:17 + c], Sjs[h], ALU.mult, ALU.add)
            sAB = []
            for h in HS:
                st["ch"] += 1
                cc = 16 + 4 * (st["ch"] % 8)
                sA, sB = sc.sub(cc, (slice(None), slice(cc, cc + 1))), sc.sub(cc, (slice(None), slice(cc + 1, cc + 2)))
                sAB.append((sA, sB))
                P.act(fs[h][:, 0:128], pBs[h][:, 0:128], AF.Square)
                rsum(sA, fs[h][:, 0:128])
            for h in HS:
                small_rstd(sAB[h][0], sAB[h][1], 128)
            for h in HS:
                P.stt(fs[h][:, 0:128], pBs[h][:, 0:128], sAB[h][1], cv(C_HN, 128), ALU.mult, ALU.mult)
                P.tt("vector", o_bf[h][:, :], fs[h][:, 0:128], hg_t[h][:, bsl], ALU.mult)
            for h in HS:
                tp2 = tps()
                P.transpose(tp2[:, 0:128], o_bf[h][:, :], ident_b[:, :])
                P.copy("scalar", oh_fm[:, js[h], bsl], tp2[:, 0:128])

    def gdn_scalars():
        Wab = piece("ab", (8, 32))
        pab = ps()
        for b in range(NB):
            for kc in range(8):
                P.mm(pab[:, b * 32:(b + 1) * 32], xn[:, kc, b * 128:(b + 1) * 128], Wab[:, kc, :], start=(kc == 0), stop=(kc == 7))
        p3 = pab[:, 0:128].rr("p (b c) -> p b c", b=NB)
        z, g_, be = sm[0][:, :], sm[1][:, :], sm[2][:, :]
        z3 = z.rr("p (b c) -> p b c", b=NB)
        P.tt("vector", z3, p3[:, :, 0:16], cv(C_DT, 16).unsq(1).bc([128, NB, 16]), ALU.add)
        P.act(z, z, AF.Exp)
        P.act(z, z, AF.Ln, bias=1.0)
        P.tt("vector", g_.rr("p (b c) -> p b c", b=NB), z3, nea[:, :].unsq(1).bc([128, NB, 16]), ALU.mult)
        P.act(be.rr("p (b c) -> p b c", b=NB), p3[:, :, 16:32], AF.Sigmoid)
        pg_ = ps()
        P.mm(pg_[:, 0:64], cv(C_TU, 128), g_)
        P.mm(pg_[:, 64:128], cv(C_ONE, 128), g_)
        gam, egam, begam, egk, egend = [sm[i][:, :] for i in range(3, 8)]
        P.copy("vector", gam, pg_[:, 0:64])
        P.act(egam, gam, AF.Exp)
        P.tt("vector", begam, be, egam, ALU.mult)
        P.tt("vector", egk, pg_[:, 64:128], gam, ALU.subtract)
        P.act(egk, egk, AF.Exp)
        P.copy("vector", egend, pg_[:, 64:128])
        P.act(egend, egend, AF.Exp)
        return g_, be, egam, begam, egk, egend

    def gdn_head(j, scal):
        g_, be, egam, begam, egk, egend = scal
        W = piece("gd_%d" % j, (8, 4, 128))
        if GDN_LEVEL < 1:
            piece("gz_%d" % j, (8, 2, 128))
            return
        gidx = (j, 8 + j, 16 + 2 * j, 17 + 2 * j)
        ys = []
        for c in range(4):
            pc = ps()
            for kc in range(8):
                P.mm(pc[:, :], W[:, kc, c, :], xn[:, kc, :], start=(kc == 0), stop=(kc == 7))
            cb = fs[2 + c]
            gi = gidx[c]
            P.copy("vector", cb[:, 0:3], halo[:, gi, 0:3])
            P.copy("scalar", cb[:, 3:3 + TT], pc[:, :])
            if GDN_SUB >= 1:
                P.copy("vector", halo[:, gi, 0:3], cb[:, TT:TT + 3])
            acc = fs[6 + c][:, 0:TT]
            if GDN_SUB < 2:
                ys.append(acc)
                continue
            if c >= 2 and POOL_CONV:
                tmpc = fs[c - 2][:, 0:TT]
                P.ts("gpsimd", acc, cb[:, 3:3 + TT], cv(C_CW + gi * 4 + 3), 0.0, ALU.mult, ALU.add)
                for tpi in range(3):
                    P.ts("gpsimd", tmpc, cb[:, tpi:tpi + TT], cv(C_CW + gi * 4 + tpi), 0.0, ALU.mult, ALU.add)
                    P.tt("gpsimd", acc, acc, tmpc, ALU.add)
            else:
                P.ts("vector", acc, cb[:, 3:3 + TT], cv(C_CW + gi * 4 + 3), None, ALU.mult)
                for tpi in range(3):
                    P.stt(acc, cb[:, tpi:tpi + TT], cv(C_CW + gi * 4 + tpi), acc, ALU.mult, ALU.add)
            ys.append(acc)
        if GDN_SUB < 3:
            piece("gz_%d" % j, (8, 2, 128))
            return
        for c, dst, scl in ((0, gq_f, 128 ** -0.5), (1, gk_f, 1.0)):
            y = fs[2 + c][:, 0:TT]
            P.act(y, ys[c], AF.Silu)
            P.act(bs[c][:, :], y, AF.Square)
            if GDN_SUB < 4:
                continue
            pn = ps()
            P.mm(pn[:, :], ones_b[:, :], bs[c][:, :])
            rn = ys[c]
            P.act(rn, pn[:, :], AF.Ln, bias=EPS)
            P.act(rn, rn, AF.Exp, scale=-0.5)
            if GDN_SUB < 5:
                continue
            P.stt(dst[:, :], y, scl, rn, ALU.mult, ALU.mult)
        if GDN_SUB >= 6:
            for vh in range(2):
                P.act(gv_f[vh][:, :], ys[2 + vh], AF.Silu)
        Wz = piece("gz_%d" % j, (8, 2, 128))
        if GDN_LEVEL < 2:
            return
        pz = [ps(), ps()]
        for b in range(NB):
            for kc in range(8):
                P.mm(pz[b // 2][:, (b % 2) * 256:(b % 2) * 256 + 256], xn[:, kc, b * 128:(b + 1) * 128],
                     Wz[:, kc, :, :].rr("p a b -> p (a b)"), start=(kc == 0), stop=(kc == 7))
        for i in range(2):
            P.act(gz_t[:, 2 * i:2 * i + 2, :, :].rr("p a b c -> p (a b c)"), pz[i][:, :], AF.Silu)
        tk = tps()
        for b in range(NB):
            P.transpose(tk[:, b * 128:(b + 1) * 128], gk_f[:, b * 128:(b + 1) * 128], ident_b[:, :])
        r3 = lambda v: v.rr("p (b t) -> p b t", b=NB)
        sc3 = lambda v, hh: v.rr("p (b c) -> p b c", b=NB)[:, :, hh:hh + 1].bc([128, NB, 128])
        for vh in range(2):
            hh = 2 * j + vh
            P.tt("vector", r3(kbg_t[vh][:, :]), r3(tk), sc3(begam, hh), ALU.mult)
            P.tt("vector", r3(kte_t[vh][:, :]), r3(tk), sc3(egk, hh), ALU.mult)
            tv = tps()
            for b in range(NB):
                P.transpose(tv[:, b * 128:(b + 1) * 128], gv_f[vh][:, b * 128:(b + 1) * 128], ident_b[:, :])
            P.tt("vector", r3(bv_t[vh][:, :]), r3(tv), sc3(be, hh), ALU.mult)
        g3 = g_.rr("p (b c) -> p b c", b=NB)
        for b in range(NB):
            P.tt("vector", Gm[:, 2 * b:2 * b + 2, :], cv(C_SL, 128).unsq(1).bc([128, 2, 128]),
                 g3[:, b, 2 * j:2 * j + 2].unsq(2).bc([128, 2, 128]), ALU.mult)
        for i in range(2):
            pd = ps()
            P.mm(pd[:, :], cv(C_TU, 128), Gm[:, 4 * i:4 * i + 4, :].rr("p a b -> p (a b)"))
            Dc = fs[i][:, 0:TT]
            P.ts("vector", Dc, pd[:, :], -80.0, None, ALU.max)
            P.act(Ex[:, 4 * i:4 * i + 4, :].rr("p a b -> p (a b)"), Dc, AF.Exp)
        P.tt("vector", Ls[:, :, :], Ex[:, :, :], cv(C_SL, 128).unsq(1).bc([128, 8, 128]), ALU.mult)
        P.tt("vector", Lm[:, :, :], Ex[:, :, :], cv(C_TL, 128).unsq(1).bc([128, 8, 128]), ALU.mult)
        for b in range(NB):
            bsl = slice(b * 128, (b + 1) * 128)
            pk_ = ps()
            P.mm(pk_[:, 0:128], gk_f[:, bsl], gk_f[:, bsl])
            P.mm(pk_[:, 128:256], gq_f[:, bsl], gk_f[:, bsl])
            P.copy("scalar", Gm[:, 2 * b:2 * b + 2, :].rr("p a b -> p (a b)"), pk_[:, 0:256])
        F32R = mybir.dt.float32r
        CH_R = (lambda v: v.bitcast(F32R)) if CHAIN_F32R else (lambda v: v)
        Aall = Ex
        for vh in range(2):
            tb = tps()
            for b in range(NB):
                hh = 2 * j + vh
                col = b * 16 + hh
                q_ = 2 * b + vh
                P.stt(CH_R(Aall[:, q_, :]), Gm[:, 2 * b, :], be[:, col:col + 1], Ls[:, q_, :], ALU.mult, ALU.mult)
                P.tt("vector", qkl[b % 2][:, :], Gm[:, 2 * b + 1, :], Lm[:, q_, :], ALU.mult)
                P.transpose(tb[:, b * 128:(b + 1) * 128], qkl[b % 2][:, :], ident_b[:, :])
            P.copy("scalar", qklT_all[:, vh, :, :].rr("p a b -> p (a b)"), tb)
        identF = cv(C_ID, 128)
        Bv = [fs[4][:, 0:TT].rr("p (a b) -> p a b", a=NB), fs[5][:, 0:TT].rr("p (a b) -> p a b", a=NB)]
        Avs = [V(Ex.t[:, vh:8:2, :], (("Ex", 0),)) for vh in range(2)]
        X2 = [Gm[:, :, :], X2b[:, :, :]]
        for vh in range(2):
            pT = ps()
            for b in range(NB):
                P.transpose(pT[:, b * 128:(b + 1) * 128], Ex[:, 2 * b + vh, :], identF)
            P.copy("vector", CH_R(Bv[vh].rr("p a b -> p (a b)")), pT[:, :])
        for k in range(7):
            if k == 0:
                for vh in range(2):
                    P.tt("vector", CH_R(X2[vh][:, 0:NB, :]), Bv[vh], cv(C_MK, 128).unsq(1).bc([128, NB, 128]), ALU.mult)
                    P.tt("vector", CH_R(X2[vh][:, NB:2 * NB, :]), Avs[vh], cv(C_MK + 128, 128).unsq(1).bc([128, NB, 128]), ALU.mult)
                    P.tt("vector", CH_R(TT2[vh][:, :, :]), identF.unsq(1).bc([128, 2 * NB, 128]), X2[vh][:, :, :], ALU.subtract)
                continue
            pXs = []
            for vh in range(2):
                pX, pX2 = ps(), ps()
                for b in range(NB):
                    P.mm(pX[:, b * 128:(b + 1) * 128], CH_R(Avs[vh][:, b, :]), CH_R(TT2[vh][:, b, :]))
                for b in range(NB):
                    P.mm(pX2[:, b * 128:(b + 1) * 128], CH_R(Bv[vh][:, b, :]), CH_R(TT2[vh][:, NB + b, :]))
                pXs.append((pX, pX2))
            for vh in range(2):
                pX, pX2 = pXs[vh]
                P.tt("vector", CH_R(X2[vh][:, 0:NB, :]), pX[:, :].rr("p (a b) -> p a b", a=NB),
                     cv(C_MK + 256 * k, 128).unsq(1).bc([128, NB, 128]), ALU.mult)
                P.tt("vector", CH_R(X2[vh][:, NB:2 * NB, :]), pX2[:, :].rr("p (a b) -> p a b", a=NB),
                     cv(C_MK + 256 * k + 128, 128).unsq(1).bc([128, NB, 128]), ALU.mult)
            pYs = []
            for vh in range(2):
                pY, pY2 = ps(), ps()
                for b in range(NB):
                    P.mm(pY[:, b * 128:(b + 1) * 128], CH_R(TT2[vh][:, NB + b, :]), CH_R(X2[vh][:, b, :]))
                for b in range(NB):
                    P.mm(pY2[:, b * 128:(b + 1) * 128], CH_R(TT2[vh][:, b, :]), CH_R(X2[vh][:, NB + b, :]))
                pYs.append((pY, pY2))
            for vh in range(2):
                pY, pY2 = pYs[vh]
                P.tt("vector", CH_R(TT2[vh][:, 0:NB, :].rr("p a b -> p (a b)")), TT2[vh][:, 0:NB, :].rr("p a b -> p (a b)"), pY[:, :], ALU.subtract)
                P.tt("vector", CH_R(TT2[vh][:, NB:2 * NB, :].rr("p a b -> p (a b)")), TT2[vh][:, NB:2 * NB, :].rr("p a b -> p (a b)"), pY2[:, :], ALU.subtract)
        for vh in range(2):
            P.copy("vector", Ttb_g[:, vh, :, :], TT2[vh][:, 0:NB, :])
        VH = (0, 1)
        hhs = [2 * j + vh for vh in VH]
        Sfs = [V(S_g.t[:, hh, :], (("S_g", hh),)) for hh in hhs]
        Sbs = [V(Sb_g.t[:, hh, :], (("Sb_g", hh),)) for hh in hhs]
        for b in range(NB):
            bsl = slice(b * 128, (b + 1) * 128)
            cols = [b * 16 + hh for hh in hhs]
            pus = [ps(), ps()]
            for vh in VH:
                P.mm(pus[vh][:, 0:128], Ttb_g[:, vh, b, :], bv_t[vh][:, bsl])
                P.mm(pus[vh][:, 128:256], kbg_t[vh][:, bsl], Ttb_g[:, vh, b, :])
            for vh in VH:
                P.copy("scalar", u_sb[vh][:, :], pus[vh][:, 0:128])
                P.copy("scalar", wT_sb[vh][:, :], pus[vh][:, 128:256])
            prs = [ps(), ps()]
            for vh in VH:
                P.mm(prs[vh][:, 0:128], wT_sb[vh][:, :], Sbs[vh])
            for vh in VH:
                P.tt("vector", vnew[vh][:, :], u_sb[vh][:, :], prs[vh][:, 0:128], ALU.subtract)
            for vh in VH:
                P.mm(prs[vh][:, 128:256], gq_f[:, bsl], Sbs[vh])
                P.mm(prs[vh][:, 256:384], qklT_all[:, vh, b, :], vnew[vh][:, :])
                P.mm(prs[vh][:, 384:512], kte_t[vh][:, bsl], vnew[vh][:, :])
            for vh in VH:
                P.stt(Sfs[vh], Sfs[vh], egend[:, cols[vh]:cols[vh] + 1], prs[vh][:, 384:512], ALU.mult, ALU.add)
                P.copy("scalar", Sbs[vh], Sfs[vh])
            for vh in VH:
                P.act(o_sb[vh][:, :], prs[vh][:, 128:256], AF.Copy, scale=egam[:, cols[vh]:cols[vh] + 1])
                P.tt("vector", o_sb[vh][:, :], o_sb[vh][:, :], prs[vh][:, 256:384], ALU.add)
            sAB = []
            for vh in VH:
                st["ch"] += 1
                cc = 16 + 4 * (st["ch"] % 8)
                sA, sB = sc.sub(cc, (slice(None), slice(cc, cc + 1))), sc.sub(cc, (slice(None), slice(cc + 1, cc + 2)))
                sAB.append((sA, sB))
                P.act(fs[vh][:, 0:128], o_sb[vh][:, :], AF.Square)
                rsum(sA, fs[vh][:, 0:128])
            for vh in VH:
                small_rstd(sAB[vh][0], sAB[vh][1], 128)
            for vh in VH:
                P.stt(fs[vh][:, 0:128], o_sb[vh][:, :], sAB[vh][1], cv(C_GN, 128), ALU.mult, ALU.mult)
                P.tt("vector", vnew[vh][:, :], fs[vh][:, 0:128], gz_t[:, b, vh, :], ALU.mult)
            for vh in VH:
                tp2 = tps()
                P.transpose(tp2[:, 0:128], vnew[vh][:, :], ident_b[:, :])
                P.copy("scalar", og_fm[:, hhs[vh], bsl], tp2[:, 0:128])

    def merge_and_out():
        for i in range(2):
            Wgh = piece("gh_%d" % i, (8, 4, 128))
            Wbh = piece("bh_%d" % i, (4, 8, 128))
            for m in range(4):
                pgt, py = ps(), ps()
                for kc in range(8):
                    P.mm(pgt[:, :], Wgh[:, kc, m, :], xn[:, kc, :], start=(kc == 0), stop=(kc == 7))
                for hd in range(8):
                    P.mm(py[:, :], Wbh[:, m, hd, :], oh_fm[:, hd, :], start=(hd == 0), stop=(hd == 7))
                sg = fs[2 + (m % 2)][:, 0:TT]
                P.act(sg, pgt[:, :], AF.Sigmoid)
                P.tt("vector", ymf[:, m, :], sg, py[:, :], ALU.mult)
            Wgg = piece("gg_%d" % i, (8, 4, 128))
            for m in range(4):
                if m % 2 == 0:
                    Wbg = piece("bg_%d" % (2 * i + m // 2), (2, 16, 128))
                pgt, py = ps(), ps()
                for kc in range(8):
                    P.mm(pgt[:, :], Wgg[:, kc, m, :], xn[:, kc, :], start=(kc == 0), stop=(kc == 7))
                for hd in range(16):
                    P.mm(py[:, :], Wbg[:, m % 2, hd, :], og_fm[:, hd, :], start=(hd == 0), stop=(hd == 15))
                sg = fs[2 + (m % 2)][:, 0:TT]
                P.act(sg, pgt[:, :], AF.Sigmoid)
                P.tt("vector", sg, sg, py[:, :], ALU.mult)
                P.tt("vector", ym[:, 4 * i + m, :], sg, ymf[:, m, :], ALU.add)
        for i in range(2):
            Wo = piece("wo_%d" % i, (4, 8, 128))
            for m in range(4):
                po = ps()
                for kc in range(8):
                    P.mm(po[:, :], Wo[:, m, kc, :], ym[:, kc, :], start=(kc == 0), stop=(kc == 7))
                P.tt("vector", h[:, 4 * i + m, :], h[:, 4 * i + m, :], po[:, :], ALU.add)

    for t in range(n_tiles):
        tsl = slice(t * TT, (t + 1) * TT)
        P.dma("sync", h[:, :, :], x_d[:, tsl].rr("(kc p) t -> p kc t", p=128), "xin")
        if "ffn1" in stages:
            ffn(1, C_G1)
        if t == 0:
            tap("h1", h[:, :, :])
        rstd = rmsnorm(C_GM)
        apply_norm(rstd, C_GM, lambda kc: xn[:, kc, :])
        HGT = os.environ.get("HG_TILES")
        HGH = int(os.environ.get("HG_HEADS", "8"))
        for _ in range(int(os.environ.get("PS_SHIFT", "0"))):
            ps()
        if "hgrn" in stages and (HGT is None or str(t) in HGT.split(",")):
            for j in range(8):
                hgrn_head(j)
                if j % 2 == 1:
                    hgrn_rec(j - 1)
        else:
            for j in range(8):
                piece("hg_%d" % j, (8, 4, 128))
        if t == 0:
            tap("oh", oh_fm[:, :, :])
        if "gdn" in stages:
            scal = gdn_scalars()
            for j in range(8):
                gdn_head(j, scal)
        else:
            piece("ab", (8, 32))
            for j in range(8):
                piece("gd_%d" % j, (8, 4, 128))
                piece("gz_%d" % j, (8, 2, 128))
        if t == 0:
            tap("og", og_fm[:, :, :])
        if "merge" in stages:
            merge_and_out()
        else:
            for i in range(2):
                for nm in ("gh_%d" % i, "bh_%d" % i, "gg_%d" % i, "bg_%d" % (2 * i), "bg_%d" % (2 * i + 1)):
                    piece(nm, (8, 4, 128))
            piece("wo_0", (4, 8, 128))
            piece("wo_1", (4, 8, 128))
        if t == 0:
            tap("h2", h[:, :, :])
        if "ffn2" in stages:
            ffn(2, C_G2)
        else:
            for i in range(11):
                piece("f2_in_%d" % i, (8, 4, 128))
            for m in range(8):
                piece("f2_out_%d" % m, (NFF, 128))
        rstd = rmsnorm(C_GF)
        apply_norm(rstd, C_GF, lambda kc: h[:, kc, :])
        P.dma("sync", out_d[:, tsl].rr("(kc p) t -> p kc t", p=128), h[:, :, :], "xout")
    P.emit()
    P.close()
    return nc


_CACHE = {}


def kernel(**inputs):
    inp = {k: np.asarray(v) for k, v in inputs.items()}
    x = inp["x"]
    B, S, _ = x.shape
    if "nc" not in _CACHE:
        _CACHE["nc"] = build_program(S)
    nc = _CACHE["nc"]
    ws = pack_weights(inp)
    cst = pack_consts(inp)
    in_maps = [{"xT": np.ascontiguousarray(x[b].T), "ws": ws, "cst": cst} for b in range(B)]
    res = run_bass_kernel_spmd(nc, in_maps, core_ids=list(range(B)))
    out = np.stack([np.asarray(res.results[b]["outT"]).T for b in range(B)], axis=0)
    return np.ascontiguousarray(out.astype(np.float32))
```

```python
import os
import numpy as np
import concourse.bass as bass
import concourse.mybir as mybir
from concourse.bass_utils import run_bass_kernel_spmd

F32 = mybir.dt.float32
BF16 = mybir.dt.bfloat16
AF = mybir.ActivationFunctionType
ALU = mybir.AluOpType


STRICT_SAME_ENGINE = True


class V:
    __slots__ = ("ap", "keys")

    def __init__(self, ap, keys):
        self.ap = ap
        self.keys = keys

    def __getitem__(self, idx):
        return V(self.ap[idx], self.keys)

    def bc(self, shape):
        return V(self.ap.to_broadcast(list(shape)), self.keys)

    def unsq(self, axis):
        return V(self.ap.unsqueeze(axis), self.keys)

    def bitcast(self, dt):
        return V(self.ap.bitcast(dt), self.keys)

    def rr(self, s, **kw):
        return V(self.ap.rearrange(s, **kw), self.keys)


class Buf:
    def __init__(self, t, name, nsub=1):
        self.t = t
        self.name = name

    def __getitem__(self, idx):
        return V(self.t[idx], ((self.name, 0),))

    def sub(self, k, idx):
        return V(self.t[idx], ((self.name, k),))


class Op:
    __slots__ = ("eng", "fn", "deps", "needs_inc", "sem", "val", "is_dma", "idx")

    def __init__(self, eng, fn, is_dma):
        self.eng = eng
        self.fn = fn
        self.deps = []
        self.needs_inc = False
        self.sem = None
        self.val = None
        self.is_dma = is_dma


class Prog:
    ENGS = ("sync", "scalar", "vector", "gpsimd", "tensor")

    def __init__(self, nc):
        self.nc = nc
        self.ops = {e: [] for e in self.ENGS}
        self.state = {}
        self.stack = []
        self.dma_sems = {}
        self.dma_counts = {}
        self.eng_sems = {}
        self.all_dma_ops = []

    def sbuf(self, name, shape, dt):
        g = self.nc.sbuf_tensor(name, list(shape), dt)
        t = g.__enter__()
        self.stack.append(g)
        return Buf(t, name)

    def psum(self, name, shape, dt):
        g = self.nc.psum_tensor(name, list(shape), dt)
        t = g.__enter__()
        self.stack.append(g)
        return Buf(t, name)

    def sem(self, name):
        g = self.nc.semaphore(name)
        s = g.__enter__()
        self.stack.append(g)
        return s

    def op(self, eng, fn, reads=(), writes=(), dma_key=None):
        o = Op(eng, fn, dma_key is not None)
        rkey = ("dma", dma_key) if dma_key is not None else eng
        deps = {}
        for v in reads:
            for k in v.keys:
                st = self.state.get(k)
                if st and st[0] is not None:
                    deps[id(st[0])] = (st[0], "raw")
        for v in writes:
            for k in v.keys:
                st = self.state.get(k)
                if st:
                    if st[0] is not None and id(st[0]) not in deps:
                        deps[id(st[0])] = (st[0], "waw")
                    for r in st[1].values():
                        if id(r) not in deps:
                            deps[id(r)] = (r, "war")
        for d, kind in deps.values():
            if d is o:
                continue
            if not d.is_dma and not o.is_dma and d.eng == eng:
                if eng == "tensor":
                    continue
                if kind == "waw" and not STRICT_SAME_ENGINE:
                    continue
                if kind == "war" and not STRICT_SAME_ENGINE:
                    continue
            o.deps.append(d)
            d.needs_inc = True
        for v in reads:
            for k in v.keys:
                st = self.state.setdefault(k, [None, {}])
                st[1][rkey] = o
        for v in writes:
            for k in v.keys:
                self.state[k] = [o, {}]
        if dma_key is not None:
            if dma_key not in self.dma_sems:
                self.dma_sems[dma_key] = self.sem("d_" + str(dma_key))
                self.dma_counts[dma_key] = 0
            self.dma_counts[dma_key] += 16
            o.sem = self.dma_sems[dma_key]
            o.val = self.dma_counts[dma_key]
            o.needs_inc = True
            self.all_dma_ops.append(o)
        self.ops[eng].append(o)
        return o

    def dma(self, eng, out, in_, key):
        return self.op(eng, lambda e: e.dma_start(out=out.ap, in_=in_.ap),
                       reads=[in_], writes=[out], dma_key=key)

    def mm(self, out, lhsT, rhs, start=True, stop=True, extra_reads=()):
        return self.op("tensor", lambda e: e.matmul(out.ap, lhsT.ap, rhs.ap, start=start, stop=stop),
                       reads=[lhsT, rhs] + list(extra_reads), writes=[out])

    def transpose(self, out, in_, ident):
        return self.op("tensor", lambda e: e.transpose(out.ap, in_.ap, ident.ap),
                       reads=[in_, ident], writes=[out])

    def act(self, out, in_, func, bias=None, scale=None, accum_out=None, eng="scalar"):
        reads = [in_]
        kw = {}
        if bias is not None:
            if isinstance(bias, V):
                reads.append(bias)
                kw["bias"] = bias.ap
            else:
                kw["bias"] = bias
        if scale is not None:
            if isinstance(scale, V):
                reads.append(scale)
                kw["scale"] = scale.ap
            else:
                kw["scale"] = scale
        writes = [out]
        if accum_out is not None:
            writes.append(accum_out)
            kw["accum_out"] = accum_out.ap
        return self.op("scalar", lambda e: e.activation(out.ap, in_.ap, func, **kw),
                       reads=reads, writes=writes)

    def tt(self, eng, out, in0, in1, op):
        return self.op(eng, lambda e: e.tensor_tensor(out.ap, in0.ap, in1.ap, op),
                       reads=[in0, in1], writes=[out])

    def ts(self, eng, out, in0, s1, s2, op0, op1=None, accum_out=None):
        reads = [in0]
        a1 = s1
        a2 = s2
        if isinstance(s1, V):
            reads.append(s1)
            a1 = s1.ap
        if isinstance(s2, V):
            reads.append(s2)
            a2 = s2.ap
        writes = [out]
        kw = {}
        if accum_out is not None:
            writes.append(accum_out)
            kw["accum_out"] = accum_out.ap
        if op1 is None:
            return self.op(eng, lambda e: e.tensor_scalar(out.ap, in0.ap, a1, a2, op0, **kw),
                           reads=reads, writes=writes)
        return self.op(eng, lambda e: e.tensor_scalar(out.ap, in0.ap, a1, a2, op0, op1, **kw),
                       reads=reads, writes=writes)

    def stt(self, out, in0, scalar, in1, op0, op1, eng="vector"):
        reads = [in0, in1]
        a = scalar
        if isinstance(scalar, V):
            reads.append(scalar)
            a = scalar.ap
        return self.op(eng, lambda e: e.scalar_tensor_tensor(out.ap, in0.ap, a, in1.ap, op0, op1),
                       reads=reads, writes=[out])

    def copy(self, eng, out, in_):
        if eng == "scalar":
            return self.op(eng, lambda e: e.copy(out.ap, in_.ap), reads=[in_], writes=[out])
        return self.op(eng, lambda e: e.tensor_copy(out.ap, in_.ap), reads=[in_], writes=[out])

    def memset(self, eng, out, val):
        return self.op(eng, lambda e: e.memset(out.ap, val), reads=[], writes=[out])

    def emit(self, final_wait_ops=()):
        nc = self.nc
        for eng in self.ENGS:
            cnt = 0
            for o in self.ops[eng]:
                if o.is_dma:
                    continue
                if o.needs_inc:
                    if eng not in self.eng_sems:
                        self.eng_sems[eng] = self.sem("e_" + eng)
                    cnt += 1
                    o.sem = self.eng_sems[eng]
                    o.val = cnt
        final_dmas = list(self.all_dma_ops)
        with nc.Block() as block:
            def run(eng_name):
                def body(e):
                    waited = {}
                    for o in self.ops[eng_name]:
                        need = {}
                        for d in o.deps:
                            k = id(d.sem)
                            if k not in need or need[k][1] < d.val:
                                need[k] = (d.sem, d.val)
                        for k, (s, v) in need.items():
                            if waited.get(k, 0) >= v:
                                continue
                            e.wait_ge(s, v)
                            waited[k] = v
                        ins = o.fn(e)
                        if o.needs_inc:
                            ins.then_inc(o.sem, 16 if o.is_dma else 1)
                    if eng_name == "sync":
                        last = {}
                        for o in final_dmas:
                            last[id(o.sem)] = (o.sem, max(o.val, last.get(id(o.sem), (None, 0))[1]))
                        for k, (s, v) in last.items():
                            if waited.get(k, 0) < v:
                                e.wait_ge(s, v)
                return body
            block.sync(run("sync"))
            block.scalar(run("scalar"))
            block.vector(run("vector"))
            block.gpsimd(run("gpsimd"))
            block.tensor(run("tensor"))

    def close(self):
        while self.stack:
            g = self.stack.pop()
            g.__exit__(None, None, None)


import os

D = 1024
DFF = 2816
NFF = DFF // 128
TT = 512
NB = TT // 128
EPS = 1e-6
PW = 4096
NSLOT = 4
TDT = F32
POOL_CONV = bool(int(os.environ.get('POOL_CONV', '0')))
CHAIN_F32R = bool(int(os.environ.get('CHAIN_F32R', '0')))

O_HQ, O_HF, O_HI, O_HG = 0, 1024, 2048, 3072
O_GQ, O_GK, O_GV, O_GA, O_GB, O_GZ, O_GH, O_GG = 4096, 5120, 6144, 8192, 8208, 8224, 10272, 11296

C_ID, C_ONE, C_TL, C_SL, C_TU, C_RM = 0, 128, 256, 384, 512, 640
C_G1, C_GM, C_G2, C_GF = 1152, 1160, 1168, 1176
C_HN, C_GN = 1184, 1312
C_L0, C_L1 = 1440, 1448
C_CW = 1456
C_AL, C_DT = 1584, 1600
C_MK = 1616
NCST = 1616 + 7 * 256


def piece_names():
    names = []
    for k in (1, 2):
        if k == 2:
            names += ["hg_%d" % j for j in range(8)] + ["ab"]
            for j in range(8):
                names += ["gd_%d" % j, "gz_%d" % j]
            for i in range(2):
                names += ["gh_%d" % i, "bh_%d" % i, "gg_%d" % i, "bg_%d" % (2 * i), "bg_%d" % (2 * i + 1)]
            names += ["wo_0", "wo_1"]
        names += ["f%d_in_%d" % (k, i) for i in range(11)]
        names += ["f%d_out_%d" % (k, m) for m in range(8)]
    return names


def pack_weights(inp):
    names = piece_names()
    ws = np.zeros((len(names), 128, PW), np.float32)

    def put(name, arr):
        a = np.ascontiguousarray(arr).reshape(128, -1)
        ws[names.index(name), :, :a.shape[1]] = a

    for k, (wi, wo) in enumerate(((inp["ffn1_w_in"], inp["ffn1_w_out"]), (inp["ffn2_w_in"], inp["ffn2_w_out"])), 1):
        wr = np.asarray(wi)[0].reshape(8, 128, 2 * DFF)
        for i in range(11):
            cols = []
            for j in (2 * i, 2 * i + 1):
                cols.append(wr[:, :, j * 128:(j + 1) * 128])
                cols.append(wr[:, :, DFF + j * 128:DFF + (j + 1) * 128])
            put("f%d_in_%d" % (k, i), np.stack(cols, axis=2).transpose(1, 0, 2, 3))
        w2 = np.asarray(wo)[0].reshape(NFF, 128, D)
        for m in range(8):
            put("f%d_out_%d" % (k, m), w2[:, :, m * 128:(m + 1) * 128].transpose(1, 0, 2))
    wr = np.asarray(inp["w_in"])[0].reshape(8, 128, -1)

    def cols(off, j, n=128):
        return wr[:, :, off + j * 128: off + j * 128 + n]

    for j in range(8):
        put("hg_%d" % j, np.stack([cols(O_HQ, j), cols(O_HF, j), cols(O_HG, j), cols(O_HI, j)], axis=2).transpose(1, 0, 2, 3))
        put("gd_%d" % j, np.stack([cols(O_GQ, j), cols(O_GK, j), cols(O_GV, 2 * j), cols(O_GV, 2 * j + 1)], axis=2).transpose(1, 0, 2, 3))
        put("gz_%d" % j, np.stack([cols(O_GZ, 2 * j), cols(O_GZ, 2 * j + 1)], axis=2).transpose(1, 0, 2, 3))
    put("ab", wr[:, :, O_GA:O_GA + 32].transpose(1, 0, 2))
    for i in range(2):
        put("gh_%d" % i, np.stack([cols(O_GH, 4 * i + m) for m in range(4)], axis=2).transpose(1, 0, 2, 3))
        put("gg_%d" % i, np.stack([cols(O_GG, 4 * i + m) for m in range(4)], axis=2).transpose(1, 0, 2, 3))
    bh = np.asarray(inp["w_branch_hgrn"])[0].reshape(8, 128, D)
    for i in range(2):
        put("bh_%d" % i, np.stack([bh[:, :, (4 * i + m) * 128:(4 * i + m + 1) * 128] for m in range(4)], axis=0).transpose(2, 0, 1, 3))
    bg = np.asarray(inp["w_branch_gdn"])[0].reshape(16, 128, D)
    for i in range(4):
        put("bg_%d" % i, np.stack([bg[:, :, (2 * i + m) * 128:(2 * i + m + 1) * 128] for m in range(2)], axis=0).transpose(2, 0, 1, 3))
    wo = np.asarray(inp["w_out"])[0].reshape(8, 128, D)
    for i in range(2):
        put("wo_%d" % i, np.stack([wo[:, :, (4 * i + m) * 128:(4 * i + m + 1) * 128] for m in range(4)], axis=0).transpose(2, 0, 1, 3))
    return ws


def pack_consts(inp):
    c = np.zeros((128, NCST), np.float32)
    r = np.arange(128)
    c[:, C_ID:C_ID + 128] = np.eye(128)
    c[:, C_ONE:C_ONE + 128] = 1.0
    c[:, C_TL:C_TL + 128] = (r[None, :] <= r[:, None])
    c[:, C_SL:C_SL + 128] = (r[None, :] < r[:, None])
    c[:, C_TU:C_TU + 128] = (r[:, None] <= r[None, :])
    rm = np.ones(512, np.float32)
    rm[::64] = 0.0
    c[:, C_RM:C_RM + 512] = rm[None, :]
    for col, key in ((C_G1, "ffn1_norm"), (C_GM, "mix_norm"), (C_G2, "ffn2_norm")):
        c[:, col:col + 8] = np.asarray(inp[key])[0].reshape(8, 128).T
    c[:, C_GF:C_GF + 8] = np.asarray(inp["final_norm"]).reshape(8, 128).T
    c[:, C_HN:C_HN + 128] = np.asarray(inp["hgrn_out_norm"])[0][None, :]
    c[:, C_GN:C_GN + 128] = np.asarray(inp["gdn_out_norm"])[0][None, :]
    lbl = np.asarray(inp["hgrn_lb_logits"])
    c[:, C_L0:C_L0 + 8] = lbl[0].reshape(8, 128).T
    c[:, C_L1:C_L1 + 8] = lbl[1].reshape(8, 128).T
    cw = np.asarray(inp["gdn_conv_w"])[0]
    c[:, C_CW:C_CW + 128] = cw.T.reshape(32, 128, 4).transpose(1, 0, 2).reshape(128, 128)
    for k in range(7):
        n = 1 << k
        tt_, ss_ = r[:, None], r[None, :]
        mk = ((tt_ // (2 * n) == ss_ // (2 * n)) & (tt_ % (2 * n) >= n) & (ss_ % (2 * n) < n)).astype(np.float32)
        c[:, C_MK + 256 * k:C_MK + 256 * k + 128] = mk.T
        c[:, C_MK + 256 * k + 128:C_MK + 256 * (k + 1)] = mk
    c[:, C_AL:C_AL + 16] = np.asarray(inp["gdn_a_log"])[0][None, :]
    c[:, C_DT:C_DT + 16] = np.asarray(inp["gdn_dt_bias"])[0][None, :]
    return c


import os
GDN_LEVEL = int(os.environ.get('GDN_LEVEL', '9'))
GDN_SUB = int(os.environ.get('GDN_SUB', '9'))


def build_program(n_tok, taps=None, stages=("ffn1", "hgrn", "gdn", "merge", "ffn2"), n_pre=0):
    n_tiles = n_tok // TT
    names = piece_names()
    NP = len(names)
    nc = bass.Bass("TRN2", target_bir_lowering=False)
    P = Prog(nc)
    x_d = Buf(nc.dram_tensor("xT", [D, n_tok], F32, kind="ExternalInput").ap(), "xT")
    WSB = bool(int(os.environ.get("WS_BF16", "0")))
    ws_d = Buf(nc.dram_tensor("ws", [NP, 128, PW], BF16 if WSB else F32, kind="ExternalInput").ap(), "ws")
    cst_d = Buf(nc.dram_tensor("cst", [128, NCST], F32, kind="ExternalInput").ap(), "cstd")
    out_d = Buf(nc.dram_tensor("outT", [D, n_tok - n_pre * TT], F32, kind="ExternalOutput").ap(), "outT")
    tap_d = {}
    if taps:
        for nm, shp in taps.items():
            tdt = BF16 if nm in ("oh", "og") else F32
            tap_d[nm] = Buf(nc.dram_tensor("tap_" + nm, list(shp), tdt, kind="ExternalOutput").ap(), "tap_" + nm)

    tapped = set()

    def tap(nm, v):
        if taps and nm in tap_d and nm not in tapped:
            tapped.add(nm)
            P.dma("sync", V(tap_d[nm].t, ((tap_d[nm].name, 0),)), v, "tap_" + nm)

    cst = P.sbuf("cst_sb", [128, NCST], F32)
    ident_b = P.sbuf("ident_b", [128, 128], BF16)
    ones_b = P.sbuf("ones_b", [128, 128], BF16)
    tu_b = P.sbuf("tu_b", [128, 128], BF16)
    lbt = P.sbuf("lbt", [128, 16], F32)
    nea = P.sbuf("nea", [128, 16], F32)
    slots = [P.sbuf("slot%d" % i, [128, PW], BF16) for i in range(NSLOT)]
    h = P.sbuf("h", [128, 8, TT], F32)
    xn = P.sbuf("xn", [128, 8, TT], BF16)
    hid = P.sbuf("hid", [128, NFF, TT], BF16)
    fs = [P.sbuf("fs%d" % i, [128, TT + 4], F32) for i in range(10)]
    oh_fm = P.sbuf("oh_fm", [128, 8, TT], BF16)
    og_fm = hid[:, 0:16, :]
    ymf = hid[:, 16:20, :]
    ym = P.sbuf("ym", [128, 8, TT], BF16)
    S_h = P.sbuf("S_h", [128, 8, 128], F32)
    S_g = P.sbuf("S_g", [128, 16, 128], F32)
    Sb_g = P.sbuf("Sb_g", [128, 16, 128], BF16)
    halo = P.sbuf("halo", [128, 32, 4], F32)
    sm = [P.sbuf("sm%d" % i, [128, 64], F32) for i in range(8)]
    sc = P.sbuf("sc", [128, 64], F32)
    hq_e = [P.sbuf("hq_e%d" % i, [128, TT], BF16) for i in range(2)]
    bs = hq_e
    hk_e = [P.sbuf("hk_e%d" % i, [128, TT], BF16) for i in range(2)]
    hv_t = [P.sbuf("hv_t%d" % i, [128, TT], BF16) for i in range(2)]
    hk_z = [[P.sbuf("hk_z%d_%d" % (par, i), [128, TT], BF16) for i in range(2)] for par in range(2)]
    hq_z = [[P.sbuf("hq_z%d_%d" % (par, i), [128, TT], BF16) for i in range(2)] for par in range(2)]
    hg_t = [P.sbuf("hg_t%d" % i, [128, TT], BF16) for i in range(2)]
    hsc = [P.sbuf("hsc%d" % i, [128, 32], F32) for i in range(2)]
    gq_f = P.sbuf("gq_f", [128, TT], BF16)
    gk_f = P.sbuf("gk_f", [128, TT], BF16)
    gv_f = [P.sbuf("gv_f%d" % i, [128, TT], BF16) for i in range(2)]
    gz_t = P.sbuf("gz_t", [128, NB, 2, 128], BF16)
    kbg_t = [P.sbuf("kbg_t%d" % i, [128, TT], BF16) for i in range(2)]
    kte_t = [P.sbuf("kte_t%d" % i, [128, TT], BF16) for i in range(2)]
    bv_t = [P.sbuf("bv_t%d" % i, [128, TT], BF16) for i in range(2)]
    Gm = P.sbuf("Gm", [128, 8, 128], F32)
    Ex = P.sbuf("Ex", [128, 8, 128], F32)
    Ls = P.sbuf("Ls", [128, 8, 128], BF16)
    Lm = P.sbuf("Lm", [128, 8, 128], BF16)
    NCH = 2
    qkl = [P.sbuf("qkl%d" % c, [128, 128], BF16) for c in range(2)]
    qklT_all = P.sbuf("qklT_all", [128, 2, NB, 128], BF16)
    TT2 = [P.sbuf("TT2_%d" % i, [128, 2 * NB, 128], F32) for i in range(2)]
    X2b = P.sbuf("X2b", [128, 2 * NB, 128], F32)
    Ttb_g = P.sbuf("Ttb_g", [128, 2, NB, 128], BF16)
    u_sb = [P.sbuf("u_sb%d" % c, [128, 128], F32) for c in range(NCH)]
    wT_sb = [P.sbuf("wT_sb%d" % c, [128, 128], BF16) for c in range(NCH)]
    vnew = [P.sbuf("vnew%d" % c, [128, 128], BF16) for c in range(NCH)]
    o_sb = [P.sbuf("o_sb%d" % c, [128, 128], F32) for c in range(NCH)]
    o_bf = vnew
    Smid = wT_sb
    scT = [P.sbuf("scT%d" % c, [128, 128], BF16) for c in range(2)]

    NBK = 6
    banks = [P.psum("bank%d" % i, [128, 512], F32) for i in range(NBK)]
    tbanks = [P.psum("tbank%d" % i, [128, 1024], BF16) for i in range(2)]
    st = {"ps": 0, "tp": 0, "piece": 0, "ch": 0}

    def ps():
        b = banks[st["ps"] % NBK]
        st["ps"] += 1
        return b

    def tps():
        k = st["tp"] % 2
        st["tp"] += 1
        return tbanks[k][:, 0:512]

    USE_WSB = bool(int(os.environ.get("USE_WSB", "1"))) and n_tiles > 1 and not WSB
    if USE_WSB:
        wsb_t = nc.dram_tensor("wsb", [NP, 128, PW], BF16, kind="Internal").ap()

    def piece(name, shape):
        i = st["piece"]
        assert names[i % NP] == name, (names[i % NP], name)
        st["piece"] += 1
        k = i % NSLOT
        sl = slots[k]
        n = 1
        for s_ in shape:
            n *= s_
        pi = i % NP
        if USE_WSB and i >= NP:
            P.dma("sync", sl[:, 0:n], V(wsb_t[pi, :, 0:n], (("wsb", pi),)), "slot%d" % k)
        else:
            P.dma("sync" if WSB else "gpsimd", sl[:, 0:n], ws_d[pi, :, 0:n], "slot%d" % k)
            if USE_WSB:
                P.dma("sync", V(wsb_t[pi, :, 0:n], (("wsb", pi),)), sl[:, 0:n], "wb%d" % k)
        v = sl[:, 0:n]
        if len(shape) == 2:
            return v.rr("p (a b) -> p a b", a=shape[0])
        if len(shape) == 3:
            return v.rr("p (a b c) -> p a b c", a=shape[0], b=shape[1])
        return v

    cv = lambda c0, n=1: cst[:, c0:c0 + n]

    P.dma("sync", cst[:, :], cst_d[:, :], "cst")
    P.copy("vector", ident_b[:, :], cv(C_ID, 128))
    P.copy("vector", ones_b[:, :], cv(C_ONE, 128))
    P.copy("vector", tu_b[:, :], cv(C_TU, 128))
    P.tt("vector", sc[:, 0:8], cv(C_L0, 8), cv(C_L1, 8), ALU.subtract)
    P.act(lbt[:, 0:8], sc[:, 0:8], AF.Sigmoid)
    P.act(lbt[:, 8:16], sc[:, 0:8], AF.Sigmoid, scale=-1.0)
    P.act(sc[:, 16:32], cv(C_AL, 16), AF.Exp)
    P.ts("vector", nea[:, :], sc[:, 16:32], -1.0, None, ALU.mult)
    P.memset("vector", S_h[:, :, :], 0.0)
    P.memset("vector", S_g[:, :, :], 0.0)
    P.memset("vector", Sb_g[:, :, :], 0.0)
    P.memset("vector", halo[:, :, :], 0.0)
    for par in range(2):
        for i in range(2):
            P.memset("vector", hk_z[par][i][:, :], 0.0)
            P.memset("vector", hq_z[par][i][:, :], 0.0)
    for i in range(2):
        P.memset("vector", scT[i][:, :], 0.0)

    def rmsnorm(gcol):
        sq = hid[:, 0:8, :]
        P.act(sq, h[:, :, :], AF.Square)
        pb = ps()
        for kc in range(8):
            P.mm(pb[:, :], ones_b[:, :], hid[:, kc, :], start=(kc == 0), stop=(kc == 7))
        P.act(fs[0][:, 0:TT], pb[:, :], AF.Ln, scale=1.0 / D, bias=EPS)
        P.act(fs[1][:, 0:TT], fs[0][:, 0:TT], AF.Exp, scale=-0.5)
        return fs[1][:, 0:TT]

    def apply_norm(rstd, gcol, dst_fn):
        for kc in range(8):
            P.stt(dst_fn(kc), h[:, kc, :], cv(gcol + kc), rstd, ALU.mult, ALU.mult)

    def ffn(k, gcol):
        rstd = rmsnorm(gcol)
        apply_norm(rstd, gcol, lambda kc: xn[:, kc, :])
        for i in range(11):
            W = piece("f%d_in_%d" % (k, i), (8, 4, 128))
            for jj in range(2):
                j = 2 * i + jj
                pa, pb = ps(), ps()
                for kc in range(8):
                    P.mm(pa[:, :], W[:, kc, 2 * jj, :], xn[:, kc, :], start=(kc == 0), stop=(kc == 7))
                for kc in range(8):
                    P.mm(pb[:, :], W[:, kc, 2 * jj + 1, :], xn[:, kc, :], start=(kc == 0), stop=(kc == 7))
                sa = fs[2 + (j % 2)][:, 0:TT]
                P.act(sa, pa[:, :], AF.Silu)
                P.tt("vector", hid[:, j, :], sa, pb[:, :], ALU.mult)
        for m in range(8):
            W2 = piece("f%d_out_%d" % (k, m), (NFF, 128))
            pb = ps()
            for kc in range(NFF):
                P.mm(pb[:, :], W2[:, kc, :], hid[:, kc, :], start=(kc == 0), stop=(kc == NFF - 1))
            P.stt(h[:, m, :], pb[:, :], 0.5, h[:, m, :], ALU.mult, ALU.add)

    def rsum(out_col, in_):
        P.op("vector", lambda e: e.reduce_sum(out_col.ap, in_.ap, mybir.AxisListType.X), reads=[in_], writes=[out_col])

    def small_rstd(ss_col, out_col, n):
        P.act(out_col, ss_col, AF.Ln, scale=1.0 / n, bias=EPS)
        P.act(out_col, out_col, AF.Exp, scale=-0.5)

    def hgrn_head(j):
        pp = j % 2
        NCK = TT // 64
        W = piece("hg_%d" % j, (8, 4, 128))
        pq, pf = ps(), ps()
        for kc in range(8):
            P.mm(pq[:, :], W[:, kc, 0, :], xn[:, kc, :], start=(kc == 0), stop=(kc == 7))
        for kc in range(8):
            P.mm(pf[:, :], W[:, kc, 1, :], xn[:, kc, :], start=(kc == 0), stop=(kc == 7))
        pg, pi = ps(), ps()
        for b in range(NB):
            for kc in range(8):
                P.mm(pg[:, b * 128:(b + 1) * 128], xn[:, kc, b * 128:(b + 1) * 128], W[:, kc, 2, :], start=(kc == 0), stop=(kc == 7))
        for b in range(NB):
            for kc in range(8):
                P.mm(pi[:, b * 128:(b + 1) * 128], xn[:, kc, b * 128:(b + 1) * 128], W[:, kc, 3, :], start=(kc == 0), stop=(kc == 7))
        q, sg, f, lf, bb, bm, eq, ek = [fs[i][:, 0:TT] for i in range(2, 10)]
        P.act(q, pq[:, :], AF.Silu)
        P.act(sg, pf[:, :], AF.Sigmoid)
        P.act(hg_t[pp][:, :], pg[:, :], AF.Silu)
        P.copy("scalar", hv_t[pp][:, :], pi[:, :])
        P.ts("vector", f, sg, lbt[:, 8 + j:9 + j], lbt[:, j:j + 1], ALU.mult, ALU.add)
        P.act(lf, f, AF.Ln)
        P.op("vector", lambda e: e.tensor_tensor_scan(bb.ap, cst.t[:, C_RM:C_RM + TT], lf.ap, 0.0, ALU.mult, ALU.add),
             reads=[cv(C_RM, TT), lf], writes=[bb])
        b3 = bb.rr("p (c t) -> p c t", c=NCK)
        bm3 = bm.rr("p (c t) -> p c t", c=NCK)
        P.tt("vector", bm3, b3, b3[:, :, 31:32].bc([128, NCK, 64]), ALU.subtract)
        P.act(eq, bm, AF.Exp)
        P.act(ek, bm, AF.Exp, scale=-1.0)
        P.stt(hq_e[pp][:, :], q, 128 ** -0.5, eq, ALU.mult, ALU.mult)
        he4 = hq_e[pp][:, :].rr("p (b t) -> p b t", b=NB)
        for par in range(2):
            P.copy("vector", hq_z[par][pp][:, :].rr("p (b t) -> p b t", b=NB)[:, :, par * 64:(par + 1) * 64], he4[:, :, par * 64:(par + 1) * 64])
        P.ts("vector", f, f, -1.0, 1.0, ALU.mult, ALU.add)
        P.tt("vector", hk_e[pp][:, :], f, ek, ALU.mult)
        hs = hsc[pp]
        P.act(hs[:, 0:8], b3[:, :, 63], AF.Exp)
        P.act(hs[:, 8:16], b3[:, :, 31], AF.Exp)
        P.act(hs[:, 16:24], bm3[:, :, 63], AF.Exp)
        tp = tps()
        for b in range(NB):
            P.transpose(tp[:, b * 128:(b + 1) * 128], hk_e[pp][:, b * 128:(b + 1) * 128], ident_b[:, :])
        P.copy("scalar", hk_z[0][pp][0:64, :], tp[0:64, :])
        P.copy("scalar", hk_z[1][pp][64:128, :], tp[64:128, :])

    def hgrn_rec(j0, pre=False):
        HS = (0, 1)
        js = [j0, j0 + 1]
        Sjs = [V(S_h.t[:, j, :], (("S_h", j),)) for j in js]
        for b in range(NB):
            bsl = slice(b * 128, (b + 1) * 128)
            if pre:
                pAs = [ps(), ps()]
                for par in range(2):
                    c = 2 * b + par
                    for h in HS:
                        P.mm(pAs[h][:, 128 * (par + 1):128 * (par + 2)], hk_z[par][h][:, bsl], hv_t[h][:, bsl])
                    for h in HS:
                        P.ts("vector", Sjs[h], Sjs[h], hsc[h][:, c:c + 1], None, ALU.mult)
                        P.stt(Sjs[h], pAs[h][:, 128 * (par + 1):128 * (par + 2)], hsc[h][:, 16 + c:17 + c], Sjs[h], ALU.mult, ALU.add)
                continue
            pAs, pBs = [ps(), ps()], [ps(), ps()]
            for h in HS:
                P.mm(pAs[h][:, 0:128], hk_e[h][:, bsl], hq_e[h][:, bsl])
            for h in HS:
                P.tt("vector", scT[h][0:64, 0:64], pAs[h][0:64, 0:64], cst[0:64, C_TU:C_TU + 64], ALU.mult)
                P.tt("vector", scT[h][64:128, 64:128], pAs[h][64:128, 64:128], cst[64:128, C_TU + 64:C_TU + 128], ALU.mult)
            for h in HS:
                P.mm(pBs[h][:, 0:128], scT[h][:, :], hv_t[h][:, bsl], start=True, stop=False)
            for par in range(2):
                c = 2 * b + par
                for h in HS:
                    P.act(Smid[h][:, :], Sjs[h], AF.Copy, scale=hsc[h][:, 8 + c:9 + c])
                for h in HS:
                    P.mm(pBs[h][:, 0:128], hq_z[par][h][:, bsl], Smid[h][:, :], start=False, stop=(par == 1))
                    P.mm(pAs[h][:, 128 * (par + 1):128 * (par + 2)], hk_z[par][h][:, bsl], hv_t[h][:, bsl])
                for h in HS:
                    P.ts("vector", Sjs[h], Sjs[h], hsc[h][:, c:c + 1], None, ALU.mult)
                    P.stt(Sjs[h], pAs[h][:, 128 * (par + 1):128 * (par + 2)], hsc[h][:, 16 + c:17 + c], Sjs[h], ALU.mult, ALU.add)
            sAB = []
            for h in HS:
                st["ch"] += 1
                cc = 16 + 4 * (st["ch"] % 8)
                sA, sB = sc.sub(cc, (slice(None), slice(cc, cc + 1))), sc.sub(cc, (slice(None), slice(cc + 1, cc + 2)))
                sAB.append((sA, sB))
                P.act(fs[h][:, 0:128], pBs[h][:, 0:128], AF.Square)
                rsum(sA, fs[h][:, 0:128])
            for h in HS:
                small_rstd(sAB[h][0], sAB[h][1], 128)
            for h in HS:
                P.stt(fs[h][:, 0:128], pBs[h][:, 0:128], sAB[h][1], cv(C_HN, 128), ALU.mult, ALU.mult)
                P.tt("vector", o_bf[h][:, :], fs[h][:, 0:128], hg_t[h][:, bsl], ALU.mult)
            for h in HS:
                tp2 = tps()
                P.transpose(tp2[:, 0:128], o_bf[h][:, :], ident_b[:, :])
                P.copy("scalar", oh_fm[:, js[h], bsl], tp2[:, 0:128])

    def gdn_scalars():
        Wab = piece("ab", (8, 32))
        pab = ps()
        for b in range(NB):
            for kc in range(8):
                P.mm(pab[:, b * 32:(b + 1) * 32], xn[:, kc, b * 128:(b + 1) * 128], Wab[:, kc, :], start=(kc == 0), stop=(kc == 7))
        p3 = pab[:, 0:128].rr("p (b c) -> p b c", b=NB)
        z, g_, be = sm[0][:, :], sm[1][:, :], sm[2][:, :]
        z3 = z.rr("p (b c) -> p b c", b=NB)
        P.tt("vector", z3, p3[:, :, 0:16], cv(C_DT, 16).unsq(1).bc([128, NB, 16]), ALU.add)
        P.act(z, z, AF.Exp)
        P.act(z, z, AF.Ln, bias=1.0)
        P.tt("vector", g_.rr("p (b c) -> p b c", b=NB), z3, nea[:, :].unsq(1).bc([128, NB, 16]), ALU.mult)
        P.act(be.rr("p (b c) -> p b c", b=NB), p3[:, :, 16:32], AF.Sigmoid)
        pg_ = ps()
        P.mm(pg_[:, 0:64], cv(C_TU, 128), g_)
        P.mm(pg_[:, 64:128], cv(C_ONE, 128), g_)
        gam, egam, begam, egk, egend = [sm[i][:, :] for i in range(3, 8)]
        P.copy("vector", gam, pg_[:, 0:64])
        P.act(egam, gam, AF.Exp)
        P.tt("vector", begam, be, egam, ALU.mult)
        P.tt("vector", egk, pg_[:, 64:128], gam, ALU.subtract)
        P.act(egk, egk, AF.Exp)
        P.copy("vector", egend, pg_[:, 64:128])
        P.act(egend, egend, AF.Exp)
        return g_, be, egam, begam, egk, egend

    def gdn_head(j, scal, pre=False):
        g_, be, egam, begam, egk, egend = scal
        W = piece("gd_%d" % j, (8, 4, 128))
        if GDN_LEVEL < 1:
            piece("gz_%d" % j, (8, 2, 128))
            return
        gidx = (j, 8 + j, 16 + 2 * j, 17 + 2 * j)
        ys = []
        for c in range(4):
            pc = ps()
            for kc in range(8):
                P.mm(pc[:, :], W[:, kc, c, :], xn[:, kc, :], start=(kc == 0), stop=(kc == 7))
            cb = fs[2 + c]
            gi = gidx[c]
            P.copy("vector", cb[:, 0:3], halo[:, gi, 0:3])
            P.copy("scalar", cb[:, 3:3 + TT], pc[:, :])
            if GDN_SUB >= 1:
                P.copy("vector", halo[:, gi, 0:3], cb[:, TT:TT + 3])
            acc = fs[6 + c][:, 0:TT]
            if GDN_SUB < 2:
                ys.append(acc)
                continue
            if c >= 2 and POOL_CONV:
                tmpc = fs[c - 2][:, 0:TT]
                P.ts("gpsimd", acc, cb[:, 3:3 + TT], cv(C_CW + gi * 4 + 3), 0.0, ALU.mult, ALU.add)
                for tpi in range(3):
                    P.ts("gpsimd", tmpc, cb[:, tpi:tpi + TT], cv(C_CW + gi * 4 + tpi), 0.0, ALU.mult, ALU.add)
                    P.tt("gpsimd", acc, acc, tmpc, ALU.add)
            else:
                P.ts("vector", acc, cb[:, 3:3 + TT], cv(C_CW + gi * 4 + 3), None, ALU.mult)
                for tpi in range(3):
                    P.stt(acc, cb[:, tpi:tpi + TT], cv(C_CW + gi * 4 + tpi), acc, ALU.mult, ALU.add)
            ys.append(acc)
        if GDN_SUB < 3:
            piece("gz_%d" % j, (8, 2, 128))
            return
        for c, dst, scl in ((0, gq_f, 128 ** -0.5), (1, gk_f, 1.0)):
            y = fs[2 + c][:, 0:TT]
            P.act(y, ys[c], AF.Silu)
            P.act(bs[c][:, :], y, AF.Square)
            if GDN_SUB < 4:
                continue
            pn = ps()
            P.mm(pn[:, :], ones_b[:, :], bs[c][:, :])
            rn = ys[c]
            P.act(rn, pn[:, :], AF.Ln, bias=EPS)
            P.act(rn, rn, AF.Exp, scale=-0.5)
            if GDN_SUB < 5:
                continue
            P.stt(dst[:, :], y, scl, rn, ALU.mult, ALU.mult)
        if GDN_SUB >= 6:
            for vh in range(2):
                P.act(gv_f[vh][:, :], ys[2 + vh], AF.Silu)
        Wz = piece("gz_%d" % j, (8, 2, 128))
        if GDN_LEVEL < 2:
            return
        pz = [ps(), ps()]
        for b in range(NB):
            for kc in range(8):
                P.mm(pz[b // 2][:, (b % 2) * 256:(b % 2) * 256 + 256], xn[:, kc, b * 128:(b + 1) * 128],
                     Wz[:, kc, :, :].rr("p a b -> p (a b)"), start=(kc == 0), stop=(kc == 7))
        for i in range(2):
            P.act(gz_t[:, 2 * i:2 * i + 2, :, :].rr("p a b c -> p (a b c)"), pz[i][:, :], AF.Silu)
        tk = tps()
        for b in range(NB):
            P.transpose(tk[:, b * 128:(b + 1) * 128], gk_f[:, b * 128:(b + 1) * 128], ident_b[:, :])
        r3 = lambda v: v.rr("p (b t) -> p b t", b=NB)
        sc3 = lambda v, hh: v.rr("p (b c) -> p b c", b=NB)[:, :, hh:hh + 1].bc([128, NB, 128])
        for vh in range(2):
            hh = 2 * j + vh
            P.tt("vector", r3(kbg_t[vh][:, :]), r3(tk), sc3(begam, hh), ALU.mult)
            P.tt("vector", r3(kte_t[vh][:, :]), r3(tk), sc3(egk, hh), ALU.mult)
            tv = tps()
            for b in range(NB):
                P.transpose(tv[:, b * 128:(b + 1) * 128], gv_f[vh][:, b * 128:(b + 1) * 128], ident_b[:, :])
            P.tt("vector", r3(bv_t[vh][:, :]), r3(tv), sc3(be, hh), ALU.mult)
        g3 = g_.rr("p (b c) -> p b c", b=NB)
        for b in range(NB):
            P.tt("vector", Gm[:, 2 * b:2 * b + 2, :], cv(C_SL, 128).unsq(1).bc([128, 2, 128]),
                 g3[:, b, 2 * j:2 * j + 2].unsq(2).bc([128, 2, 128]), ALU.mult)
        for i in range(2):
            pd = ps()
            P.mm(pd[:, :], cv(C_TU, 128), Gm[:, 4 * i:4 * i + 4, :].rr("p a b -> p (a b)"))
            Dc = fs[i][:, 0:TT]
            P.ts("vector", Dc, pd[:, :], -80.0, None, ALU.max)
            P.act(Ex[:, 4 * i:4 * i + 4, :].rr("p a b -> p (a b)"), Dc, AF.Exp)
        P.tt("vector", Ls[:, :, :], Ex[:, :, :], cv(C_SL, 128).unsq(1).bc([128, 8, 128]), ALU.mult)
        P.tt("vector", Lm[:, :, :], Ex[:, :, :], cv(C_TL, 128).unsq(1).bc([128, 8, 128]), ALU.mult)
        for b in range(NB):
            bsl = slice(b * 128, (b + 1) * 128)
            pk_ = ps()
            P.mm(pk_[:, 0:128], gk_f[:, bsl], gk_f[:, bsl])
            P.mm(pk_[:, 128:256], gq_f[:, bsl], gk_f[:, bsl])
            P.copy("scalar", Gm[:, 2 * b:2 * b + 2, :].rr("p a b -> p (a b)"), pk_[:, 0:256])
        F32R = mybir.dt.float32r
        CH_R = (lambda v: v.bitcast(F32R)) if CHAIN_F32R else (lambda v: v)
        Aall = Ex
        for vh in range(2):
            tb = tps()
            for b in range(NB):
                hh = 2 * j + vh
                col = b * 16 + hh
                q_ = 2 * b + vh
                P.stt(CH_R(Aall[:, q_, :]), Gm[:, 2 * b, :], be[:, col:col + 1], Ls[:, q_, :], ALU.mult, ALU.mult)
                P.tt("vector", qkl[b % 2][:, :], Gm[:, 2 * b + 1, :], Lm[:, q_, :], ALU.mult)
                P.transpose(tb[:, b * 128:(b + 1) * 128], qkl[b % 2][:, :], ident_b[:, :])
            P.copy("scalar", qklT_all[:, vh, :, :].rr("p a b -> p (a b)"), tb)
        identF = cv(C_ID, 128)
        Bv = [fs[4][:, 0:TT].rr("p (a b) -> p a b", a=NB), fs[5][:, 0:TT].rr("p (a b) -> p a b", a=NB)]
        Avs = [V(Ex.t[:, vh:8:2, :], (("Ex", 0),)) for vh in range(2)]
        X2 = [Gm[:, :, :], X2b[:, :, :]]
        for vh in range(2):
            pT = ps()
            for b in range(NB):
                P.transpose(pT[:, b * 128:(b + 1) * 128], Ex[:, 2 * b + vh, :], identF)
            P.copy("vector", CH_R(Bv[vh].rr("p a b -> p (a b)")), pT[:, :])
        for k in range(7):
            if k == 0:
                for vh in range(2):
                    P.tt("vector", CH_R(X2[vh][:, 0:NB, :]), Bv[vh], cv(C_MK, 128).unsq(1).bc([128, NB, 128]), ALU.mult)
                    P.tt("vector", CH_R(X2[vh][:, NB:2 * NB, :]), Avs[vh], cv(C_MK + 128, 128).unsq(1).bc([128, NB, 128]), ALU.mult)
                    P.tt("vector", CH_R(TT2[vh][:, :, :]), identF.unsq(1).bc([128, 2 * NB, 128]), X2[vh][:, :, :], ALU.subtract)
                continue
            pXs = []
            for vh in range(2):
                pX, pX2 = ps(), ps()
                for b in range(NB):
                    P.mm(pX[:, b * 128:(b + 1) * 128], CH_R(Avs[vh][:, b, :]), CH_R(TT2[vh][:, b, :]))
                for b in range(NB):
                    P.mm(pX2[:, b * 128:(b + 1) * 128], CH_R(Bv[vh][:, b, :]), CH_R(TT2[vh][:, NB + b, :]))
                pXs.append((pX, pX2))
            for vh in range(2):
                pX, pX2 = pXs[vh]
                P.tt("vector", CH_R(X2[vh][:, 0:NB, :]), pX[:, :].rr("p (a b) -> p a b", a=NB),
                     cv(C_MK + 256 * k, 128).unsq(1).bc([128, NB, 128]), ALU.mult)
                P.tt("vector", CH_R(X2[vh][:, NB:2 * NB, :]), pX2[:, :].rr("p (a b) -> p a b", a=NB),
                     cv(C_MK + 256 * k + 128, 128).unsq(1).bc([128, NB, 128]), ALU.mult)
            pYs = []
            for vh in range(2):
                pY, pY2 = ps(), ps()
                for b in range(NB):
                    P.mm(pY[:, b * 128:(b + 1) * 128], CH_R(TT2[vh][:, NB + b, :]), CH_R(X2[vh][:, b, :]))
                for b in range(NB):
                    P.mm(pY2[:, b * 128:(b + 1) * 128], CH_R(TT2[vh][:, b, :]), CH_R(X2[vh][:, NB + b, :]))
                pYs.append((pY, pY2))
            for vh in range(2):
                pY, pY2 = pYs[vh]
                P.tt("vector", CH_R(TT2[vh][:, 0:NB, :].rr("p a b -> p (a b)")), TT2[vh][:, 0:NB, :].rr("p a b -> p (a b)"), pY[:, :], ALU.subtract)
                P.tt("vector", CH_R(TT2[vh][:, NB:2 * NB, :].rr("p a b -> p (a b)")), TT2[vh][:, NB:2 * NB, :].rr("p a b -> p (a b)"), pY2[:, :], ALU.subtract)
        for vh in range(2):
            P.copy("vector", Ttb_g[:, vh, :, :], TT2[vh][:, 0:NB, :])
        VH = (0, 1)
        hhs = [2 * j + vh for vh in VH]
        Sfs = [V(S_g.t[:, hh, :], (("S_g", hh),)) for hh in hhs]
        Sbs = [V(Sb_g.t[:, hh, :], (("Sb_g", hh),)) for hh in hhs]
        for b in range(NB):
            bsl = slice(b * 128, (b + 1) * 128)
            cols = [b * 16 + hh for hh in hhs]
            pus = [ps(), ps()]
            for vh in VH:
                P.mm(pus[vh][:, 0:128], Ttb_g[:, vh, b, :], bv_t[vh][:, bsl])
                P.mm(pus[vh][:, 128:256], kbg_t[vh][:, bsl], Ttb_g[:, vh, b, :])
            for vh in VH:
                P.copy("scalar", u_sb[vh][:, :], pus[vh][:, 0:128])
                P.copy("scalar", wT_sb[vh][:, :], pus[vh][:, 128:256])
            prs = [ps(), ps()]
            for vh in VH:
                P.mm(prs[vh][:, 0:128], wT_sb[vh][:, :], Sbs[vh])
            for vh in VH:
                P.tt("vector", vnew[vh][:, :], u_sb[vh][:, :], prs[vh][:, 0:128], ALU.subtract)
            for vh in VH:
                if not pre:
                    P.mm(prs[vh][:, 128:256], gq_f[:, bsl], Sbs[vh])
                    P.mm(prs[vh][:, 256:384], qklT_all[:, vh, b, :], vnew[vh][:, :])
                P.mm(prs[vh][:, 384:512], kte_t[vh][:, bsl], vnew[vh][:, :])
            for vh in VH:
                P.stt(Sfs[vh], Sfs[vh], egend[:, cols[vh]:cols[vh] + 1], prs[vh][:, 384:512], ALU.mult, ALU.add)
                P.copy("scalar", Sbs[vh], Sfs[vh])
            if pre:
                continue
            for vh in VH:
                P.act(o_sb[vh][:, :], prs[vh][:, 128:256], AF.Copy, scale=egam[:, cols[vh]:cols[vh] + 1])
                P.tt("vector", o_sb[vh][:, :], o_sb[vh][:, :], prs[vh][:, 256:384], ALU.add)
            sAB = []
            for vh in VH:
                st["ch"] += 1
                cc = 16 + 4 * (st["ch"] % 8)
                sA, sB = sc.sub(cc, (slice(None), slice(cc, cc + 1))), sc.sub(cc, (slice(None), slice(cc + 1, cc + 2)))
                sAB.append((sA, sB))
                P.act(fs[vh][:, 0:128], o_sb[vh][:, :], AF.Square)
                rsum(sA, fs[vh][:, 0:128])
            for vh in VH:
                small_rstd(sAB[vh][0], sAB[vh][1], 128)
            for vh in VH:
                P.stt(fs[vh][:, 0:128], o_sb[vh][:, :], sAB[vh][1], cv(C_GN, 128), ALU.mult, ALU.mult)
                P.tt("vector", vnew[vh][:, :], fs[vh][:, 0:128], gz_t[:, b, vh, :], ALU.mult)
            for vh in VH:
                tp2 = tps()
                P.transpose(tp2[:, 0:128], vnew[vh][:, :], ident_b[:, :])
                P.copy("scalar", og_fm[:, hhs[vh], bsl], tp2[:, 0:128])

    def merge_and_out():
        for i in range(2):
            Wgh = piece("gh_%d" % i, (8, 4, 128))
            Wbh = piece("bh_%d" % i, (4, 8, 128))
            for m in range(4):
                pgt, py = ps(), ps()
                for kc in range(8):
                    P.mm(pgt[:, :], Wgh[:, kc, m, :], xn[:, kc, :], start=(kc == 0), stop=(kc == 7))
                for hd in range(8):
                    P.mm(py[:, :], Wbh[:, m, hd, :], oh_fm[:, hd, :], start=(hd == 0), stop=(hd == 7))
                sg = fs[2 + (m % 2)][:, 0:TT]
                P.act(sg, pgt[:, :], AF.Sigmoid)
                P.tt("vector", ymf[:, m, :], sg, py[:, :], ALU.mult)
            Wgg = piece("gg_%d" % i, (8, 4, 128))
            for m in range(4):
                if m % 2 == 0:
                    Wbg = piece("bg_%d" % (2 * i + m // 2), (2, 16, 128))
                pgt, py = ps(), ps()
                for kc in range(8):
                    P.mm(pgt[:, :], Wgg[:, kc, m, :], xn[:, kc, :], start=(kc == 0), stop=(kc == 7))
                for hd in range(16):
                    P.mm(py[:, :], Wbg[:, m % 2, hd, :], og_fm[:, hd, :], start=(hd == 0), stop=(hd == 15))
                sg = fs[2 + (m % 2)][:, 0:TT]
                P.act(sg, pgt[:, :], AF.Sigmoid)
                P.tt("vector", sg, sg, py[:, :], ALU.mult)
                P.tt("vector", ym[:, 4 * i + m, :], sg, ymf[:, m, :], ALU.add)
        for i in range(2):
            Wo = piece("wo_%d" % i, (4, 8, 128))
            for m in range(4):
                po = ps()
                for kc in range(8):
                    P.mm(po[:, :], Wo[:, m, kc, :], ym[:, kc, :], start=(kc == 0), stop=(kc == 7))
                P.tt("vector", h[:, 4 * i + m, :], h[:, 4 * i + m, :], po[:, :], ALU.add)

    for t in range(n_tiles):
        pre = t < n_pre
        tsl = slice(t * TT, (t + 1) * TT)
        osl = slice((t - n_pre) * TT, (t - n_pre + 1) * TT)
        P.dma("sync", h[:, :, :], x_d[:, tsl].rr("(kc p) t -> p kc t", p=128), "xin")
        ffn(1, C_G1)
        rstd = rmsnorm(C_GM)
        apply_norm(rstd, C_GM, lambda kc: xn[:, kc, :])
        for j in range(8):
            hgrn_head(j)
            if j % 2 == 1:
                hgrn_rec(j - 1, pre)
        scal = gdn_scalars()
        for j in range(8):
            gdn_head(j, scal, pre)
        if pre:
            for i in range(2):
                for nm in ("gh_%d" % i, "bh_%d" % i, "gg_%d" % i, "bg_%d" % (2 * i), "bg_%d" % (2 * i + 1)):
                    piece(nm, (8, 4, 128))
            piece("wo_0", (4, 8, 128))
            piece("wo_1", (4, 8, 128))
            for i in range(11):
                piece("f2_in_%d" % i, (8, 4, 128))
            for m in range(8):
                piece("f2_out_%d" % m, (NFF, 128))
            continue
        merge_and_out()
        ffn(2, C_G2)
        rstd = rmsnorm(C_GF)
        apply_norm(rstd, C_GF, lambda kc: h[:, kc, :])
        P.dma("sync", out_d[:, osl].rr("(kc p) t -> p kc t", p=128), h[:, :, :], "xout")
    P.emit()
    P.close()
    return nc


_CACHE = {}


def kernel(**inputs):
    inp = {k: np.asarray(v) for k, v in inputs.items()}
    x = inp["x"]
    B, S, _ = x.shape
    half = S // 2
    if "nc" not in _CACHE:
        _CACHE["nc"] = build_program(S, n_pre=half // TT)
    nc = _CACHE["nc"]
    ws = pack_weights(inp)
    cst = pack_consts(inp)
    in_maps = []
    for b in range(B):
        for hf in range(2):
            prev = np.zeros((half, x.shape[2]), np.float32) if hf == 0 else x[b, :half]
            own = x[b, hf * half:(hf + 1) * half]
            in_maps.append({"xT": np.ascontiguousarray(np.concatenate([prev, own], axis=0).T), "ws": ws, "cst": cst})
    res = run_bass_kernel_spmd(nc, in_maps, core_ids=list(range(2 * B)))
    out = np.empty((B, S, x.shape[2]), np.float32)
    for b in range(B):
        for hf in range(2):
            out[b, hf * half:(hf + 1) * half] = np.asarray(res.results[2 * b + hf]["outT"]).T
    return out
```

```python
import os
import numpy as np
import concourse.bass as bass
import concourse.mybir as mybir
from concourse.bass_utils import run_bass_kernel_spmd

F32 = mybir.dt.float32
BF16 = mybir.dt.bfloat16
AF = mybir.ActivationFunctionType
ALU = mybir.AluOpType


STRICT_SAME_ENGINE = True


class V:
    __slots__ = ("ap", "keys")

    def __init__(self, ap, keys):
        self.ap = ap
        self.keys = keys

    def __getitem__(self, idx):
        return V(self.ap[idx], self.keys)

    def bc(self, shape):
        return V(self.ap.to_broadcast(list(shape)), self.keys)

    def unsq(self, axis):
        return V(self.ap.unsqueeze(axis), self.keys)

    def bitcast(self, dt):
        return V(self.ap.bitcast(dt), self.keys)

    def rr(self, s, **kw):
        return V(self.ap.rearrange(s, **kw), self.keys)


class Buf:
    def __init__(self, t, name, nsub=1):
        self.t = t
        self.name = name

    def __getitem__(self, idx):
        return V(self.t[idx], ((self.name, 0),))

    def sub(self, k, idx):
        return V(self.t[idx], ((self.name, k),))


class Op:
    __slots__ = ("eng", "fn", "deps", "needs_inc", "sem", "val", "is_dma", "idx")

    def __init__(self, eng, fn, is_dma):
        self.eng = eng
        self.fn = fn
        self.deps = []
        self.needs_inc = False
        self.sem = None
        self.val = None
        self.is_dma = is_dma


class Prog:
    ENGS = ("sync", "scalar", "vector", "gpsimd", "tensor")

    def __init__(self, nc):
        self.nc = nc
        self.ops = {e: [] for e in self.ENGS}
        self.state = {}
        self.stack = []
        self.dma_sems = {}
        self.dma_counts = {}
        self.eng_sems = {}
        self.all_dma_ops = []

    def sbuf(self, name, shape, dt):
        g = self.nc.sbuf_tensor(name, list(shape), dt)
        t = g.__enter__()
        self.stack.append(g)
        return Buf(t, name)

    def psum(self, name, shape, dt):
        g = self.nc.psum_tensor(name, list(shape), dt)
        t = g.__enter__()
        self.stack.append(g)
        return Buf(t, name)

    def sem(self, name):
        g = self.nc.semaphore(name)
        s = g.__enter__()
        self.stack.append(g)
        return s

    def op(self, eng, fn, reads=(), writes=(), dma_key=None):
        o = Op(eng, fn, dma_key is not None)
        rkey = ("dma", dma_key) if dma_key is not None else eng
        deps = {}
        for v in reads:
            for k in v.keys:
                st = self.state.get(k)
                if st and st[0] is not None:
                    deps[id(st[0])] = (st[0], "raw")
        for v in writes:
            for k in v.keys:
                st = self.state.get(k)
                if st:
                    if st[0] is not None and id(st[0]) not in deps:
                        deps[id(st[0])] = (st[0], "waw")
                    for r in st[1].values():
                        if id(r) not in deps:
                            deps[id(r)] = (r, "war")
        for d, kind in deps.values():
            if d is o:
                continue
            if not d.is_dma and not o.is_dma and d.eng == eng:
                if eng == "tensor":
                    continue
                if kind == "waw" and not STRICT_SAME_ENGINE:
                    continue
                if kind == "war" and not STRICT_SAME_ENGINE:
                    continue
            o.deps.append(d)
            d.needs_inc = True
        for v in reads:
            for k in v.keys:
                st = self.state.setdefault(k, [None, {}])
                st[1][rkey] = o
        for v in writes:
            for k in v.keys:
                self.state[k] = [o, {}]
        if dma_key is not None:
            if dma_key not in self.dma_sems:
                self.dma_sems[dma_key] = self.sem("d_" + str(dma_key))
                self.dma_counts[dma_key] = 0
            self.dma_counts[dma_key] += 16
            o.sem = self.dma_sems[dma_key]
            o.val = self.dma_counts[dma_key]
            o.needs_inc = True
            self.all_dma_ops.append(o)
        self.ops[eng].append(o)
        return o

    def dma(self, eng, out, in_, key):
        return self.op(eng, lambda e: e.dma_start(out=out.ap, in_=in_.ap),
                       reads=[in_], writes=[out], dma_key=key)

    def mm(self, out, lhsT, rhs, start=True, stop=True, extra_reads=()):
        return self.op("tensor", lambda e: e.matmul(out.ap, lhsT.ap, rhs.ap, start=start, stop=stop),
                       reads=[lhsT, rhs] + list(extra_reads), writes=[out])

    def transpose(self, out, in_, ident):
        return self.op("tensor", lambda e: e.transpose(out.ap, in_.ap, ident.ap),
                       reads=[in_, ident], writes=[out])

    def act(self, out, in_, func, bias=None, scale=None, accum_out=None, eng="scalar"):
        reads = [in_]
        kw = {}
        if bias is not None:
            if isinstance(bias, V):
                reads.append(bias)
                kw["bias"] = bias.ap
            else:
                kw["bias"] = bias
        if scale is not None:
            if isinstance(scale, V):
                reads.append(scale)
                kw["scale"] = scale.ap
            else:
                kw["scale"] = scale
        writes = [out]
        if accum_out is not None:
            writes.append(accum_out)
            kw["accum_out"] = accum_out.ap
        return self.op("scalar", lambda e: e.activation(out.ap, in_.ap, func, **kw),
                       reads=reads, writes=writes)

    def tt(self, eng, out, in0, in1, op):
        return self.op(eng, lambda e: e.tensor_tensor(out.ap, in0.ap, in1.ap, op),
                       reads=[in0, in1], writes=[out])

    def ts(self, eng, out, in0, s1, s2, op0, op1=None, accum_out=None):
        reads = [in0]
        a1 = s1
        a2 = s2
        if isinstance(s1, V):
            reads.append(s1)
            a1 = s1.ap
        if isinstance(s2, V):
            reads.append(s2)
            a2 = s2.ap
        writes = [out]
        kw = {}
        if accum_out is not None:
            writes.append(accum_out)
            kw["accum_out"] = accum_out.ap
        if op1 is None:
            return self.op(eng, lambda e: e.tensor_scalar(out.ap, in0.ap, a1, a2, op0, **kw),
                           reads=reads, writes=writes)
        return self.op(eng, lambda e: e.tensor_scalar(out.ap, in0.ap, a1, a2, op0, op1, **kw),
                       reads=reads, writes=writes)

    def stt(self, out, in0, scalar, in1, op0, op1, eng="vector"):
        reads = [in0, in1]
        a = scalar
        if isinstance(scalar, V):
            reads.append(scalar)
            a = scalar.ap
        return self.op(eng, lambda e: e.scalar_tensor_tensor(out.ap, in0.ap, a, in1.ap, op0, op1),
                       reads=reads, writes=[out])

    def copy(self, eng, out, in_):
        if eng == "scalar":
            return self.op(eng, lambda e: e.copy(out.ap, in_.ap), reads=[in_], writes=[out])
        return self.op(eng, lambda e: e.tensor_copy(out.ap, in_.ap), reads=[in_], writes=[out])

    def memset(self, eng, out, val):
        return self.op(eng, lambda e: e.memset(out.ap, val), reads=[], writes=[out])

    def emit(self, final_wait_ops=()):
        nc = self.nc
        for eng in self.ENGS:
            cnt = 0
            for o in self.ops[eng]:
                if o.is_dma:
                    continue
                if o.needs_inc:
                    if eng not in self.eng_sems:
                        self.eng_sems[eng] = self.sem("e_" + eng)
                    cnt += 1
                    o.sem = self.eng_sems[eng]
                    o.val = cnt
        final_dmas = list(self.all_dma_ops)
        with nc.Block() as block:
            def run(eng_name):
                def body(e):
                    waited = {}
                    for o in self.ops[eng_name]:
                        need = {}
                        for d in o.deps:
                            k = id(d.sem)
                            if k not in need or need[k][1] < d.val:
                                need[k] = (d.sem, d.val)
                        for k, (s, v) in need.items():
                            if waited.get(k, 0) >= v:
                                continue
                            e.wait_ge(s, v)
                            waited[k] = v
                        ins = o.fn(e)
                        if o.needs_inc:
                            ins.then_inc(o.sem, 16 if o.is_dma else 1)
                    if eng_name == "sync":
                        last = {}
                        for o in final_dmas:
                            last[id(o.sem)] = (o.sem, max(o.val, last.get(id(o.sem), (None, 0))[1]))
                        for k, (s, v) in last.items():
                            if waited.get(k, 0) < v:
                                e.wait_ge(s, v)
                return body
            block.sync(run("sync"))
            block.scalar(run("scalar"))
            block.vector(run("vector"))
            block.gpsimd(run("gpsimd"))
            block.tensor(run("tensor"))

    def close(self):
        while self.stack:
            g = self.stack.pop()
            g.__exit__(None, None, None)


import os

D = 1024
DFF = 2816
NFF = DFF // 128
TT = 512
NB = TT // 128
EPS = 1e-6
PW = 4096
NSLOT = 4
TDT = F32
POOL_CONV = bool(int(os.environ.get('POOL_CONV', '0')))
CHAIN_F32R = bool(int(os.environ.get('CHAIN_F32R', '0')))

O_HQ, O_HF, O_HI, O_HG = 0, 1024, 2048, 3072
O_GQ, O_GK, O_GV, O_GA, O_GB, O_GZ, O_GH, O_GG = 4096, 5120, 6144, 8192, 8208, 8224, 10272, 11296

C_ID, C_ONE, C_TL, C_SL, C_TU, C_RM = 0, 128, 256, 384, 512, 640
C_G1, C_GM, C_G2, C_GF = 1152, 1160, 1168, 1176
C_HN, C_GN = 1184, 1312
C_L0, C_L1 = 1440, 1448
C_CW = 1456
C_AL, C_DT = 1584, 1600
C_MK = 1616
NCST = 1616 + 7 * 256


def piece_names():
    names = []
    for k in (1, 2):
        if k == 2:
            names += ["hg_%d" % j for j in range(8)] + ["ab"]
            for j in range(8):
                names += ["gd_%d" % j, "gz_%d" % j]
            for i in range(2):
                names += ["gh_%d" % i, "bh_%d" % i, "gg_%d" % i, "bg_%d" % (2 * i), "bg_%d" % (2 * i + 1)]
            names += ["wo_0", "wo_1"]
        names += ["f%d_in_%d" % (k, i) for i in range(11)]
        names += ["f%d_out_%d" % (k, m) for m in range(8)]
    return names


def pack_weights(inp):
    names = piece_names()
    ws = np.zeros((len(names), 128, PW), np.float32)

    def put(name, arr):
        a = np.ascontiguousarray(arr).reshape(128, -1)
        ws[names.index(name), :, :a.shape[1]] = a

    for k, (wi, wo) in enumerate(((inp["ffn1_w_in"], inp["ffn1_w_out"]), (inp["ffn2_w_in"], inp["ffn2_w_out"])), 1):
        wr = np.asarray(wi)[0].reshape(8, 128, 2 * DFF)
        for i in range(11):
            cols = []
            for j in (2 * i, 2 * i + 1):
                cols.append(wr[:, :, j * 128:(j + 1) * 128])
                cols.append(wr[:, :, DFF + j * 128:DFF + (j + 1) * 128])
            put("f%d_in_%d" % (k, i), np.stack(cols, axis=2).transpose(1, 0, 2, 3))
        w2 = np.asarray(wo)[0].reshape(NFF, 128, D)
        for m in range(8):
            put("f%d_out_%d" % (k, m), w2[:, :, m * 128:(m + 1) * 128].transpose(1, 0, 2))
    wr = np.asarray(inp["w_in"])[0].reshape(8, 128, -1)

    def cols(off, j, n=128):
        return wr[:, :, off + j * 128: off + j * 128 + n]

    for j in range(8):
        put("hg_%d" % j, np.stack([cols(O_HQ, j), cols(O_HF, j), cols(O_HG, j), cols(O_HI, j)], axis=2).transpose(1, 0, 2, 3))
        put("gd_%d" % j, np.stack([cols(O_GQ, j), cols(O_GK, j), cols(O_GV, 2 * j), cols(O_GV, 2 * j + 1)], axis=2).transpose(1, 0, 2, 3))
        put("gz_%d" % j, np.stack([cols(O_GZ, 2 * j), cols(O_GZ, 2 * j + 1)], axis=2).transpose(1, 0, 2, 3))
    put("ab", wr[:, :, O_GA:O_GA + 32].transpose(1, 0, 2))
    for i in range(2):
        put("gh_%d" % i, np.stack([cols(O_GH, 4 * i + m) for m in range(4)], axis=2).transpose(1, 0, 2, 3))
        put("gg_%d" % i, np.stack([cols(O_GG, 4 * i + m) for m in range(4)], axis=2).transpose(1, 0, 2, 3))
    bh = np.asarray(inp["w_branch_hgrn"])[0].reshape(8, 128, D)
    for i in range(2):
        put("bh_%d" % i, np.stack([bh[:, :, (4 * i + m) * 128:(4 * i + m + 1) * 128] for m in range(4)], axis=0).transpose(2, 0, 1, 3))
    bg = np.asarray(inp["w_branch_gdn"])[0].reshape(16, 128, D)
    for i in range(4):
        put("bg_%d" % i, np.stack([bg[:, :, (2 * i + m) * 128:(2 * i + m + 1) * 128] for m in range(2)], axis=0).transpose(2, 0, 1, 3))
    wo = np.asarray(inp["w_out"])[0].reshape(8, 128, D)
    for i in range(2):
        put("wo_%d" % i, np.stack([wo[:, :, (4 * i + m) * 128:(4 * i + m + 1) * 128] for m in range(4)], axis=0).transpose(2, 0, 1, 3))
    return ws


def pack_consts(inp):
    c = np.zeros((128, NCST), np.float32)
    r = np.arange(128)
    c[:, C_ID:C_ID + 128] = np.eye(128)
    c[:, C_ONE:C_ONE + 128] = 1.0
    c[:, C_TL:C_TL + 128] = (r[None, :] <= r[:, None])
    c[:, C_SL:C_SL + 128] = (r[None, :] < r[:, None])
    c[:, C_TU:C_TU + 128] = (r[:, None] <= r[None, :])
    rm = np.ones(512, np.float32)
    rm[::64] = 0.0
    c[:, C_RM:C_RM + 512] = rm[None, :]
    for col, key in ((C_G1, "ffn1_norm"), (C_GM, "mix_norm"), (C_G2, "ffn2_norm")):
        c[:, col:col + 8] = np.asarray(inp[key])[0].reshape(8, 128).T
    c[:, C_GF:C_GF + 8] = np.asarray(inp["final_norm"]).reshape(8, 128).T
    c[:, C_HN:C_HN + 128] = np.asarray(inp["hgrn_out_norm"])[0][None, :]
    c[:, C_GN:C_GN + 128] = np.asarray(inp["gdn_out_norm"])[0][None, :]
    lbl = np.asarray(inp["hgrn_lb_logits"])
    c[:, C_L0:C_L0 + 8] = lbl[0].reshape(8, 128).T
    c[:, C_L1:C_L1 + 8] = lbl[1].reshape(8, 128).T
    cw = np.asarray(inp["gdn_conv_w"])[0]
    c[:, C_CW:C_CW + 128] = cw.T.reshape(32, 128, 4).transpose(1, 0, 2).reshape(128, 128)
    for k in range(7):
        n = 1 << k
        tt_, ss_ = r[:, None], r[None, :]
        mk = ((tt_ // (2 * n) == ss_ // (2 * n)) & (tt_ % (2 * n) >= n) & (ss_ % (2 * n) < n)).astype(np.float32)
        c[:, C_MK + 256 * k:C_MK + 256 * k + 128] = mk.T
        c[:, C_MK + 256 * k + 128:C_MK + 256 * (k + 1)] = mk
    c[:, C_AL:C_AL + 16] = np.asarray(inp["gdn_a_log"])[0][None, :]
    c[:, C_DT:C_DT + 16] = np.asarray(inp["gdn_dt_bias"])[0][None, :]
    return c


import os
GDN_LEVEL = int(os.environ.get('GDN_LEVEL', '9'))
GDN_SUB = int(os.environ.get('GDN_SUB', '9'))


def build_program(n_tok, taps=None, stages=("ffn1", "hgrn", "gdn", "merge", "ffn2"), n_pre=0):
    n_tiles = n_tok // TT
    names = piece_names()
    NP = len(names)
    nc = bass.Bass("TRN2", target_bir_lowering=False)
    P = Prog(nc)
    x_d = Buf(nc.dram_tensor("xT", [D, n_tok], F32, kind="ExternalInput").ap(), "xT")
    WSB = bool(int(os.environ.get("WS_BF16", "0")))
    ws_d = Buf(nc.dram_tensor("ws", [NP, 128, PW], BF16 if WSB else F32, kind="ExternalInput").ap(), "ws")
    cst_d = Buf(nc.dram_tensor("cst", [128, NCST], F32, kind="ExternalInput").ap(), "cstd")
    out_d = Buf(nc.dram_tensor("outT", [D, n_tok - n_pre * TT], F32, kind="ExternalOutput").ap(), "outT")
    tap_d = {}
    if taps:
        for nm, shp in taps.items():
            tdt = BF16 if nm in ("oh", "og") else F32
            tap_d[nm] = Buf(nc.dram_tensor("tap_" + nm, list(shp), tdt, kind="ExternalOutput").ap(), "tap_" + nm)

    tapped = set()

    def tap(nm, v):
        if taps and nm in tap_d and nm not in tapped:
            tapped.add(nm)
            P.dma("sync", V(tap_d[nm].t, ((tap_d[nm].name, 0),)), v, "tap_" + nm)

    cst = P.sbuf("cst_sb", [128, NCST], F32)
    ident_b = P.sbuf("ident_b", [128, 128], BF16)
    ones_b = P.sbuf("ones_b", [128, 128], BF16)
    tu_b = P.sbuf("tu_b", [128, 128], BF16)
    lbt = P.sbuf("lbt", [128, 16], F32)
    nea = P.sbuf("nea", [128, 16], F32)
    slots = [P.sbuf("slot%d" % i, [128, PW], BF16) for i in range(NSLOT)]
    h = P.sbuf("h", [128, 8, TT], F32)
    xn = P.sbuf("xn", [128, 8, TT], BF16)
    hid = P.sbuf("hid", [128, NFF, TT], BF16)
    fs = [P.sbuf("fs%d" % i, [128, TT + 4], F32) for i in range(10)]
    oh_fm = P.sbuf("oh_fm", [128, 8, TT], BF16)
    og_fm = hid[:, 0:16, :]
    ymf = hid[:, 16:20, :]
    ym = P.sbuf("ym", [128, 8, TT], BF16)
    S_h = P.sbuf("S_h", [128, 8, 128], F32)
    S_g = P.sbuf("S_g", [128, 16, 128], F32)
    Sb_g = P.sbuf("Sb_g", [128, 16, 128], BF16)
    halo = P.sbuf("halo", [128, 32, 4], F32)
    sm = [P.sbuf("sm%d" % i, [128, 64], F32) for i in range(8)]
    sc = P.sbuf("sc", [128, 64], F32)
    hq_e = [P.sbuf("hq_e%d" % i, [128, TT], BF16) for i in range(2)]
    bs = hq_e
    hk_e = [P.sbuf("hk_e%d" % i, [128, TT], BF16) for i in range(2)]
    hv_t = [P.sbuf("hv_t%d" % i, [128, TT], BF16) for i in range(2)]
    hk_z = [[P.sbuf("hk_z%d_%d" % (par, i), [128, TT], BF16) for i in range(2)] for par in range(2)]
    hq_z = [[P.sbuf("hq_z%d_%d" % (par, i), [128, TT], BF16) for i in range(2)] for par in range(2)]
    hg_t = [P.sbuf("hg_t%d" % i, [128, TT], BF16) for i in range(2)]
    hsc = [P.sbuf("hsc%d" % i, [128, 32], F32) for i in range(2)]
    gq_f = P.sbuf("gq_f", [128, TT], BF16)
    gk_f = P.sbuf("gk_f", [128, TT], BF16)
    gv_f = [P.sbuf("gv_f%d" % i, [128, TT], BF16) for i in range(2)]
    gz_t = P.sbuf("gz_t", [128, NB, 2, 128], BF16)
    kbg_t = [P.sbuf("kbg_t%d" % i, [128, TT], BF16) for i in range(2)]
    kte_t = [P.sbuf("kte_t%d" % i, [128, TT], BF16) for i in range(2)]
    bv_t = [P.sbuf("bv_t%d" % i, [128, TT], BF16) for i in range(2)]
    Gm = P.sbuf("Gm", [128, 8, 128], F32)
    Ex = P.sbuf("Ex", [128, 8, 128], F32)
    Ls = P.sbuf("Ls", [128, 8, 128], BF16)
    Lm = P.sbuf("Lm", [128, 8, 128], BF16)
    NCH = 2
    qkl = [P.sbuf("qkl%d" % c, [128, 128], BF16) for c in range(2)]
    qklT_all = P.sbuf("qklT_all", [128, 2, NB, 128], BF16)
    TT2 = [P.sbuf("TT2_%d" % i, [128, 2 * NB, 128], F32) for i in range(2)]
    X2b = P.sbuf("X2b", [128, 2 * NB, 128], F32)
    Ttb_g = P.sbuf("Ttb_g", [128, 2, NB, 128], BF16)
    u_sb = [P.sbuf("u_sb%d" % c, [128, 128], F32) for c in range(NCH)]
    wT_sb = [P.sbuf("wT_sb%d" % c, [128, 128], BF16) for c in range(NCH)]
    vnew = [P.sbuf("vnew%d" % c, [128, 128], BF16) for c in range(NCH)]
    o_sb = [P.sbuf("o_sb%d" % c, [128, 128], F32) for c in range(NCH)]
    o_bf = vnew
    Smid = wT_sb
    scT = [P.sbuf("scT%d" % c, [128, 128], BF16) for c in range(2)]

    NBK = 6
    banks = [P.psum("bank%d" % i, [128, 512], F32) for i in range(NBK)]
    tbanks = [P.psum("tbank%d" % i, [128, 1024], BF16) for i in range(2)]
    st = {"ps": 0, "tp": 0, "piece": 0, "ch": 0}

    def ps():
        b = banks[st["ps"] % NBK]
        st["ps"] += 1
        return b

    def tps():
        k = st["tp"] % 2
        st["tp"] += 1
        return tbanks[k][:, 0:512]

    USE_WSB = bool(int(os.environ.get("USE_WSB", "1"))) and n_tiles > 1 and not WSB
    if USE_WSB:
        wsb_t = nc.dram_tensor("wsb", [NP, 128, PW], BF16, kind="Internal").ap()

    def piece(name, shape):
        i = st["piece"]
        assert names[i % NP] == name, (names[i % NP], name)
        st["piece"] += 1
        k = i % NSLOT
        sl = slots[k]
        n = 1
        for s_ in shape:
            n *= s_
        pi = i % NP
        if USE_WSB and i >= NP:
            P.dma("sync", sl[:, 0:n], V(wsb_t[pi, :, 0:n], (("wsb", pi),)), "slot%d" % k)
        else:
            P.dma("sync" if WSB else "gpsimd", sl[:, 0:n], ws_d[pi, :, 0:n], "slot%d" % k)
            if USE_WSB:
                P.dma("sync", V(wsb_t[pi, :, 0:n], (("wsb", pi),)), sl[:, 0:n], "wb%d" % k)
        v = sl[:, 0:n]
        if len(shape) == 2:
            return v.rr("p (a b) -> p a b", a=shape[0])
        if len(shape) == 3:
            return v.rr("p (a b c) -> p a b c", a=shape[0], b=shape[1])
        return v

    cv = lambda c0, n=1: cst[:, c0:c0 + n]

    P.dma("sync", cst[:, :], cst_d[:, :], "cst")
    P.copy("vector", ident_b[:, :], cv(C_ID, 128))
    P.copy("vector", ones_b[:, :], cv(C_ONE, 128))
    P.copy("vector", tu_b[:, :], cv(C_TU, 128))
    P.tt("vector", sc[:, 0:8], cv(C_L0, 8), cv(C_L1, 8), ALU.subtract)
    P.act(lbt[:, 0:8], sc[:, 0:8], AF.Sigmoid)
    P.act(lbt[:, 8:16], sc[:, 0:8], AF.Sigmoid, scale=-1.0)
    P.act(sc[:, 16:32], cv(C_AL, 16), AF.Exp)
    P.ts("vector", nea[:, :], sc[:, 16:32], -1.0, None, ALU.mult)
    P.memset("vector", S_h[:, :, :], 0.0)
    P.memset("vector", S_g[:, :, :], 0.0)
    P.memset("vector", Sb_g[:, :, :], 0.0)
    P.memset("vector", halo[:, :, :], 0.0)
    for par in range(2):
        for i in range(2):
            P.memset("vector", hk_z[par][i][:, :], 0.0)
            P.memset("vector", hq_z[par][i][:, :], 0.0)
    for i in range(2):
        P.memset("vector", scT[i][:, :], 0.0)

    def rmsnorm(gcol):
        sq = hid[:, 0:8, :]
        P.act(sq, h[:, :, :], AF.Square)
        pb = ps()
        for kc in range(8):
            P.mm(pb[:, :], ones_b[:, :], hid[:, kc, :], start=(kc == 0), stop=(kc == 7))
        P.act(fs[0][:, 0:TT], pb[:, :], AF.Ln, scale=1.0 / D, bias=EPS)
        P.act(fs[1][:, 0:TT], fs[0][:, 0:TT], AF.Exp, scale=-0.5)
        return fs[1][:, 0:TT]

    def apply_norm(rstd, gcol, dst_fn):
        for kc in range(8):
            P.stt(dst_fn(kc), h[:, kc, :], cv(gcol + kc), rstd, ALU.mult, ALU.mult)

    def ffn(k, gcol):
        rstd = rmsnorm(gcol)
        apply_norm(rstd, gcol, lambda kc: xn[:, kc, :])
        for i in range(11):
            W = piece("f%d_in_%d" % (k, i), (8, 4, 128))
            for jj in range(2):
                j = 2 * i + jj
                pa, pb = ps(), ps()
                for kc in range(8):
                    P.mm(pa[:, :], W[:, kc, 2 * jj, :], xn[:, kc, :], start=(kc == 0), stop=(kc == 7))
                for kc in range(8):
                    P.mm(pb[:, :], W[:, kc, 2 * jj + 1, :], xn[:, kc, :], start=(kc == 0), stop=(kc == 7))
                sa = fs[2 + (j % 2)][:, 0:TT]
                P.act(sa, pa[:, :], AF.Silu)
                P.tt("vector", hid[:, j, :], sa, pb[:, :], ALU.mult)
        for m in range(8):
            W2 = piece("f%d_out_%d" % (k, m), (NFF, 128))
            pb = ps()
            for kc in range(NFF):
                P.mm(pb[:, :], W2[:, kc, :], hid[:, kc, :], start=(kc == 0), stop=(kc == NFF - 1))
            P.stt(h[:, m, :], pb[:, :], 0.5, h[:, m, :], ALU.mult, ALU.add)

    def rsum(out_col, in_):
        P.op("vector", lambda e: e.reduce_sum(out_col.ap, in_.ap, mybir.AxisListType.X), reads=[in_], writes=[out_col])

    def small_rstd(ss_col, out_col, n):
        P.act(out_col, ss_col, AF.Ln, scale=1.0 / n, bias=EPS)
        P.act(out_col, out_col, AF.Exp, scale=-0.5)

    def hgrn_head(j, pre=False):
        pp = j % 2
        NCK = TT // 64
        W = piece("hg_%d" % j, (8, 4, 128))
        pq, pf = ps(), ps()
        if not pre:
            for kc in range(8):
                P.mm(pq[:, :], W[:, kc, 0, :], xn[:, kc, :], start=(kc == 0), stop=(kc == 7))
        for kc in range(8):
            P.mm(pf[:, :], W[:, kc, 1, :], xn[:, kc, :], start=(kc == 0), stop=(kc == 7))
        pg, pi = ps(), ps()
        for b in range(NB):
            if pre:
                break
            for kc in range(8):
                P.mm(pg[:, b * 128:(b + 1) * 128], xn[:, kc, b * 128:(b + 1) * 128], W[:, kc, 2, :], start=(kc == 0), stop=(kc == 7))
        for b in range(NB):
            for kc in range(8):
                P.mm(pi[:, b * 128:(b + 1) * 128], xn[:, kc, b * 128:(b + 1) * 128], W[:, kc, 3, :], start=(kc == 0), stop=(kc == 7))
        q, sg, f, lf, bb, bm, eq, ek = [fs[i][:, 0:TT] for i in range(2, 10)]
        if not pre:
            P.act(q, pq[:, :], AF.Silu)
            P.act(hg_t[pp][:, :], pg[:, :], AF.Silu)
        P.act(sg, pf[:, :], AF.Sigmoid)
        P.copy("scalar", hv_t[pp][:, :], pi[:, :])
        P.ts("vector", f, sg, lbt[:, 8 + j:9 + j], lbt[:, j:j + 1], ALU.mult, ALU.add)
        P.act(lf, f, AF.Ln)
        P.op("vector", lambda e: e.tensor_tensor_scan(bb.ap, cst.t[:, C_RM:C_RM + TT], lf.ap, 0.0, ALU.mult, ALU.add),
             reads=[cv(C_RM, TT), lf], writes=[bb])
        b3 = bb.rr("p (c t) -> p c t", c=NCK)
        bm3 = bm.rr("p (c t) -> p c t", c=NCK)
        P.tt("vector", bm3, b3, b3[:, :, 31:32].bc([128, NCK, 64]), ALU.subtract)
        P.act(ek, bm, AF.Exp, scale=-1.0)
        if not pre:
            P.act(eq, bm, AF.Exp)
            P.stt(hq_e[pp][:, :], q, 128 ** -0.5, eq, ALU.mult, ALU.mult)
            he4 = hq_e[pp][:, :].rr("p (b t) -> p b t", b=NB)
            for par in range(2):
                P.copy("vector", hq_z[par][pp][:, :].rr("p (b t) -> p b t", b=NB)[:, :, par * 64:(par + 1) * 64], he4[:, :, par * 64:(par + 1) * 64])
        P.ts("vector", f, f, -1.0, 1.0, ALU.mult, ALU.add)
        P.tt("vector", hk_e[pp][:, :], f, ek, ALU.mult)
        hs = hsc[pp]
        P.act(hs[:, 0:8], b3[:, :, 63], AF.Exp)
        P.act(hs[:, 8:16], b3[:, :, 31], AF.Exp)
        P.act(hs[:, 16:24], bm3[:, :, 63], AF.Exp)
        tp = tps()
        for b in range(NB):
            P.transpose(tp[:, b * 128:(b + 1) * 128], hk_e[pp][:, b * 128:(b + 1) * 128], ident_b[:, :])
        P.copy("scalar", hk_z[0][pp][0:64, :], tp[0:64, :])
        P.copy("scalar", hk_z[1][pp][64:128, :], tp[64:128, :])

    def hgrn_rec(j0, pre=False):
        HS = (0, 1)
        js = [j0, j0 + 1]
        Sjs = [V(S_h.t[:, j, :], (("S_h", j),)) for j in js]
        for b in range(NB):
            bsl = slice(b * 128, (b + 1) * 128)
            if pre:
                pAs = [ps(), ps()]
                for par in range(2):
                    c = 2 * b + par
                    for h in HS:
                        P.mm(pAs[h][:, 128 * (par + 1):128 * (par + 2)], hk_z[par][h][:, bsl], hv_t[h][:, bsl])
                    for h in HS:
                        P.ts("vector", Sjs[h], Sjs[h], hsc[h][:, c:c + 1], None, ALU.mult)
                        P.stt(Sjs[h], pAs[h][:, 128 * (par + 1):128 * (par + 2)], hsc[h][:, 16 + c:17 + c], Sjs[h], ALU.mult, ALU.add)
                continue
            pAs, pBs = [ps(), ps()], [ps(), ps()]
            for h in HS:
                P.mm(pAs[h][:, 0:128], hk_e[h][:, bsl], hq_e[h][:, bsl])
            for h in HS:
                P.tt("vector", scT[h][0:64, 0:64], pAs[h][0:64, 0:64], cst[0:64, C_TU:C_TU + 64], ALU.mult)
                P.tt("vector", scT[h][64:128, 64:128], pAs[h][64:128, 64:128], cst[64:128, C_TU + 64:C_TU + 128], ALU.mult)
            for h in HS:
                P.mm(pBs[h][:, 0:128], scT[h][:, :], hv_t[h][:, bsl], start=True, stop=False)
            for par in range(2):
                c = 2 * b + par
                for h in HS:
                    P.act(Smid[h][:, :], Sjs[h], AF.Copy, scale=hsc[h][:, 8 + c:9 + c])
                for h in HS:
                    P.mm(pBs[h][:, 0:128], hq_z[par][h][:, bsl], Smid[h][:, :], start=False, stop=(par == 1))
                    P.mm(pAs[h][:, 128 * (par + 1):128 * (par + 2)], hk_z[par][h][:, bsl], hv_t[h][:, bsl])
                for h in HS:
                    P.ts("vector", Sjs[h], Sjs[h], hsc[h][:, c:c + 1], None, ALU.mult)
                    P.stt(Sjs[h], pAs[h][:, 128 * (par + 1):128 * (par + 2)], hsc[h][:, 16 + c:17 + c], Sjs[h], ALU.mult, ALU.add)
            sAB = []
            for h in HS:
                st["ch"] += 1
                cc = 16 + 4 * (st["ch"] % 8)
                sA, sB = sc.sub(cc, (slice(None), slice(cc, cc + 1))), sc.sub(cc, (slice(None), slice(cc + 1, cc + 2)))
                sAB.append((sA, sB))
                P.act(fs[h][:, 0:128], pBs[h][:, 0:128], AF.Square)
                rsum(sA, fs[h][:, 0:128])
            for h in HS:
                small_rstd(sAB[h][0], sAB[h][1], 128)
            for h in HS:
                P.stt(fs[h][:, 0:128], pBs[h][:, 0:128], sAB[h][1], cv(C_HN, 128), ALU.mult, ALU.mult)
                P.tt("vector", o_bf[h][:, :], fs[h][:, 0:128], hg_t[h][:, bsl], ALU.mult)
            for h in HS:
                tp2 = tps()
                P.transpose(tp2[:, 0:128], o_bf[h][:, :], ident_b[:, :])
                P.copy("scalar", oh_fm[:, js[h], bsl], tp2[:, 0:128])

    def gdn_scalars():
        Wab = piece("ab", (8, 32))
        pab = ps()
        for b in range(NB):
            for kc in range(8):
                P.mm(pab[:, b * 32:(b + 1) * 32], xn[:, kc, b * 128:(b + 1) * 128], Wab[:, kc, :], start=(kc == 0), stop=(kc == 7))
        p3 = pab[:, 0:128].rr("p (b c) -> p b c", b=NB)
        z, g_, be = sm[0][:, :], sm[1][:, :], sm[2][:, :]
        z3 = z.rr("p (b c) -> p b c", b=NB)
        P.tt("vector", z3, p3[:, :, 0:16], cv(C_DT, 16).unsq(1).bc([128, NB, 16]), ALU.add)
        P.act(z, z, AF.Exp)
        P.act(z, z, AF.Ln, bias=1.0)
        P.tt("vector", g_.rr("p (b c) -> p b c", b=NB), z3, nea[:, :].unsq(1).bc([128, NB, 16]), ALU.mult)
        P.act(be.rr("p (b c) -> p b c", b=NB), p3[:, :, 16:32], AF.Sigmoid)
        pg_ = ps()
        P.mm(pg_[:, 0:64], cv(C_TU, 128), g_)
        P.mm(pg_[:, 64:128], cv(C_ONE, 128), g_)
        gam, egam, begam, egk, egend = [sm[i][:, :] for i in range(3, 8)]
        P.copy("vector", gam, pg_[:, 0:64])
        P.act(egam, gam, AF.Exp)
        P.tt("vector", begam, be, egam, ALU.mult)
        P.tt("vector", egk, pg_[:, 64:128], gam, ALU.subtract)
        P.act(egk, egk, AF.Exp)
        P.copy("vector", egend, pg_[:, 64:128])
        P.act(egend, egend, AF.Exp)
        return g_, be, egam, begam, egk, egend

    def gdn_head(j, scal, pre=False):
        g_, be, egam, begam, egk, egend = scal
        W = piece("gd_%d" % j, (8, 4, 128))
        if GDN_LEVEL < 1:
            piece("gz_%d" % j, (8, 2, 128))
            return
        gidx = (j, 8 + j, 16 + 2 * j, 17 + 2 * j)
        ys = [None] * 4
        for c in range(4):
            pc = ps()
            for kc in range(8):
                P.mm(pc[:, :], W[:, kc, c, :], xn[:, kc, :], start=(kc == 0), stop=(kc == 7))
            cb = fs[2 + c]
            gi = gidx[c]
            P.copy("vector", cb[:, 0:3], halo[:, gi, 0:3])
            P.copy("scalar", cb[:, 3:3 + TT], pc[:, :])
            if GDN_SUB >= 1:
                P.copy("vector", halo[:, gi, 0:3], cb[:, TT:TT + 3])
            acc = fs[6 + c][:, 0:TT]
            if pre and c == 0:
                continue
            if c >= 2 and POOL_CONV:
                tmpc = fs[c - 2][:, 0:TT]
                P.ts("gpsimd", acc, cb[:, 3:3 + TT], cv(C_CW + gi * 4 + 3), 0.0, ALU.mult, ALU.add)
                for tpi in range(3):
                    P.ts("gpsimd", tmpc, cb[:, tpi:tpi + TT], cv(C_CW + gi * 4 + tpi), 0.0, ALU.mult, ALU.add)
                    P.tt("gpsimd", acc, acc, tmpc, ALU.add)
            else:
                P.ts("vector", acc, cb[:, 3:3 + TT], cv(C_CW + gi * 4 + 3), None, ALU.mult)
                for tpi in range(3):
                    P.stt(acc, cb[:, tpi:tpi + TT], cv(C_CW + gi * 4 + tpi), acc, ALU.mult, ALU.add)
            ys[c] = acc
        if GDN_SUB < 3:
            piece("gz_%d" % j, (8, 2, 128))
            return
        for c, dst, scl in ((0, gq_f, 128 ** -0.5), (1, gk_f, 1.0)):
            if pre and c == 0:
                continue
            y = fs[2 + c][:, 0:TT]
            P.act(y, ys[c], AF.Silu)
            P.act(bs[c][:, :], y, AF.Square)
            if GDN_SUB < 4:
                continue
            pn = ps()
            P.mm(pn[:, :], ones_b[:, :], bs[c][:, :])
            rn = ys[c]
            P.act(rn, pn[:, :], AF.Ln, bias=EPS)
            P.act(rn, rn, AF.Exp, scale=-0.5)
            if GDN_SUB < 5:
                continue
            P.stt(dst[:, :], y, scl, rn, ALU.mult, ALU.mult)
        if GDN_SUB >= 6:
            for vh in range(2):
                P.act(gv_f[vh][:, :], ys[2 + vh], AF.Silu)
        Wz = piece("gz_%d" % j, (8, 2, 128))
        if GDN_LEVEL < 2:
            return
        pz = [ps(), ps()]
        for b in range(NB):
            if pre:
                break
            for kc in range(8):
                P.mm(pz[b // 2][:, (b % 2) * 256:(b % 2) * 256 + 256], xn[:, kc, b * 128:(b + 1) * 128],
                     Wz[:, kc, :, :].rr("p a b -> p (a b)"), start=(kc == 0), stop=(kc == 7))
        for i in range(2):
            if pre:
                break
            P.act(gz_t[:, 2 * i:2 * i + 2, :, :].rr("p a b c -> p (a b c)"), pz[i][:, :], AF.Silu)
        tk = tps()
        for b in range(NB):
            P.transpose(tk[:, b * 128:(b + 1) * 128], gk_f[:, b * 128:(b + 1) * 128], ident_b[:, :])
        r3 = lambda v: v.rr("p (b t) -> p b t", b=NB)
        sc3 = lambda v, hh: v.rr("p (b c) -> p b c", b=NB)[:, :, hh:hh + 1].bc([128, NB, 128])
        for vh in range(2):
            hh = 2 * j + vh
            P.tt("vector", r3(kbg_t[vh][:, :]), r3(tk), sc3(begam, hh), ALU.mult)
            P.tt("vector", r3(kte_t[vh][:, :]), r3(tk), sc3(egk, hh), ALU.mult)
            tv = tps()
            for b in range(NB):
                P.transpose(tv[:, b * 128:(b + 1) * 128], gv_f[vh][:, b * 128:(b + 1) * 128], ident_b[:, :])
            P.tt("vector", r3(bv_t[vh][:, :]), r3(tv), sc3(be, hh), ALU.mult)
        g3 = g_.rr("p (b c) -> p b c", b=NB)
        for b in range(NB):
            P.tt("vector", Gm[:, 2 * b:2 * b + 2, :], cv(C_SL, 128).unsq(1).bc([128, 2, 128]),
                 g3[:, b, 2 * j:2 * j + 2].unsq(2).bc([128, 2, 128]), ALU.mult)
        for i in range(2):
            pd = ps()
            P.mm(pd[:, :], cv(C_TU, 128), Gm[:, 4 * i:4 * i + 4, :].rr("p a b -> p (a b)"))
            Dc = fs[i][:, 0:TT]
            P.ts("vector", Dc, pd[:, :], -80.0, None, ALU.max)
            P.act(Ex[:, 4 * i:4 * i + 4, :].rr("p a b -> p (a b)"), Dc, AF.Exp)
        P.tt("vector", Ls[:, :, :], Ex[:, :, :], cv(C_SL, 128).unsq(1).bc([128, 8, 128]), ALU.mult)
        P.tt("vector", Lm[:, :, :], Ex[:, :, :], cv(C_TL, 128).unsq(1).bc([128, 8, 128]), ALU.mult)
        for b in range(NB):
            bsl = slice(b * 128, (b + 1) * 128)
            pk_ = ps()
            P.mm(pk_[:, 0:128], gk_f[:, bsl], gk_f[:, bsl])
            if pre:
                P.copy("scalar", Gm[:, 2 * b, :], pk_[:, 0:128])
                continue
            P.mm(pk_[:, 128:256], gq_f[:, bsl], gk_f[:, bsl])
            P.copy("scalar", Gm[:, 2 * b:2 * b + 2, :].rr("p a b -> p (a b)"), pk_[:, 0:256])
        F32R = mybir.dt.float32r
        CH_R = (lambda v: v.bitcast(F32R)) if CHAIN_F32R else (lambda v: v)
        Aall = Ex
        for vh in range(2):
            tb = tps()
            for b in range(NB):
                hh = 2 * j + vh
                col = b * 16 + hh
                q_ = 2 * b + vh
                P.stt(CH_R(Aall[:, q_, :]), Gm[:, 2 * b, :], be[:, col:col + 1], Ls[:, q_, :], ALU.mult, ALU.mult)
                if not pre:
                    P.tt("vector", qkl[b % 2][:, :], Gm[:, 2 * b + 1, :], Lm[:, q_, :], ALU.mult)
                    P.transpose(tb[:, b * 128:(b + 1) * 128], qkl[b % 2][:, :], ident_b[:, :])
            if not pre:
                P.copy("scalar", qklT_all[:, vh, :, :].rr("p a b -> p (a b)"), tb)
        identF = cv(C_ID, 128)
        Bv = [fs[4][:, 0:TT].rr("p (a b) -> p a b", a=NB), fs[5][:, 0:TT].rr("p (a b) -> p a b", a=NB)]
        Avs = [V(Ex.t[:, vh:8:2, :], (("Ex", 0),)) for vh in range(2)]
        X2 = [Gm[:, :, :], X2b[:, :, :]]
        for vh in range(2):
            pT = ps()
            for b in range(NB):
                P.transpose(pT[:, b * 128:(b + 1) * 128], Ex[:, 2 * b + vh, :], identF)
            P.copy("vector", CH_R(Bv[vh].rr("p a b -> p (a b)")), pT[:, :])
        for k in range(7):
            if k == 0:
                for vh in range(2):
                    P.tt("vector", CH_R(X2[vh][:, 0:NB, :]), Bv[vh], cv(C_MK, 128).unsq(1).bc([128, NB, 128]), ALU.mult)
                    P.tt("vector", CH_R(X2[vh][:, NB:2 * NB, :]), Avs[vh], cv(C_MK + 128, 128).unsq(1).bc([128, NB, 128]), ALU.mult)
                    P.tt("vector", CH_R(TT2[vh][:, :, :]), identF.unsq(1).bc([128, 2 * NB, 128]), X2[vh][:, :, :], ALU.subtract)
                continue
            pXs = []
            for vh in range(2):
                pX, pX2 = ps(), ps()
                for b in range(NB):
                    P.mm(pX[:, b * 128:(b + 1) * 128], CH_R(Avs[vh][:, b, :]), CH_R(TT2[vh][:, b, :]))
                for b in range(NB):
                    P.mm(pX2[:, b * 128:(b + 1) * 128], CH_R(Bv[vh][:, b, :]), CH_R(TT2[vh][:, NB + b, :]))
                pXs.append((pX, pX2))
            for vh in range(2):
                pX, pX2 = pXs[vh]
                P.tt("vector", CH_R(X2[vh][:, 0:NB, :]), pX[:, :].rr("p (a b) -> p a b", a=NB),
                     cv(C_MK + 256 * k, 128).unsq(1).bc([128, NB, 128]), ALU.mult)
                P.tt("vector", CH_R(X2[vh][:, NB:2 * NB, :]), pX2[:, :].rr("p (a b) -> p a b", a=NB),
                     cv(C_MK + 256 * k + 128, 128).unsq(1).bc([128, NB, 128]), ALU.mult)
            pYs = []
            for vh in range(2):
                pY, pY2 = ps(), ps()
                for b in range(NB):
                    P.mm(pY[:, b * 128:(b + 1) * 128], CH_R(TT2[vh][:, NB + b, :]), CH_R(X2[vh][:, b, :]))
                for b in range(NB):
                    P.mm(pY2[:, b * 128:(b + 1) * 128], CH_R(TT2[vh][:, b, :]), CH_R(X2[vh][:, NB + b, :]))
                pYs.append((pY, pY2))
            for vh in range(2):
                pY, pY2 = pYs[vh]
                P.tt("vector", CH_R(TT2[vh][:, 0:NB, :].rr("p a b -> p (a b)")), TT2[vh][:, 0:NB, :].rr("p a b -> p (a b)"), pY[:, :], ALU.subtract)
                P.tt("vector", CH_R(TT2[vh][:, NB:2 * NB, :].rr("p a b -> p (a b)")), TT2[vh][:, NB:2 * NB, :].rr("p a b -> p (a b)"), pY2[:, :], ALU.subtract)
        for vh in range(2):
            P.copy("vector", Ttb_g[:, vh, :, :], TT2[vh][:, 0:NB, :])
        VH = (0, 1)
        hhs = [2 * j + vh for vh in VH]
        Sfs = [V(S_g.t[:, hh, :], (("S_g", hh),)) for hh in hhs]
        Sbs = [V(Sb_g.t[:, hh, :], (("Sb_g", hh),)) for hh in hhs]
        for b in range(NB):
            bsl = slice(b * 128, (b + 1) * 128)
            cols = [b * 16 + hh for hh in hhs]
            pus = [ps(), ps()]
            for vh in VH:
                P.mm(pus[vh][:, 0:128], Ttb_g[:, vh, b, :], bv_t[vh][:, bsl])
                P.mm(pus[vh][:, 128:256], kbg_t[vh][:, bsl], Ttb_g[:, vh, b, :])
            for vh in VH:
                P.copy("scalar", u_sb[vh][:, :], pus[vh][:, 0:128])
                P.copy("scalar", wT_sb[vh][:, :], pus[vh][:, 128:256])
            prs = [ps(), ps()]
            for vh in VH:
                P.mm(prs[vh][:, 0:128], wT_sb[vh][:, :], Sbs[vh])
            for vh in VH:
                P.tt("vector", vnew[vh][:, :], u_sb[vh][:, :], prs[vh][:, 0:128], ALU.subtract)
            for vh in VH:
                if not pre:
                    P.mm(prs[vh][:, 128:256], gq_f[:, bsl], Sbs[vh])
                    P.mm(prs[vh][:, 256:384], qklT_all[:, vh, b, :], vnew[vh][:, :])
                P.mm(prs[vh][:, 384:512], kte_t[vh][:, bsl], vnew[vh][:, :])
            for vh in VH:
                P.stt(Sfs[vh], Sfs[vh], egend[:, cols[vh]:cols[vh] + 1], prs[vh][:, 384:512], ALU.mult, ALU.add)
                P.copy("scalar", Sbs[vh], Sfs[vh])
            if pre:
                continue
            for vh in VH:
                P.act(o_sb[vh][:, :], prs[vh][:, 128:256], AF.Copy, scale=egam[:, cols[vh]:cols[vh] + 1])
                P.tt("vector", o_sb[vh][:, :], o_sb[vh][:, :], prs[vh][:, 256:384], ALU.add)
            sAB = []
            for vh in VH:
                st["ch"] += 1
                cc = 16 + 4 * (st["ch"] % 8)
                sA, sB = sc.sub(cc, (slice(None), slice(cc, cc + 1))), sc.sub(cc, (slice(None), slice(cc + 1, cc + 2)))
                sAB.append((sA, sB))
                P.act(fs[vh][:, 0:128], o_sb[vh][:, :], AF.Square)
                rsum(sA, fs[vh][:, 0:128])
            for vh in VH:
                small_rstd(sAB[vh][0], sAB[vh][1], 128)
            for vh in VH:
                P.stt(fs[vh][:, 0:128], o_sb[vh][:, :], sAB[vh][1], cv(C_GN, 128), ALU.mult, ALU.mult)
                P.tt("vector", vnew[vh][:, :], fs[vh][:, 0:128], gz_t[:, b, vh, :], ALU.mult)
            for vh in VH:
                tp2 = tps()
                P.transpose(tp2[:, 0:128], vnew[vh][:, :], ident_b[:, :])
                P.copy("scalar", og_fm[:, hhs[vh], bsl], tp2[:, 0:128])

    def merge_and_out():
        for i in range(2):
            Wgh = piece("gh_%d" % i, (8, 4, 128))
            Wbh = piece("bh_%d" % i, (4, 8, 128))
            for m in range(4):
                pgt, py = ps(), ps()
                for kc in range(8):
                    P.mm(pgt[:, :], Wgh[:, kc, m, :], xn[:, kc, :], start=(kc == 0), stop=(kc == 7))
                for hd in range(8):
                    P.mm(py[:, :], Wbh[:, m, hd, :], oh_fm[:, hd, :], start=(hd == 0), stop=(hd == 7))
                sg = fs[2 + (m % 2)][:, 0:TT]
                P.act(sg, pgt[:, :], AF.Sigmoid)
                P.tt("vector", ymf[:, m, :], sg, py[:, :], ALU.mult)
            Wgg = piece("gg_%d" % i, (8, 4, 128))
            for m in range(4):
                if m % 2 == 0:
                    Wbg = piece("bg_%d" % (2 * i + m // 2), (2, 16, 128))
                pgt, py = ps(), ps()
                for kc in range(8):
                    P.mm(pgt[:, :], Wgg[:, kc, m, :], xn[:, kc, :], start=(kc == 0), stop=(kc == 7))
                for hd in range(16):
                    P.mm(py[:, :], Wbg[:, m % 2, hd, :], og_fm[:, hd, :], start=(hd == 0), stop=(hd == 15))
                sg = fs[2 + (m % 2)][:, 0:TT]
                P.act(sg, pgt[:, :], AF.Sigmoid)
                P.tt("vector", sg, sg, py[:, :], ALU.mult)
                P.tt("vector", ym[:, 4 * i + m, :], sg, ymf[:, m, :], ALU.add)
        for i in range(2):
            Wo = piece("wo_%d" % i, (4, 8, 128))
            for m in range(4):
                po = ps()
                for kc in range(8):
                    P.mm(po[:, :], Wo[:, m, kc, :], ym[:, kc, :], start=(kc == 0), stop=(kc == 7))
                P.tt("vector", h[:, 4 * i + m, :], h[:, 4 * i + m, :], po[:, :], ALU.add)

    for t in range(n_tiles):
        pre = t < n_pre
        tsl = slice(t * TT, (t + 1) * TT)
        osl = slice((t - n_pre) * TT, (t - n_pre + 1) * TT)
        P.dma("sync", h[:, :, :], x_d[:, tsl].rr("(kc p) t -> p kc t", p=128), "xin")
        ffn(1, C_G1)
        rstd = rmsnorm(C_GM)
        apply_norm(rstd, C_GM, lambda kc: xn[:, kc, :])
        for j in range(8):
            hgrn_head(j, pre)
            if j % 2 == 1:
                hgrn_rec(j - 1, pre)
        scal = gdn_scalars()
        for j in range(8):
            gdn_head(j, scal, pre)
        if pre:
            for i in range(2):
                for nm in ("gh_%d" % i, "bh_%d" % i, "gg_%d" % i, "bg_%d" % (2 * i), "bg_%d" % (2 * i + 1)):
                    piece(nm, (8, 4, 128))
            piece("wo_0", (4, 8, 128))
            piece("wo_1", (4, 8, 128))
            for i in range(11):
                piece("f2_in_%d" % i, (8, 4, 128))
            for m in range(8):
                piece("f2_out_%d" % m, (NFF, 128))
            continue
        merge_and_out()
        ffn(2, C_G2)
        rstd = rmsnorm(C_GF)
        apply_norm(rstd, C_GF, lambda kc: h[:, kc, :])
        P.dma("sync", out_d[:, osl].rr("(kc p) t -> p kc t", p=128), h[:, :, :], "xout")
    P.emit()
    P.close()
    return nc


_CACHE = {}


def kernel(**inputs):
    inp = {k: np.asarray(v) for k, v in inputs.items()}
    x = inp["x"]
    B, S, _ = x.shape
    half = S // 2
    if "nc" not in _CACHE:
        _CACHE["nc"] = build_program(S, n_pre=half // TT)
    nc = _CACHE["nc"]
    ws = pack_weights(inp)
    cst = pack_consts(inp)
    in_maps = []
    for b in range(B):
        for hf in range(2):
            prev = np.zeros((half, x.shape[2]), np.float32) if hf == 0 else x[b, :half]
            own = x[b, hf * half:(hf + 1) * half]
            in_maps.append({"xT": np.ascontiguousarray(np.concatenate([prev, own], axis=0).T), "ws": ws, "cst": cst})
    res = run_bass_kernel_spmd(nc, in_maps, core_ids=list(range(2 * B)))
    out = np.empty((B, S, x.shape[2]), np.float32)
    for b in range(B):
        for hf in range(2):
            out[b, hf * half:(hf + 1) * half] = np.asarray(res.results[2 * b + hf]["outT"]).T
    return out
```

```python
import os
import numpy as np
import concourse.bass as bass
import concourse.mybir as mybir
from concourse.bass_utils import run_bass_kernel_spmd

F32 = mybir.dt.float32
BF16 = mybir.dt.bfloat16
AF = mybir.ActivationFunctionType
ALU = mybir.AluOpType


STRICT_SAME_ENGINE = True


class V:
    __slots__ = ("ap", "keys")

    def __init__(self, ap, keys):
        self.ap = ap
        self.keys = keys

    def __getitem__(self, idx):
        return V(self.ap[idx], self.keys)

    def bc(self, shape):
        return V(self.ap.to_broadcast(list(shape)), self.keys)

    def unsq(self, axis):
        return V(self.ap.unsqueeze(axis), self.keys)

    def bitcast(self, dt):
        return V(self.ap.bitcast(dt), self.keys)

    def rr(self, s, **kw):
        return V(self.ap.rearrange(s, **kw), self.keys)


class Buf:
    def __init__(self, t, name, nsub=1):
        self.t = t
        self.name = name

    def __getitem__(self, idx):
        return V(self.t[idx], ((self.name, 0),))

    def sub(self, k, idx):
        return V(self.t[idx], ((self.name, k),))


class Op:
    __slots__ = ("eng", "fn", "deps", "needs_inc", "sem", "val", "is_dma", "idx")

    def __init__(self, eng, fn, is_dma):
        self.eng = eng
        self.fn = fn
        self.deps = []
        self.needs_inc = False
        self.sem = None
        self.val = None
        self.is_dma = is_dma


class Prog:
    ENGS = ("sync", "scalar", "vector", "gpsimd", "tensor")

    def __init__(self, nc):
        self.nc = nc
        self.ops = {e: [] for e in self.ENGS}
        self.state = {}
        self.stack = []
        self.dma_sems = {}
        self.dma_counts = {}
        self.eng_sems = {}
        self.all_dma_ops = []

    def sbuf(self, name, shape, dt):
        g = self.nc.sbuf_tensor(name, list(shape), dt)
        t = g.__enter__()
        self.stack.append(g)
        return Buf(t, name)

    def psum(self, name, shape, dt):
        g = self.nc.psum_tensor(name, list(shape), dt)
        t = g.__enter__()
        self.stack.append(g)
        return Buf(t, name)

    def sem(self, name):
        g = self.nc.semaphore(name)
        s = g.__enter__()
        self.stack.append(g)
        return s

    def op(self, eng, fn, reads=(), writes=(), dma_key=None):
        o = Op(eng, fn, dma_key is not None)
        rkey = ("dma", dma_key) if dma_key is not None else eng
        deps = {}
        for v in reads:
            for k in v.keys:
                st = self.state.get(k)
                if st and st[0] is not None:
                    deps[id(st[0])] = (st[0], "raw")
        for v in writes:
            for k in v.keys:
                st = self.state.get(k)
                if st:
                    if st[0] is not None and id(st[0]) not in deps:
                        deps[id(st[0])] = (st[0], "waw")
                    for r in st[1].values():
                        if id(r) not in deps:
                            deps[id(r)] = (r, "war")
        for d, kind in deps.values():
            if d is o:
                continue
            if not d.is_dma and not o.is_dma and d.eng == eng:
                if eng == "tensor":
                    continue
                if kind == "waw" and not STRICT_SAME_ENGINE:
                    continue
                if kind == "war" and not STRICT_SAME_ENGINE:
                    continue
            o.deps.append(d)
            d.needs_inc = True
        for v in reads:
            for k in v.keys:
                st = self.state.setdefault(k, [None, {}])
                st[1][rkey] = o
        for v in writes:
            for k in v.keys:
                self.state[k] = [o, {}]
        if dma_key is not None:
            if dma_key not in self.dma_sems:
                self.dma_sems[dma_key] = self.sem("d_" + str(dma_key))
                self.dma_counts[dma_key] = 0
            self.dma_counts[dma_key] += 16
            o.sem = self.dma_sems[dma_key]
            o.val = self.dma_counts[dma_key]
            o.needs_inc = True
            self.all_dma_ops.append(o)
        self.ops[eng].append(o)
        return o

    def dma(self, eng, out, in_, key):
        return self.op(eng, lambda e: e.dma_start(out=out.ap, in_=in_.ap),
                       reads=[in_], writes=[out], dma_key=key)

    def mm(self, out, lhsT, rhs, start=True, stop=True, extra_reads=()):
        return self.op("tensor", lambda e: e.matmul(out.ap, lhsT.ap, rhs.ap, start=start, stop=stop),
                       reads=[lhsT, rhs] + list(extra_reads), writes=[out])

    def transpose(self, out, in_, ident):
        return self.op("tensor", lambda e: e.transpose(out.ap, in_.ap, ident.ap),
                       reads=[in_, ident], writes=[out])

    def act(self, out, in_, func, bias=None, scale=None, accum_out=None, eng="scalar"):
        reads = [in_]
        kw = {}
        if bias is not None:
            if isinstance(bias, V):
                reads.append(bias)
                kw["bias"] = bias.ap
            else:
                kw["bias"] = bias
        if scale is not None:
            if isinstance(scale, V):
                reads.append(scale)
                kw["scale"] = scale.ap
            else:
                kw["scale"] = scale
        writes = [out]
        if accum_out is not None:
            writes.append(accum_out)
            kw["accum_out"] = accum_out.ap
        return self.op("scalar", lambda e: e.activation(out.ap, in_.ap, func, **kw),
                       reads=reads, writes=writes)

    def tt(self, eng, out, in0, in1, op):
        return self.op(eng, lambda e: e.tensor_tensor(out.ap, in0.ap, in1.ap, op),
                       reads=[in0, in1], writes=[out])

    def ts(self, eng, out, in0, s1, s2, op0, op1=None, accum_out=None):
        reads = [in0]
        a1 = s1
        a2 = s2
        if isinstance(s1, V):
            reads.append(s1)
            a1 = s1.ap
        if isinstance(s2, V):
            reads.append(s2)
            a2 = s2.ap
        writes = [out]
        kw = {}
        if accum_out is not None:
            writes.append(accum_out)
            kw["accum_out"] = accum_out.ap
        if op1 is None:
            return self.op(eng, lambda e: e.tensor_scalar(out.ap, in0.ap, a1, a2, op0, **kw),
                           reads=reads, writes=writes)
        return self.op(eng, lambda e: e.tensor_scalar(out.ap, in0.ap, a1, a2, op0, op1, **kw),
                       reads=reads, writes=writes)

    def stt(self, out, in0, scalar, in1, op0, op1, eng="vector"):
        reads = [in0, in1]
        a = scalar
        if isinstance(scalar, V):
            reads.append(scalar)
            a = scalar.ap
        return self.op(eng, lambda e: e.scalar_tensor_tensor(out.ap, in0.ap, a, in1.ap, op0, op1),
                       reads=reads, writes=[out])

    def copy(self, eng, out, in_):
        if eng == "scalar":
            return self.op(eng, lambda e: e.copy(out.ap, in_.ap), reads=[in_], writes=[out])
        return self.op(eng, lambda e: e.tensor_copy(out.ap, in_.ap), reads=[in_], writes=[out])

    def memset(self, eng, out, val):
        return self.op(eng, lambda e: e.memset(out.ap, val), reads=[], writes=[out])

    def emit(self, final_wait_ops=()):
        nc = self.nc
        for eng in self.ENGS:
            cnt = 0
            for o in self.ops[eng]:
                if o.is_dma:
                    continue
                if o.needs_inc:
                    if eng not in self.eng_sems:
                        self.eng_sems[eng] = self.sem("e_" + eng)
                    cnt += 1
                    o.sem = self.eng_sems[eng]
                    o.val = cnt
        final_dmas = list(self.all_dma_ops)
        with nc.Block() as block:
            def run(eng_name):
                def body(e):
                    waited = {}
                    for o in self.ops[eng_name]:
                        need = {}
                        for d in o.deps:
                            k = id(d.sem)
                            if k not in need or need[k][1] < d.val:
                                need[k] = (d.sem, d.val)
                        for k, (s, v) in need.items():
                            if waited.get(k, 0) >= v:
                                continue
                            e.wait_ge(s, v)
                            waited[k] = v
                        ins = o.fn(e)
                        if o.needs_inc:
                            ins.then_inc(o.sem, 16 if o.is_dma else 1)
                    if eng_name == "sync":
                        last = {}
                        for o in final_dmas:
                            last[id(o.sem)] = (o.sem, max(o.val, last.get(id(o.sem), (None, 0))[1]))
                        for k, (s, v) in last.items():
                            if waited.get(k, 0) < v:
                                e.wait_ge(s, v)
                return body
            block.sync(run("sync"))
            block.scalar(run("scalar"))
            block.vector(run("vector"))
            block.gpsimd(run("gpsimd"))
            block.tensor(run("tensor"))

    def close(self):
        while self.stack:
            g = self.stack.pop()
            g.__exit__(None, None, None)


import os

D = 1024
DFF = 2816
NFF = DFF // 128
TT = 512
NB = TT // 128
EPS = 1e-6
PW = 4096
NSLOT = 4
TDT = F32
POOL_CONV = bool(int(os.environ.get('POOL_CONV', '0')))
CHAIN_F32R = bool(int(os.environ.get('CHAIN_F32R', '0')))

O_HQ, O_HF, O_HI, O_HG = 0, 1024, 2048, 3072
O_GQ, O_GK, O_GV, O_GA, O_GB, O_GZ, O_GH, O_GG = 4096, 5120, 6144, 8192, 8208, 8224, 10272, 11296

C_ID, C_ONE, C_TL, C_SL, C_TU, C_RM = 0, 128, 256, 384, 512, 640
C_G1, C_GM, C_G2, C_GF = 1152, 1160, 1168, 1176
C_HN, C_GN = 1184, 1312
C_L0, C_L1 = 1440, 1448
C_CW = 1456
C_AL, C_DT = 1584, 1600
C_MK = 1616
NCST = 1616 + 7 * 256


def piece_names():
    names = []
    for k in (1, 2):
        if k == 2:
            names += ["hg_%d" % j for j in range(8)] + ["ab"]
            for j in range(8):
                names += ["gd_%d" % j, "gz_%d" % j]
            for i in range(2):
                names += ["gh_%d" % i, "bh_%d" % i, "gg_%d" % i, "bg_%d" % (2 * i), "bg_%d" % (2 * i + 1)]
            names += ["wo_0", "wo_1"]
        names += ["f%d_in_%d" % (k, i) for i in range(11)]
        names += ["f%d_out_%d" % (k, m) for m in range(8)]
    return names


def pack_weights(inp):
    names = piece_names()
    ws = np.zeros((len(names), 128, PW), np.float32)

    def put(name, arr):
        a = np.ascontiguousarray(arr).reshape(128, -1)
        ws[names.index(name), :, :a.shape[1]] = a

    for k, (wi, wo) in enumerate(((inp["ffn1_w_in"], inp["ffn1_w_out"]), (inp["ffn2_w_in"], inp["ffn2_w_out"])), 1):
        wr = np.asarray(wi)[0].reshape(8, 128, 2 * DFF)
        for i in range(11):
            cols = []
            for j in (2 * i, 2 * i + 1):
                cols.append(wr[:, :, j * 128:(j + 1) * 128])
                cols.append(wr[:, :, DFF + j * 128:DFF + (j + 1) * 128])
            put("f%d_in_%d" % (k, i), np.stack(cols, axis=2).transpose(1, 0, 2, 3))
        w2 = np.asarray(wo)[0].reshape(NFF, 128, D)
        for m in range(8):
            put("f%d_out_%d" % (k, m), w2[:, :, m * 128:(m + 1) * 128].transpose(1, 0, 2))
    wr = np.asarray(inp["w_in"])[0].reshape(8, 128, -1)

    def cols(off, j, n=128):
        return wr[:, :, off + j * 128: off + j * 128 + n]

    for j in range(8):
        put("hg_%d" % j, np.stack([cols(O_HQ, j), cols(O_HF, j), cols(O_HG, j), cols(O_HI, j)], axis=2).transpose(1, 0, 2, 3))
        put("gd_%d" % j, np.stack([cols(O_GQ, j), cols(O_GK, j), cols(O_GV, 2 * j), cols(O_GV, 2 * j + 1)], axis=2).transpose(1, 0, 2, 3))
        put("gz_%d" % j, np.stack([cols(O_GZ, 2 * j), cols(O_GZ, 2 * j + 1)], axis=2).transpose(1, 0, 2, 3))
    put("ab", wr[:, :, O_GA:O_GA + 32].transpose(1, 0, 2))
    for i in range(2):
        put("gh_%d" % i, np.stack([cols(O_GH, 4 * i + m) for m in range(4)], axis=2).transpose(1, 0, 2, 3))
        put("gg_%d" % i, np.stack([cols(O_GG, 4 * i + m) for m in range(4)], axis=2).transpose(1, 0, 2, 3))
    bh = np.asarray(inp["w_branch_hgrn"])[0].reshape(8, 128, D)
    for i in range(2):
        put("bh_%d" % i, np.stack([bh[:, :, (4 * i + m) * 128:(4 * i + m + 1) * 128] for m in range(4)], axis=0).transpose(2, 0, 1, 3))
    bg = np.asarray(inp["w_branch_gdn"])[0].reshape(16, 128, D)
    for i in range(4):
        put("bg_%d" % i, np.stack([bg[:, :, (2 * i + m) * 128:(2 * i + m + 1) * 128] for m in range(2)], axis=0).transpose(2, 0, 1, 3))
    wo = np.asarray(inp["w_out"])[0].reshape(8, 128, D)
    for i in range(2):
        put("wo_%d" % i, np.stack([wo[:, :, (4 * i + m) * 128:(4 * i + m + 1) * 128] for m in range(4)], axis=0).transpose(2, 0, 1, 3))
    return ws


def pack_consts(inp):
    c = np.zeros((128, NCST), np.float32)
    r = np.arange(128)
    c[:, C_ID:C_ID + 128] = np.eye(128)
    c[:, C_ONE:C_ONE + 128] = 1.0
    c[:, C_TL:C_TL + 128] = (r[None, :] <= r[:, None])
    c[:, C_SL:C_SL + 128] = (r[None, :] < r[:, None])
    c[:, C_TU:C_TU + 128] = (r[:, None] <= r[None, :])
    rm = np.ones(512, np.float32)
    rm[::64] = 0.0
    c[:, C_RM:C_RM + 512] = rm[None, :]
    for col, key in ((C_G1, "ffn1_norm"), (C_GM, "mix_norm"), (C_G2, "ffn2_norm")):
        c[:, col:col + 8] = np.asarray(inp[key])[0].reshape(8, 128).T
    c[:, C_GF:C_GF + 8] = np.asarray(inp["final_norm"]).reshape(8, 128).T
    c[:, C_HN:C_HN + 128] = np.asarray(inp["hgrn_out_norm"])[0][None, :]
    c[:, C_GN:C_GN + 128] = np.asarray(inp["gdn_out_norm"])[0][None, :]
    lbl = np.asarray(inp["hgrn_lb_logits"])
    c[:, C_L0:C_L0 + 8] = lbl[0].reshape(8, 128).T
    c[:, C_L1:C_L1 + 8] = lbl[1].reshape(8, 128).T
    cw = np.asarray(inp["gdn_conv_w"])[0]
    c[:, C_CW:C_CW + 128] = cw.T.reshape(32, 128, 4).transpose(1, 0, 2).reshape(128, 128)
    for k in range(7):
        n = 1 << k
        tt_, ss_ = r[:, None], r[None, :]
        mk = ((tt_ // (2 * n) == ss_ // (2 * n)) & (tt_ % (2 * n) >= n) & (ss_ % (2 * n) < n)).astype(np.float32)
        c[:, C_MK + 256 * k:C_MK + 256 * k + 128] = mk.T
        c[:, C_MK + 256 * k + 128:C_MK + 256 * (k + 1)] = mk
    c[:, C_AL:C_AL + 16] = np.asarray(inp["gdn_a_log"])[0][None, :]
    c[:, C_DT:C_DT + 16] = np.asarray(inp["gdn_dt_bias"])[0][None, :]
    return c


import os
GDN_LEVEL = int(os.environ.get('GDN_LEVEL', '9'))
GDN_SUB = int(os.environ.get('GDN_SUB', '9'))


def build_program(n_tok, taps=None, stages=("ffn1", "hgrn", "gdn", "merge", "ffn2"), n_pre=0):
    n_tiles = n_tok // TT
    names = piece_names()
    NP = len(names)
    nc = bass.Bass("TRN2", target_bir_lowering=False)
    P = Prog(nc)
    x_d = Buf(nc.dram_tensor("xT", [D, n_tok], F32, kind="ExternalInput").ap(), "xT")
    WSB = bool(int(os.environ.get("WS_BF16", "0")))
    ws_d = Buf(nc.dram_tensor("ws", [NP, 128, PW], BF16 if WSB else F32, kind="ExternalInput").ap(), "ws")
    cst_d = Buf(nc.dram_tensor("cst", [128, NCST], F32, kind="ExternalInput").ap(), "cstd")
    out_d = Buf(nc.dram_tensor("outT", [D, n_tok - n_pre * TT], F32, kind="ExternalOutput").ap(), "outT")
    tap_d = {}
    if taps:
        for nm, shp in taps.items():
            tdt = BF16 if nm in ("oh", "og") else F32
            tap_d[nm] = Buf(nc.dram_tensor("tap_" + nm, list(shp), tdt, kind="ExternalOutput").ap(), "tap_" + nm)

    tapped = set()

    def tap(nm, v):
        if taps and nm in tap_d and nm not in tapped:
            tapped.add(nm)
            P.dma("sync", V(tap_d[nm].t, ((tap_d[nm].name, 0),)), v, "tap_" + nm)

    cst = P.sbuf("cst_sb", [128, NCST], F32)
    ident_b = P.sbuf("ident_b", [128, 128], BF16)
    ones_b = P.sbuf("ones_b", [128, 128], BF16)
    tu_b = P.sbuf("tu_b", [128, 128], BF16)
    lbt = P.sbuf("lbt", [128, 16], F32)
    nea = P.sbuf("nea", [128, 16], F32)
    slots = [P.sbuf("slot%d" % i, [128, PW], BF16) for i in range(NSLOT)]
    h = P.sbuf("h", [128, 8, TT], F32)
    xn = P.sbuf("xn", [128, 8, TT], BF16)
    hid = P.sbuf("hid", [128, NFF, TT], BF16)
    fs = [P.sbuf("fs%d" % i, [128, TT + 4], F32) for i in range(10)]
    oh_fm = P.sbuf("oh_fm", [128, 8, TT], BF16)
    og_fm = hid[:, 0:16, :]
    ymf = hid[:, 16:20, :]
    ym = P.sbuf("ym", [128, 8, TT], BF16)
    S_h = P.sbuf("S_h", [128, 8, 128], F32)
    S_g = P.sbuf("S_g", [128, 16, 128], F32)
    Sb_g = P.sbuf("Sb_g", [128, 16, 128], BF16)
    halo = P.sbuf("halo", [128, 32, 4], F32)
    sm = [P.sbuf("sm%d" % i, [128, 64], F32) for i in range(8)]
    sc = P.sbuf("sc", [128, 64], F32)
    hq_e = [P.sbuf("hq_e%d" % i, [128, TT], BF16) for i in range(2)]
    bs = hq_e
    hk_e = [P.sbuf("hk_e%d" % i, [128, TT], BF16) for i in range(2)]
    hv_t = [P.sbuf("hv_t%d" % i, [128, TT], BF16) for i in range(2)]
    hk_z = [[P.sbuf("hk_z%d_%d" % (par, i), [128, TT], BF16) for i in range(2)] for par in range(2)]
    hq_z = [[P.sbuf("hq_z%d_%d" % (par, i), [128, TT], BF16) for i in range(2)] for par in range(2)]
    hg_t = [P.sbuf("hg_t%d" % i, [128, TT], BF16) for i in range(2)]
    hsc = [P.sbuf("hsc%d" % i, [128, 32], F32) for i in range(2)]
    gq_f = P.sbuf("gq_f", [128, TT], BF16)
    gk_f = P.sbuf("gk_f", [128, TT], BF16)
    gv_f = [P.sbuf("gv_f%d" % i, [128, TT], BF16) for i in range(2)]
    gz_t = P.sbuf("gz_t", [128, NB, 2, 128], BF16)
    kbg_t = [P.sbuf("kbg_t%d" % i, [128, TT], BF16) for i in range(2)]
    kte_t = [P.sbuf("kte_t%d" % i, [128, TT], BF16) for i in range(2)]
    bv_t = [P.sbuf("bv_t%d" % i, [128, TT], BF16) for i in range(2)]
    Gm = P.sbuf("Gm", [128, 8, 128], F32)
    Ex = P.sbuf("Ex", [128, 8, 128], F32)
    Ls = P.sbuf("Ls", [128, 8, 128], BF16)
    Lm = P.sbuf("Lm", [128, 8, 128], BF16)
    NCH = 2
    qkl = [P.sbuf("qkl%d" % c, [128, 128], BF16) for c in range(2)]
    qklT_all = P.sbuf("qklT_all", [128, 2, NB, 128], BF16)
    TT2 = [P.sbuf("TT2_%d" % i, [128, 2 * NB, 128], F32) for i in range(2)]
    X2b = P.sbuf("X2b", [128, 2 * NB, 128], F32)
    Ttb_g = P.sbuf("Ttb_g", [128, 2, NB, 128], BF16)
    u_sb = [P.sbuf("u_sb%d" % c, [128, 128], F32) for c in range(NCH)]
    wT_sb = [P.sbuf("wT_sb%d" % c, [128, 128], BF16) for c in range(NCH)]
    vnew = [P.sbuf("vnew%d" % c, [128, 128], BF16) for c in range(NCH)]
    o_sb = [P.sbuf("o_sb%d" % c, [128, 128], F32) for c in range(NCH)]
    o_bf = vnew
    Smid = wT_sb
    scT = [P.sbuf("scT%d" % c, [128, 128], BF16) for c in range(2)]

    NBK = 6
    banks = [P.psum("bank%d" % i, [128, 512], F32) for i in range(NBK)]
    tbanks = [P.psum("tbank%d" % i, [128, 1024], BF16) for i in range(2)]
    st = {"ps": 0, "tp": 0, "piece": 0, "ch": 0}

    def ps():
        b = banks[st["ps"] % NBK]
        st["ps"] += 1
        return b

    def tps():
        k = st["tp"] % 2
        st["tp"] += 1
        return tbanks[k][:, 0:512]

    USE_WSB = bool(int(os.environ.get("USE_WSB", "1"))) and n_tiles > 1 and not WSB
    if USE_WSB:
        wsb_t = nc.dram_tensor("wsb", [NP, 128, PW], BF16, kind="Internal").ap()

    def piece(name, shape, skip=False):
        i = st["piece"]
        assert names[i % NP] == name, (names[i % NP], name)
        st["piece"] += 1
        if skip and USE_WSB and i >= NP:
            return None
        k = i % NSLOT
        sl = slots[k]
        n = 1
        for s_ in shape:
            n *= s_
        pi = i % NP
        if USE_WSB and i >= NP:
            P.dma("sync", sl[:, 0:n], V(wsb_t[pi, :, 0:n], (("wsb", pi),)), "slot%d" % k)
        else:
            P.dma("sync" if WSB else "gpsimd", sl[:, 0:n], ws_d[pi, :, 0:n], "slot%d" % k)
            if USE_WSB:
                P.dma("sync", V(wsb_t[pi, :, 0:n], (("wsb", pi),)), sl[:, 0:n], "wb%d" % k)
        v = sl[:, 0:n]
        if len(shape) == 2:
            return v.rr("p (a b) -> p a b", a=shape[0])
        if len(shape) == 3:
            return v.rr("p (a b c) -> p a b c", a=shape[0], b=shape[1])
        return v

    cv = lambda c0, n=1: cst[:, c0:c0 + n]

    P.dma("sync", cst[:, :], cst_d[:, :], "cst")
    P.copy("vector", ident_b[:, :], cv(C_ID, 128))
    P.copy("vector", ones_b[:, :], cv(C_ONE, 128))
    P.copy("vector", tu_b[:, :], cv(C_TU, 128))
    P.tt("vector", sc[:, 0:8], cv(C_L0, 8), cv(C_L1, 8), ALU.subtract)
    P.act(lbt[:, 0:8], sc[:, 0:8], AF.Sigmoid)
    P.act(lbt[:, 8:16], sc[:, 0:8], AF.Sigmoid, scale=-1.0)
    P.act(sc[:, 16:32], cv(C_AL, 16), AF.Exp)
    P.ts("vector", nea[:, :], sc[:, 16:32], -1.0, None, ALU.mult)
    P.memset("vector", S_h[:, :, :], 0.0)
    P.memset("vector", S_g[:, :, :], 0.0)
    P.memset("vector", Sb_g[:, :, :], 0.0)
    P.memset("vector", halo[:, :, :], 0.0)
    for par in range(2):
        for i in range(2):
            P.memset("vector", hk_z[par][i][:, :], 0.0)
            P.memset("vector", hq_z[par][i][:, :], 0.0)
    for i in range(2):
        P.memset("vector", scT[i][:, :], 0.0)

    def rmsnorm(gcol):
        sq = hid[:, 0:8, :]
        P.act(sq, h[:, :, :], AF.Square)
        pb = ps()
        for kc in range(8):
            P.mm(pb[:, :], ones_b[:, :], hid[:, kc, :], start=(kc == 0), stop=(kc == 7))
        P.act(fs[0][:, 0:TT], pb[:, :], AF.Ln, scale=1.0 / D, bias=EPS)
        P.act(fs[1][:, 0:TT], fs[0][:, 0:TT], AF.Exp, scale=-0.5)
        return fs[1][:, 0:TT]

    def apply_norm(rstd, gcol, dst_fn):
        for kc in range(8):
            P.stt(dst_fn(kc), h[:, kc, :], cv(gcol + kc), rstd, ALU.mult, ALU.mult)

    def ffn(k, gcol):
        rstd = rmsnorm(gcol)
        apply_norm(rstd, gcol, lambda kc: xn[:, kc, :])
        for i in range(11):
            W = piece("f%d_in_%d" % (k, i), (8, 4, 128))
            for jj in range(2):
                j = 2 * i + jj
                pa, pb = ps(), ps()
                for kc in range(8):
                    P.mm(pa[:, :], W[:, kc, 2 * jj, :], xn[:, kc, :], start=(kc == 0), stop=(kc == 7))
                for kc in range(8):
                    P.mm(pb[:, :], W[:, kc, 2 * jj + 1, :], xn[:, kc, :], start=(kc == 0), stop=(kc == 7))
                sa = fs[2 + (j % 2)][:, 0:TT]
                P.act(sa, pa[:, :], AF.Silu)
                P.tt("vector", hid[:, j, :], sa, pb[:, :], ALU.mult)
        for m in range(8):
            W2 = piece("f%d_out_%d" % (k, m), (NFF, 128))
            pb = ps()
            for kc in range(NFF):
                P.mm(pb[:, :], W2[:, kc, :], hid[:, kc, :], start=(kc == 0), stop=(kc == NFF - 1))
            P.stt(h[:, m, :], pb[:, :], 0.5, h[:, m, :], ALU.mult, ALU.add)

    def rsum(out_col, in_):
        P.op("vector", lambda e: e.reduce_sum(out_col.ap, in_.ap, mybir.AxisListType.X), reads=[in_], writes=[out_col])

    def small_rstd(ss_col, out_col, n):
        P.act(out_col, ss_col, AF.Ln, scale=1.0 / n, bias=EPS)
        P.act(out_col, out_col, AF.Exp, scale=-0.5)

    def hgrn_head(j, pre=False):
        pp = j % 2
        NCK = TT // 64
        W = piece("hg_%d" % j, (8, 4, 128))
        pq, pf = ps(), ps()
        if not pre:
            for kc in range(8):
                P.mm(pq[:, :], W[:, kc, 0, :], xn[:, kc, :], start=(kc == 0), stop=(kc == 7))
        for kc in range(8):
            P.mm(pf[:, :], W[:, kc, 1, :], xn[:, kc, :], start=(kc == 0), stop=(kc == 7))
        pg, pi = ps(), ps()
        for b in range(NB):
            if pre:
                break
            for kc in range(8):
                P.mm(pg[:, b * 128:(b + 1) * 128], xn[:, kc, b * 128:(b + 1) * 128], W[:, kc, 2, :], start=(kc == 0), stop=(kc == 7))
        for b in range(NB):
            for kc in range(8):
                P.mm(pi[:, b * 128:(b + 1) * 128], xn[:, kc, b * 128:(b + 1) * 128], W[:, kc, 3, :], start=(kc == 0), stop=(kc == 7))
        q, sg, f, lf, bb, bm, eq, ek = [fs[i][:, 0:TT] for i in range(2, 10)]
        if not pre:
            P.act(q, pq[:, :], AF.Silu)
            P.act(hg_t[pp][:, :], pg[:, :], AF.Silu)
        P.act(sg, pf[:, :], AF.Sigmoid)
        P.copy("scalar", hv_t[pp][:, :], pi[:, :])
        P.ts("vector", f, sg, lbt[:, 8 + j:9 + j], lbt[:, j:j + 1], ALU.mult, ALU.add)
        P.act(lf, f, AF.Ln)
        P.op("vector", lambda e: e.tensor_tensor_scan(bb.ap, cst.t[:, C_RM:C_RM + TT], lf.ap, 0.0, ALU.mult, ALU.add),
             reads=[cv(C_RM, TT), lf], writes=[bb])
        b3 = bb.rr("p (c t) -> p c t", c=NCK)
        bm3 = bm.rr("p (c t) -> p c t", c=NCK)
        P.tt("vector", bm3, b3, b3[:, :, 31:32].bc([128, NCK, 64]), ALU.subtract)
        P.act(ek, bm, AF.Exp, scale=-1.0)
        if not pre:
            P.act(eq, bm, AF.Exp)
            P.stt(hq_e[pp][:, :], q, 128 ** -0.5, eq, ALU.mult, ALU.mult)
            he4 = hq_e[pp][:, :].rr("p (b t) -> p b t", b=NB)
            for par in range(2):
                P.copy("vector", hq_z[par][pp][:, :].rr("p (b t) -> p b t", b=NB)[:, :, par * 64:(par + 1) * 64], he4[:, :, par * 64:(par + 1) * 64])
        P.ts("vector", f, f, -1.0, 1.0, ALU.mult, ALU.add)
        P.tt("vector", hk_e[pp][:, :], f, ek, ALU.mult)
        hs = hsc[pp]
        P.act(hs[:, 0:8], b3[:, :, 63], AF.Exp)
        P.act(hs[:, 8:16], b3[:, :, 31], AF.Exp)
        P.act(hs[:, 16:24], bm3[:, :, 63], AF.Exp)
        tp = tps()
        for b in range(NB):
            P.transpose(tp[:, b * 128:(b + 1) * 128], hk_e[pp][:, b * 128:(b + 1) * 128], ident_b[:, :])
        P.copy("scalar", hk_z[0][pp][0:64, :], tp[0:64, :])
        P.copy("scalar", hk_z[1][pp][64:128, :], tp[64:128, :])

    def hgrn_rec(j0, pre=False):
        HS = (0, 1)
        js = [j0, j0 + 1]
        Sjs = [V(S_h.t[:, j, :], (("S_h", j),)) for j in js]
        for b in range(NB):
            bsl = slice(b * 128, (b + 1) * 128)
            if pre:
                pAs = [ps(), ps()]
                for par in range(2):
                    c = 2 * b + par
                    for h in HS:
                        P.mm(pAs[h][:, 128 * (par + 1):128 * (par + 2)], hk_z[par][h][:, bsl], hv_t[h][:, bsl])
                    for h in HS:
                        P.ts("vector", Sjs[h], Sjs[h], hsc[h][:, c:c + 1], None, ALU.mult)
                        P.stt(Sjs[h], pAs[h][:, 128 * (par + 1):128 * (par + 2)], hsc[h][:, 16 + c:17 + c], Sjs[h], ALU.mult, ALU.add)
                continue
            pAs, pBs = [ps(), ps()], [ps(), ps()]
            for h in HS:
                P.mm(pAs[h][:, 0:128], hk_e[h][:, bsl], hq_e[h][:, bsl])
            for h in HS:
                P.tt("vector", scT[h][0:64, 0:64], pAs[h][0:64, 0:64], cst[0:64, C_TU:C_TU + 64], ALU.mult)
                P.tt("vector", scT[h][64:128, 64:128], pAs[h][64:128, 64:128], cst[64:128, C_TU + 64:C_TU + 128], ALU.mult)
            for h in HS:
                P.mm(pBs[h][:, 0:128], scT[h][:, :], hv_t[h][:, bsl], start=True, stop=False)
            for par in range(2):
                c = 2 * b + par
                for h in HS:
                    P.act(Smid[h][:, :], Sjs[h], AF.Copy, scale=hsc[h][:, 8 + c:9 + c])
                for h in HS:
                    P.mm(pBs[h][:, 0:128], hq_z[par][h][:, bsl], Smid[h][:, :], start=False, stop=(par == 1))
                    P.mm(pAs[h][:, 128 * (par + 1):128 * (par + 2)], hk_z[par][h][:, bsl], hv_t[h][:, bsl])
                for h in HS:
                    P.ts("vector", Sjs[h], Sjs[h], hsc[h][:, c:c + 1], None, ALU.mult)
                    P.stt(Sjs[h], pAs[h][:, 128 * (par + 1):128 * (par + 2)], hsc[h][:, 16 + c:17 + c], Sjs[h], ALU.mult, ALU.add)
            sAB = []
            for h in HS:
                st["ch"] += 1
                cc = 16 + 4 * (st["ch"] % 8)
                sA, sB = sc.sub(cc, (slice(None), slice(cc, cc + 1))), sc.sub(cc, (slice(None), slice(cc + 1, cc + 2)))
                sAB.append((sA, sB))
                P.act(fs[h][:, 0:128], pBs[h][:, 0:128], AF.Square)
                rsum(sA, fs[h][:, 0:128])
            for h in HS:
                small_rstd(sAB[h][0], sAB[h][1], 128)
            for h in HS:
                P.stt(fs[h][:, 0:128], pBs[h][:, 0:128], sAB[h][1], cv(C_HN, 128), ALU.mult, ALU.mult)
                P.tt("vector", o_bf[h][:, :], fs[h][:, 0:128], hg_t[h][:, bsl], ALU.mult)
            for h in HS:
                tp2 = tps()
                P.transpose(tp2[:, 0:128], o_bf[h][:, :], ident_b[:, :])
                P.copy("scalar", oh_fm[:, js[h], bsl], tp2[:, 0:128])

    def gdn_scalars():
        Wab = piece("ab", (8, 32))
        pab = ps()
        for b in range(NB):
            for kc in range(8):
                P.mm(pab[:, b * 32:(b + 1) * 32], xn[:, kc, b * 128:(b + 1) * 128], Wab[:, kc, :], start=(kc == 0), stop=(kc == 7))
        p3 = pab[:, 0:128].rr("p (b c) -> p b c", b=NB)
        z, g_, be = sm[0][:, :], sm[1][:, :], sm[2][:, :]
        z3 = z.rr("p (b c) -> p b c", b=NB)
        P.tt("vector", z3, p3[:, :, 0:16], cv(C_DT, 16).unsq(1).bc([128, NB, 16]), ALU.add)
        P.act(z, z, AF.Exp)
        P.act(z, z, AF.Ln, bias=1.0)
        P.tt("vector", g_.rr("p (b c) -> p b c", b=NB), z3, nea[:, :].unsq(1).bc([128, NB, 16]), ALU.mult)
        P.act(be.rr("p (b c) -> p b c", b=NB), p3[:, :, 16:32], AF.Sigmoid)
        pg_ = ps()
        P.mm(pg_[:, 0:64], cv(C_TU, 128), g_)
        P.mm(pg_[:, 64:128], cv(C_ONE, 128), g_)
        gam, egam, begam, egk, egend = [sm[i][:, :] for i in range(3, 8)]
        P.copy("vector", gam, pg_[:, 0:64])
        P.act(egam, gam, AF.Exp)
        P.tt("vector", begam, be, egam, ALU.mult)
        P.tt("vector", egk, pg_[:, 64:128], gam, ALU.subtract)
        P.act(egk, egk, AF.Exp)
        P.copy("vector", egend, pg_[:, 64:128])
        P.act(egend, egend, AF.Exp)
        return g_, be, egam, begam, egk, egend

    def gdn_head(j, scal, pre=False):
        g_, be, egam, begam, egk, egend = scal
        W = piece("gd_%d" % j, (8, 4, 128))
        if GDN_LEVEL < 1:
            piece("gz_%d" % j, (8, 2, 128))
            return
        gidx = (j, 8 + j, 16 + 2 * j, 17 + 2 * j)
        ys = [None] * 4
        for c in range(4):
            pc = ps()
            for kc in range(8):
                P.mm(pc[:, :], W[:, kc, c, :], xn[:, kc, :], start=(kc == 0), stop=(kc == 7))
            cb = fs[2 + c]
            gi = gidx[c]
            P.copy("vector", cb[:, 0:3], halo[:, gi, 0:3])
            P.copy("scalar", cb[:, 3:3 + TT], pc[:, :])
            if GDN_SUB >= 1:
                P.copy("vector", halo[:, gi, 0:3], cb[:, TT:TT + 3])
            acc = fs[6 + c][:, 0:TT]
            if pre and c == 0:
                continue
            if c >= 2 and POOL_CONV:
                tmpc = fs[c - 2][:, 0:TT]
                P.ts("gpsimd", acc, cb[:, 3:3 + TT], cv(C_CW + gi * 4 + 3), 0.0, ALU.mult, ALU.add)
                for tpi in range(3):
                    P.ts("gpsimd", tmpc, cb[:, tpi:tpi + TT], cv(C_CW + gi * 4 + tpi), 0.0, ALU.mult, ALU.add)
                    P.tt("gpsimd", acc, acc, tmpc, ALU.add)
            else:
                P.ts("vector", acc, cb[:, 3:3 + TT], cv(C_CW + gi * 4 + 3), None, ALU.mult)
                for tpi in range(3):
                    P.stt(acc, cb[:, tpi:tpi + TT], cv(C_CW + gi * 4 + tpi), acc, ALU.mult, ALU.add)
            ys[c] = acc
        if GDN_SUB < 3:
            piece("gz_%d" % j, (8, 2, 128))
            return
        for c, dst, scl in ((0, gq_f, 128 ** -0.5), (1, gk_f, 1.0)):
            if pre and c == 0:
                continue
            y = fs[2 + c][:, 0:TT]
            P.act(y, ys[c], AF.Silu)
            P.act(bs[c][:, :], y, AF.Square)
            if GDN_SUB < 4:
                continue
            pn = ps()
            P.mm(pn[:, :], ones_b[:, :], bs[c][:, :])
            rn = ys[c]
            P.act(rn, pn[:, :], AF.Ln, bias=EPS)
            P.act(rn, rn, AF.Exp, scale=-0.5)
            if GDN_SUB < 5:
                continue
            P.stt(dst[:, :], y, scl, rn, ALU.mult, ALU.mult)
        if GDN_SUB >= 6:
            for vh in range(2):
                P.act(gv_f[vh][:, :], ys[2 + vh], AF.Silu)
        Wz = piece("gz_%d" % j, (8, 2, 128))
        if GDN_LEVEL < 2:
            return
        pz = [ps(), ps()]
        for b in range(NB):
            if pre:
                break
            for kc in range(8):
                P.mm(pz[b // 2][:, (b % 2) * 256:(b % 2) * 256 + 256], xn[:, kc, b * 128:(b + 1) * 128],
                     Wz[:, kc, :, :].rr("p a b -> p (a b)"), start=(kc == 0), stop=(kc == 7))
        for i in range(2):
            if pre:
                break
            P.act(gz_t[:, 2 * i:2 * i + 2, :, :].rr("p a b c -> p (a b c)"), pz[i][:, :], AF.Silu)
        tk = tps()
        for b in range(NB):
            P.transpose(tk[:, b * 128:(b + 1) * 128], gk_f[:, b * 128:(b + 1) * 128], ident_b[:, :])
        r3 = lambda v: v.rr("p (b t) -> p b t", b=NB)
        sc3 = lambda v, hh: v.rr("p (b c) -> p b c", b=NB)[:, :, hh:hh + 1].bc([128, NB, 128])
        for vh in range(2):
            hh = 2 * j + vh
            P.tt("vector", r3(kbg_t[vh][:, :]), r3(tk), sc3(begam, hh), ALU.mult)
            P.tt("vector", r3(kte_t[vh][:, :]), r3(tk), sc3(egk, hh), ALU.mult)
            tv = tps()
            for b in range(NB):
                P.transpose(tv[:, b * 128:(b + 1) * 128], gv_f[vh][:, b * 128:(b + 1) * 128], ident_b[:, :])
            P.tt("vector", r3(bv_t[vh][:, :]), r3(tv), sc3(be, hh), ALU.mult)
        g3 = g_.rr("p (b c) -> p b c", b=NB)
        for b in range(NB):
            P.tt("vector", Gm[:, 2 * b:2 * b + 2, :], cv(C_SL, 128).unsq(1).bc([128, 2, 128]),
                 g3[:, b, 2 * j:2 * j + 2].unsq(2).bc([128, 2, 128]), ALU.mult)
        for i in range(2):
            pd = ps()
            P.mm(pd[:, :], cv(C_TU, 128), Gm[:, 4 * i:4 * i + 4, :].rr("p a b -> p (a b)"))
            Dc = fs[i][:, 0:TT]
            P.ts("vector", Dc, pd[:, :], -80.0, None, ALU.max)
            P.act(Ex[:, 4 * i:4 * i + 4, :].rr("p a b -> p (a b)"), Dc, AF.Exp)
        P.tt("vector", Ls[:, :, :], Ex[:, :, :], cv(C_SL, 128).unsq(1).bc([128, 8, 128]), ALU.mult)
        P.tt("vector", Lm[:, :, :], Ex[:, :, :], cv(C_TL, 128).unsq(1).bc([128, 8, 128]), ALU.mult)
        for b in range(NB):
            bsl = slice(b * 128, (b + 1) * 128)
            pk_ = ps()
            P.mm(pk_[:, 0:128], gk_f[:, bsl], gk_f[:, bsl])
            if pre:
                P.copy("scalar", Gm[:, 2 * b, :], pk_[:, 0:128])
                continue
            P.mm(pk_[:, 128:256], gq_f[:, bsl], gk_f[:, bsl])
            P.copy("scalar", Gm[:, 2 * b:2 * b + 2, :].rr("p a b -> p (a b)"), pk_[:, 0:256])
        F32R = mybir.dt.float32r
        CH_R = (lambda v: v.bitcast(F32R)) if CHAIN_F32R else (lambda v: v)
        Aall = Ex
        for vh in range(2):
            tb = tps()
            for b in range(NB):
                hh = 2 * j + vh
                col = b * 16 + hh
                q_ = 2 * b + vh
                P.stt(CH_R(Aall[:, q_, :]), Gm[:, 2 * b, :], be[:, col:col + 1], Ls[:, q_, :], ALU.mult, ALU.mult)
                if not pre:
                    P.tt("vector", qkl[b % 2][:, :], Gm[:, 2 * b + 1, :], Lm[:, q_, :], ALU.mult)
                    P.transpose(tb[:, b * 128:(b + 1) * 128], qkl[b % 2][:, :], ident_b[:, :])
            if not pre:
                P.copy("scalar", qklT_all[:, vh, :, :].rr("p a b -> p (a b)"), tb)
        identF = cv(C_ID, 128)
        Bv = [fs[4][:, 0:TT].rr("p (a b) -> p a b", a=NB), fs[5][:, 0:TT].rr("p (a b) -> p a b", a=NB)]
        Avs = [V(Ex.t[:, vh:8:2, :], (("Ex", 0),)) for vh in range(2)]
        X2 = [Gm[:, :, :], X2b[:, :, :]]
        for vh in range(2):
            pT = ps()
            for b in range(NB):
                P.transpose(pT[:, b * 128:(b + 1) * 128], Ex[:, 2 * b + vh, :], identF)
            P.copy("vector", CH_R(Bv[vh].rr("p a b -> p (a b)")), pT[:, :])
        for k in range(7):
            if k == 0:
                for vh in range(2):
                    P.tt("vector", CH_R(X2[vh][:, 0:NB, :]), Bv[vh], cv(C_MK, 128).unsq(1).bc([128, NB, 128]), ALU.mult)
                    P.tt("vector", CH_R(X2[vh][:, NB:2 * NB, :]), Avs[vh], cv(C_MK + 128, 128).unsq(1).bc([128, NB, 128]), ALU.mult)
                    P.tt("vector", CH_R(TT2[vh][:, :, :]), identF.unsq(1).bc([128, 2 * NB, 128]), X2[vh][:, :, :], ALU.subtract)
                continue
            pXs = []
            for vh in range(2):
                pX, pX2 = ps(), ps()
                for b in range(NB):
                    P.mm(pX[:, b * 128:(b + 1) * 128], CH_R(Avs[vh][:, b, :]), CH_R(TT2[vh][:, b, :]))
                for b in range(NB):
                    P.mm(pX2[:, b * 128:(b + 1) * 128], CH_R(Bv[vh][:, b, :]), CH_R(TT2[vh][:, NB + b, :]))
                pXs.append((pX, pX2))
            for vh in range(2):
                pX, pX2 = pXs[vh]
                P.tt("vector", CH_R(X2[vh][:, 0:NB, :]), pX[:, :].rr("p (a b) -> p a b", a=NB),
                     cv(C_MK + 256 * k, 128).unsq(1).bc([128, NB, 128]), ALU.mult)
                P.tt("vector", CH_R(X2[vh][:, NB:2 * NB, :]), pX2[:, :].rr("p (a b) -> p a b", a=NB),
                     cv(C_MK + 256 * k + 128, 128).unsq(1).bc([128, NB, 128]), ALU.mult)
            pYs = []
            for vh in range(2):
                pY, pY2 = ps(), ps()
                for b in range(NB):
                    P.mm(pY[:, b * 128:(b + 1) * 128], CH_R(TT2[vh][:, NB + b, :]), CH_R(X2[vh][:, b, :]))
                for b in range(NB):
                    P.mm(pY2[:, b * 128:(b + 1) * 128], CH_R(TT2[vh][:, b, :]), CH_R(X2[vh][:, NB + b, :]))
                pYs.append((pY, pY2))
            for vh in range(2):
                pY, pY2 = pYs[vh]
                P.tt("vector", CH_R(TT2[vh][:, 0:NB, :].rr("p a b -> p (a b)")), TT2[vh][:, 0:NB, :].rr("p a b -> p (a b)"), pY[:, :], ALU.subtract)
                P.tt("vector", CH_R(TT2[vh][:, NB:2 * NB, :].rr("p a b -> p (a b)")), TT2[vh][:, NB:2 * NB, :].rr("p a b -> p (a b)"), pY2[:, :], ALU.subtract)
        for vh in range(2):
            P.copy("vector", Ttb_g[:, vh, :, :], TT2[vh][:, 0:NB, :])
        VH = (0, 1)
        hhs = [2 * j + vh for vh in VH]
        Sfs = [V(S_g.t[:, hh, :], (("S_g", hh),)) for hh in hhs]
        Sbs = [V(Sb_g.t[:, hh, :], (("Sb_g", hh),)) for hh in hhs]
        for b in range(NB):
            bsl = slice(b * 128, (b + 1) * 128)
            cols = [b * 16 + hh for hh in hhs]
            pus = [ps(), ps()]
            for vh in VH:
                P.mm(pus[vh][:, 0:128], Ttb_g[:, vh, b, :], bv_t[vh][:, bsl])
                P.mm(pus[vh][:, 128:256], kbg_t[vh][:, bsl], Ttb_g[:, vh, b, :])
            for vh in VH:
                P.copy("scalar", u_sb[vh][:, :], pus[vh][:, 0:128])
                P.copy("scalar", wT_sb[vh][:, :], pus[vh][:, 128:256])
            prs = [ps(), ps()]
            for vh in VH:
                P.mm(prs[vh][:, 0:128], wT_sb[vh][:, :], Sbs[vh])
            for vh in VH:
                P.tt("vector", vnew[vh][:, :], u_sb[vh][:, :], prs[vh][:, 0:128], ALU.subtract)
            for vh in VH:
                if not pre:
                    P.mm(prs[vh][:, 128:256], gq_f[:, bsl], Sbs[vh])
                    P.mm(prs[vh][:, 256:384], qklT_all[:, vh, b, :], vnew[vh][:, :])
                P.mm(prs[vh][:, 384:512], kte_t[vh][:, bsl], vnew[vh][:, :])
            for vh in VH:
                P.stt(Sfs[vh], Sfs[vh], egend[:, cols[vh]:cols[vh] + 1], prs[vh][:, 384:512], ALU.mult, ALU.add)
                P.copy("scalar", Sbs[vh], Sfs[vh])
            if pre:
                continue
            for vh in VH:
                P.act(o_sb[vh][:, :], prs[vh][:, 128:256], AF.Copy, scale=egam[:, cols[vh]:cols[vh] + 1])
                P.tt("vector", o_sb[vh][:, :], o_sb[vh][:, :], prs[vh][:, 256:384], ALU.add)
            sAB = []
            for vh in VH:
                st["ch"] += 1
                cc = 16 + 4 * (st["ch"] % 8)
                sA, sB = sc.sub(cc, (slice(None), slice(cc, cc + 1))), sc.sub(cc, (slice(None), slice(cc + 1, cc + 2)))
                sAB.append((sA, sB))
                P.act(fs[vh][:, 0:128], o_sb[vh][:, :], AF.Square)
                rsum(sA, fs[vh][:, 0:128])
            for vh in VH:
                small_rstd(sAB[vh][0], sAB[vh][1], 128)
            for vh in VH:
                P.stt(fs[vh][:, 0:128], o_sb[vh][:, :], sAB[vh][1], cv(C_GN, 128), ALU.mult, ALU.mult)
                P.tt("vector", vnew[vh][:, :], fs[vh][:, 0:128], gz_t[:, b, vh, :], ALU.mult)
            for vh in VH:
                tp2 = tps()
                P.transpose(tp2[:, 0:128], vnew[vh][:, :], ident_b[:, :])
                P.copy("scalar", og_fm[:, hhs[vh], bsl], tp2[:, 0:128])

    def merge_and_out():
        for i in range(2):
            Wgh = piece("gh_%d" % i, (8, 4, 128))
            Wbh = piece("bh_%d" % i, (4, 8, 128))
            for m in range(4):
                pgt, py = ps(), ps()
                for kc in range(8):
                    P.mm(pgt[:, :], Wgh[:, kc, m, :], xn[:, kc, :], start=(kc == 0), stop=(kc == 7))
                for hd in range(8):
                    P.mm(py[:, :], Wbh[:, m, hd, :], oh_fm[:, hd, :], start=(hd == 0), stop=(hd == 7))
                sg = fs[2 + (m % 2)][:, 0:TT]
                P.act(sg, pgt[:, :], AF.Sigmoid)
                P.tt("vector", ymf[:, m, :], sg, py[:, :], ALU.mult)
            Wgg = piece("gg_%d" % i, (8, 4, 128))
            for m in range(4):
                if m % 2 == 0:
                    Wbg = piece("bg_%d" % (2 * i + m // 2), (2, 16, 128))
                pgt, py = ps(), ps()
                for kc in range(8):
                    P.mm(pgt[:, :], Wgg[:, kc, m, :], xn[:, kc, :], start=(kc == 0), stop=(kc == 7))
                for hd in range(16):
                    P.mm(py[:, :], Wbg[:, m % 2, hd, :], og_fm[:, hd, :], start=(hd == 0), stop=(hd == 15))
                sg = fs[2 + (m % 2)][:, 0:TT]
                P.act(sg, pgt[:, :], AF.Sigmoid)
                P.tt("vector", sg, sg, py[:, :], ALU.mult)
                P.tt("vector", ym[:, 4 * i + m, :], sg, ymf[:, m, :], ALU.add)
        for i in range(2):
            Wo = piece("wo_%d" % i, (4, 8, 128))
            for m in range(4):
                po = ps()
                for kc in range(8):
                    P.mm(po[:, :], Wo[:, m, kc, :], ym[:, kc, :], start=(kc == 0), stop=(kc == 7))
                P.tt("vector", h[:, 4 * i + m, :], h[:, 4 * i + m, :], po[:, :], ALU.add)

    for t in range(n_tiles):
        pre = t < n_pre
        tsl = slice(t * TT, (t + 1) * TT)
        osl = slice((t - n_pre) * TT, (t - n_pre + 1) * TT)
        P.dma("sync", h[:, :, :], x_d[:, tsl].rr("(kc p) t -> p kc t", p=128), "xin")
        ffn(1, C_G1)
        rstd = rmsnorm(C_GM)
        apply_norm(rstd, C_GM, lambda kc: xn[:, kc, :])
        for j in range(8):
            hgrn_head(j, pre)
            if j % 2 == 1:
                hgrn_rec(j - 1, pre)
        scal = gdn_scalars()
        for j in range(8):
            gdn_head(j, scal, pre)
        if pre:
            for i in range(2):
                for nm in ("gh_%d" % i, "bh_%d" % i, "gg_%d" % i, "bg_%d" % (2 * i), "bg_%d" % (2 * i + 1)):
                    piece(nm, (8, 4, 128), skip=True)
            piece("wo_0", (4, 8, 128), skip=True)
            piece("wo_1", (4, 8, 128), skip=True)
            for i in range(11):
                piece("f2_in_%d" % i, (8, 4, 128), skip=True)
            for m in range(8):
                piece("f2_out_%d" % m, (NFF, 128), skip=True)
            continue
        merge_and_out()
        ffn(2, C_G2)
        rstd = rmsnorm(C_GF)
        apply_norm(rstd, C_GF, lambda kc: h[:, kc, :])
        P.dma("sync", out_d[:, osl].rr("(kc p) t -> p kc t", p=128), h[:, :, :], "xout")
    P.emit()
    P.close()
    return nc


_CACHE = {}


def kernel(**inputs):
    inp = {k: np.asarray(v) for k, v in inputs.items()}
    x = inp["x"]
    B, S, _ = x.shape
    half = S // 2
    if "nc" not in _CACHE:
        _CACHE["nc"] = build_program(S, n_pre=half // TT)
    nc = _CACHE["nc"]
    ws = pack_weights(inp)
    cst = pack_consts(inp)
    in_maps = []
    for b in range(B):
        for hf in range(2):
            prev = np.zeros((half, x.shape[2]), np.float32) if hf == 0 else x[b, :half]
            own = x[b, hf * half:(hf + 1) * half]
            in_maps.append({"xT": np.ascontiguousarray(np.concatenate([prev, own], axis=0).T), "ws": ws, "cst": cst})
    res = run_bass_kernel_spmd(nc, in_maps, core_ids=list(range(2 * B)))
    out = np.empty((B, S, x.shape[2]), np.float32)
    for b in range(B):
        for hf in range(2):
            out[b, hf * half:(hf + 1) * half] = np.asarray(res.results[2 * b + hf]["outT"]).T
    return out
```
